# Optimizing a Trainium2 kernel written in Bass

```python
import math
import jax, jax.numpy as jnp
from jax import lax
import numpy as np

D_MODEL = 1024
BATCH = 4
SEQ = 8192
DEPTH = 1

CHUNK = 64
Q_BLOCK = 128

GLA_HEADS = 4
GLA_DK = 128
GLA_DV = 256
GLA_LOWRANK = 16
GLA_GATE_TEMP = 16.0

MLA_HEADS = 8
MLA_Q_LORA = 384
MLA_KV_LORA = 256
MLA_NOPE = 128
MLA_ROPE = 64
MLA_V = 128
ROPE_THETA = 10000.0

D_FF = ((8 * D_MODEL // 3 + 255) // 256) * 256

NORM_EPS = 1e-6

IN_SPLITS = (
    GLA_HEADS * GLA_DK,
    GLA_HEADS * GLA_DK,
    GLA_HEADS * GLA_DV,
    GLA_HEADS * GLA_DV,
    GLA_LOWRANK,
    MLA_Q_LORA,
    MLA_KV_LORA,
    MLA_ROPE,
)
D_IN = sum(IN_SPLITS)
GLA_WIDTH = GLA_HEADS * GLA_DV
MLA_WIDTH = MLA_HEADS * MLA_V

kernel_name = "hybrid_gla_mla_gated_sandwich_block"


def rms_norm(x, g):
    xf = x.astype(jnp.float32)
    y = xf * lax.rsqrt(jnp.mean(xf * xf, axis=-1, keepdims=True) + NORM_EPS)
    return (y * g.astype(jnp.float32)).astype(x.dtype)


def rope_tables(positions):
    inv_freq = 1.0 / (ROPE_THETA ** (jnp.arange(0, MLA_ROPE, 2, dtype=jnp.float32) / MLA_ROPE))
    ang = positions.astype(jnp.float32)[..., None] * inv_freq
    return jnp.cos(ang), jnp.sin(ang)


def apply_rope(x, cos, sin):
    half = x.shape[-1] // 2
    x1 = x[..., :half].astype(jnp.float32)
    x2 = x[..., half:].astype(jnp.float32)
    out = jnp.concatenate([x1 * cos - x2 * sin, x2 * cos + x1 * sin], axis=-1)
    return out.astype(x.dtype)


def gla_branch(h_q, h_k, h_v, h_g, h_a, w_a2, b_a2, gla_norm):
    B, S, _ = h_q.shape
    n_chunks = S // CHUNK
    f32 = jnp.float32
    q = h_q.reshape(B, S, GLA_HEADS, GLA_DK).astype(f32) * (GLA_DK ** -0.5)
    k = h_k.reshape(B, S, GLA_HEADS, GLA_DK).astype(f32)
    v = h_v.reshape(B, S, GLA_HEADS, GLA_DV).astype(f32)
    log_a = jax.nn.log_sigmoid((h_a @ w_a2 + b_a2).astype(f32)) / GLA_GATE_TEMP
    log_a = log_a.reshape(B, S, GLA_HEADS, GLA_DK)

    def to_chunks(t):
        return t.reshape(B, n_chunks, CHUNK, GLA_HEADS, t.shape[-1]).transpose(1, 0, 3, 2, 4)

    q, k, v, log_a = to_chunks(q), to_chunks(k), to_chunks(v), to_chunks(log_a)
    b = jnp.cumsum(log_a, axis=3)
    b_end = b[:, :, :, -1:, :]
    k_dec = k * jnp.exp(b_end - b)
    chunk_decay = jnp.exp(b_end[:, :, :, 0, :])

    def step(state, inp):
        qc, kc, vc, dc = inp
        state = dc[..., None] * state + jnp.einsum('bhck,bhcv->bhkv', kc, vc)
        out = jnp.einsum('bhck,bhkv->bhcv', qc, state)
        return state, out

    state0 = jnp.zeros((B, GLA_HEADS, GLA_DK, GLA_DV), f32)
    _, o = lax.scan(step, state0, (q, k_dec, v, chunk_decay))
    o = o.transpose(1, 0, 3, 2, 4).reshape(B, S, GLA_HEADS, GLA_DV)
    o = rms_norm(o, gla_norm)
    g = h_g.reshape(B, S, GLA_HEADS, GLA_DV).astype(f32)
    o = o * jax.nn.silu(g)
    return o.reshape(B, S, GLA_WIDTH).astype(h_q.dtype)


def mla_branch(c_q, c_kv, k_pe, cos, sin, q_norm, w_uq, kv_norm, w_ukv):
    B, S, _ = c_q.shape
    q = (rms_norm(c_q, q_norm) @ w_uq).reshape(B, S, MLA_HEADS, MLA_NOPE + MLA_ROPE)
    q_nope = q[..., :MLA_NOPE]
    q_pe = apply_rope(q[..., MLA_NOPE:], cos[:, :, None, :], sin[:, :, None, :])
    kv = (rms_norm(c_kv, kv_norm) @ w_ukv).reshape(B, S, MLA_HEADS, MLA_NOPE + MLA_V)
    k_nope = kv[..., :MLA_NOPE]
    v = kv[..., MLA_NOPE:]
    k_pe = apply_rope(k_pe, cos, sin)
    scale = (MLA_NOPE + MLA_ROPE) ** -0.5
    n_blocks = S // Q_BLOCK

    def blocks(t):
        return t.reshape(B, n_blocks, Q_BLOCK, MLA_HEADS, t.shape[-1]).transpose(1, 0, 2, 3, 4)

    key_chunk = jnp.arange(S) // CHUNK

    def attend(args):
        qn_b, qp_b, blk = args
        s = (jnp.einsum('bqhd,bkhd->bhqk', qn_b, k_nope)
             + jnp.einsum('bqhr,bkr->bhqk', qp_b, k_pe)).astype(jnp.float32) * scale
        q_chunk = (blk * Q_BLOCK + jnp.arange(Q_BLOCK)) // CHUNK
        mask = key_chunk[None, :] <= q_chunk[:, None]
        s = jnp.where(mask[None, None], s, -jnp.inf)
        p = jax.nn.softmax(s, axis=-1).astype(v.dtype)
        return jnp.einsum('bhqk,bkhd->bqhd', p, v)

    o = lax.map(attend, (blocks(q_nope), blocks(q_pe), jnp.arange(n_blocks)))
    return o.transpose(1, 0, 2, 3, 4).reshape(B, S, MLA_WIDTH)


def setup_inputs(seed: int = 0) -> dict:
    key = jax.random.key(seed)
    ks = jax.random.split(key, 24)
    f32 = jnp.float32

    def w(k, shape, fan_in):
        return jax.random.normal(k, shape, f32) * (fan_in ** -0.5)

    def gain(k, dim):
        return 1.0 + 0.02 * jax.random.normal(k, (DEPTH, dim), f32)

    x = jax.random.normal(ks[0], (BATCH, SEQ, D_MODEL), f32)
    offsets = jax.random.randint(ks[1], (BATCH, 1), 0, 4096, dtype=jnp.int32)
    positions = offsets + jnp.arange(SEQ, dtype=jnp.int32)[None, :]
    return {
        "x": x,
        "positions": positions,
        "pre_mix_norm": gain(ks[2], D_MODEL),
        "w_in": w(ks[3], (DEPTH, D_MODEL, D_IN), D_MODEL),
        "w_a2": w(ks[4], (DEPTH, GLA_LOWRANK, GLA_HEADS * GLA_DK), GLA_LOWRANK),
        "b_a2": 0.1 * jax.random.normal(ks[5], (DEPTH, GLA_HEADS * GLA_DK), f32),
        "gla_norm": gain(ks[6], GLA_DV),
        "w_o_gla": w(ks[7], (DEPTH, GLA_WIDTH, D_MODEL), GLA_WIDTH),
        "q_norm": gain(ks[8], MLA_Q_LORA),
        "w_uq": w(ks[9], (DEPTH, MLA_Q_LORA, MLA_HEADS * (MLA_NOPE + MLA_ROPE)), MLA_Q_LORA),
        "kv_norm": gain(ks[10], MLA_KV_LORA),
        "w_ukv": w(ks[11], (DEPTH, MLA_KV_LORA, MLA_HEADS * (MLA_NOPE + MLA_V)), MLA_KV_LORA),
        "w_o_mla": w(ks[12], (DEPTH, MLA_WIDTH, D_MODEL), MLA_WIDTH),
        "w_gate": w(ks[13], (DEPTH, D_MODEL, 2 * D_MODEL), D_MODEL),
        "b_gate": 0.02 * jax.random.normal(ks[14], (DEPTH, 2 * D_MODEL), f32),
        "w_out": w(ks[15], (DEPTH, D_MODEL, D_MODEL), D_MODEL),
        "post_mix_norm": gain(ks[16], D_MODEL),
        "pre_ffn_norm": gain(ks[17], D_MODEL),
        "w_ffn_gate": w(ks[18], (DEPTH, D_MODEL, D_FF), D_MODEL),
        "w_ffn_up": w(ks[19], (DEPTH, D_MODEL, D_FF), D_MODEL),
        "w_ffn_down": w(ks[20], (DEPTH, D_FF, D_MODEL), D_FF),
        "post_ffn_norm": gain(ks[21], D_MODEL),
    }


def reference(x, positions, pre_mix_norm, w_in, w_a2, b_a2, gla_norm, w_o_gla,
              q_norm, w_uq, kv_norm, w_ukv, w_o_mla, w_gate, b_gate, w_out,
              post_mix_norm, pre_ffn_norm, w_ffn_gate, w_ffn_up, w_ffn_down,
              post_ffn_norm):
    cos, sin = rope_tables(positions)
    split_points = list(np.cumsum(IN_SPLITS)[:-1])
    for l in range(DEPTH):
        h = rms_norm(x, pre_mix_norm[l])
        proj = h @ w_in[l]
        h_q, h_k, h_v, h_g, h_a, c_q, c_kv, k_pe = jnp.split(proj, split_points, axis=-1)
        y_a = gla_branch(h_q, h_k, h_v, h_g, h_a, w_a2[l], b_a2[l], gla_norm[l]) @ w_o_gla[l]
        y_b = mla_branch(c_q, c_kv, k_pe, cos, sin, q_norm[l], w_uq[l], kv_norm[l], w_ukv[l]) @ w_o_mla[l]
        gates = jax.nn.sigmoid(h @ w_gate[l] + b_gate[l])
        g_a, g_b = jnp.split(gates, 2, axis=-1)
        mixed = (g_a * y_a + g_b * y_b) @ w_out[l]
        x = x + rms_norm(mixed, post_mix_norm[l])
        h = rms_norm(x, pre_ffn_norm[l])
        f = (jax.nn.silu(h @ w_ffn_gate[l]) * (h @ w_ffn_up[l])) @ w_ffn_down[l]
        x = x + rms_norm(f, post_ffn_norm[l])
    return x
```

```python
import math
from contextlib import ExitStack

import numpy as np
import concourse.bass as bass
import concourse.mybir as mybir
from concourse.bass_utils import run_bass_kernel_spmd

F32 = mybir.dt.float32
BF16 = mybir.dt.bfloat16
I32 = mybir.dt.int32
AF = mybir.ActivationFunctionType
ALU = mybir.AluOpType

D = 1024
KD = 8
DFF = 2816
KF = 22
EPS = 1e-6
NSH = 1936
NOW = 1920
GT = 4


PSUM_TOKENS = frozenset("b%d" % i for i in range(8))


class Op:
    __slots__ = ("eng", "fn", "deps", "signal", "sem", "count", "key", "idx")

    def __init__(self, eng, fn, key):
        self.eng, self.fn, self.key = eng, fn, key
        self.deps, self.signal, self.sem, self.count = [], False, None, 0


class Sched:
    ENGS = ("pe", "act", "dve", "pool", "sp")

    def __init__(self):
        self.ops = {e: [] for e in self.ENGS}
        self.last_w = {}
        self.readers = {}
        self.nops = 0

    def add(self, eng, fn, r=(), w=(), key=None):
        op = Op(eng, fn, key)
        if key is not None:
            op.signal = True
        op.idx = self.nops
        self.nops += 1
        raw = set()
        deps = set()
        for t in r:
            lw = self.last_w.get(t)
            if lw is not None:
                deps.add(lw)
                raw.add(lw)
            if t in PSUM_TOKENS:
                for rd in self.readers.get(t, {}).values():
                    if rd.eng != eng:
                        deps.add(rd)
        for t in w:
            lw = self.last_w.get(t)
            if lw is not None:
                deps.add(lw)
            for rd in self.readers.get(t, {}).values():
                deps.add(rd)
        for d in deps:
            if d is op:
                continue
            if d.key is None and key is None and d.eng == eng:
                if eng == "pe":
                    continue
            op.deps.append(d)
            d.signal = True
        for t in r:
            rk = eng if key is None else ("dma", op.idx)
            self.readers.setdefault(t, {})[rk] = op
        for t in w:
            self.last_w[t] = op
            self.readers[t] = {}
        self.ops[eng].append(op)
        return op

    def barrier(self):
        lasts = []
        for e in self.ENGS:
            last_eng = None
            last_dma = {}
            for op in self.ops[e]:
                if op.key is None:
                    last_eng = op
                else:
                    last_dma[op.key] = op
            if last_eng is not None:
                lasts.append(last_eng)
            lasts.extend(last_dma.values())
        for e in self.ENGS:
            op = Op(e, lambda eng: eng.nop(), None)
            op.idx = self.nops
            self.nops += 1
            for d in lasts:
                if d.key is None and d.eng == e:
                    continue
                op.deps.append(d)
                d.signal = True
            self.ops[e].append(op)

    def finalize(self, nc, es):
        self.sems = {}
        cnt = {}
        for e in self.ENGS:
            for op in self.ops[e]:
                if not op.signal:
                    continue
                if op.key is not None:
                    k = ("dma", op.key)
                    inc = 16
                else:
                    k = ("eng", e)
                    inc = 1
                if k not in self.sems:
                    self.sems[k] = es.enter_context(nc.semaphore("s_%s_%s" % (k[0], str(k[1]))))
                    cnt[k] = 0
                cnt[k] += inc
                op.sem = k
                op.count = cnt[k]

    def emit(self, eng_name, engine):
        waited = {}
        for op in self.ops[eng_name]:
            need = {}
            for d in op.deps:
                if need.get(d.sem, 0) < d.count:
                    need[d.sem] = d.count
            for k, v in need.items():
                if waited.get(k, 0) < v:
                    engine.wait_ge(self.sems[k], v)
                    waited[k] = v
            ins = op.fn(engine)
            if op.signal:
                ins.then_inc(self.sems[op.sem], 16 if op.key is not None else 1)


class K:
    def __init__(self, S, upto=99, dbg=False):
        self.S = S
        self.NT = S // 128
        self.NOWN = self.NT // 2
        self.NG = self.NOWN // GT
        assert self.NG * GT * 2 * 128 == S
        self.upto = upto
        self.dbg = dbg
        self.nc = bass.Bass("TRN2", target_bir_lowering=False)
        self.s = Sched()
        self.dkeys = {}

    def mm(self, out, lhsT, rhs, start, stop, r, w):
        return self.s.add("pe", lambda e: e.matmul(out, lhsT, rhs, start=start, stop=stop), r, w)

    def tr(self, out, in_, r, w):
        ident = self.identb[:]
        return self.s.add("pe", lambda e: e.transpose(out, in_, ident), r + ["ident"], w)

    def act(self, out, in_, func, r, w, scale=None, bias=None, accum=None):
        kw = {}
        if scale is not None:
            kw["scale"] = scale
        if bias is not None:
            kw["bias"] = bias
        if accum is not None:
            kw["accum_out"] = accum
        return self.s.add("act", lambda e: e.activation(out, in_, func, **kw), r, w)

    def ts(self, eng, out, in0, s1, op0, r, w, s2=None, op1=None):
        if op1 is None:
            return self.s.add(eng, lambda e: e.tensor_scalar(out, in0, s1, None, op0), r, w)
        return self.s.add(eng, lambda e: e.tensor_scalar(out, in0, s1, s2, op0, op1), r, w)

    def tt(self, eng, out, in0, in1, op, r, w):
        return self.s.add(eng, lambda e: e.tensor_tensor(out, in0, in1, op), r, w)

    def stt(self, out, in0, scalar, in1, op0, op1, r, w):
        return self.s.add("dve", lambda e: e.scalar_tensor_tensor(out, in0, scalar, in1, op0, op1), r, w)

    def cp(self, eng, out, in_, r, w):
        if eng == "act":
            return self.s.add("act", lambda e: e.activation(out, in_, AF.Copy), r, w)
        return self.s.add(eng, lambda e: e.tensor_copy(out, in_), r, w)

    def memset(self, eng, ap, val, w):
        return self.s.add(eng, lambda e: e.memset(ap, val), [], w)

    def recip(self, out, in_, r, w):
        return self.s.add("dve", lambda e: e.reciprocal(out, in_), r, w)

    def dma(self, q, out, in_, key, r, w):
        base = key.rstrip("0123456789")
        ch = "chain_" + base
        return self.s.add(q, lambda e: e.dma_start(out, in_), list(r) + [ch], list(w) + [ch], key=base)

    def sb(self, es, name, shape, dt):
        return es.enter_context(self.nc.sbuf_tensor(name, list(shape), dt))

    def din(self, name, shape, dt=F32):
        t = self.nc.dram_tensor(name, list(shape), dt, kind="ExternalInput").ap()
        self.in_names.append(name)
        return t

    def dscr(self, name, shape, dt):
        return self.nc.dram_tensor(name, list(shape), dt, kind="Internal").ap()

    def probe(self, tag):
        import os
        if not os.environ.get("SBPROBE"):
            return
        try:
            with self.nc.sbuf_tensor("probe_" + tag, [128, 100000], F32):
                pass
        except AssertionError as e:
            msg = str(e)
            i = msg.find("(base=")
            print("SBUF", tag, msg[i:i + 40])

    def rstd_from_ss(self, ss, lnv, rstd, n, rt, wt):
        self.act(lnv, ss, AF.Ln, rt, [wt + "_ln"], scale=1.0 / n, bias=self.epsc[:, 0:1])
        self.act(rstd, lnv, AF.Exp, [wt + "_ln"], [wt], scale=-0.5)

    def load_w(self, dst, src, nk, ncol, gain, tok, defer=False):
        engs = ("dve", "act")
        CB = self.wcb
        for k in range(nk):
            for c0 in range(0, ncol, CB):
                c1 = min(ncol, c0 + CB)
                sl = self.wslot % 2
                self.wslot += 1
                stg = self.wstage[sl]
                self.dma("sp", stg[:, 0:c1 - c0], src[k * 128:(k + 1) * 128, c0:c1], "wst" + "AB"[sl],
                         [], ["wst%d" % sl])
                eng = engs[self.wrot % 2]
                self.wrot += 1
                o = dst[:, k, c0:c1]
                i = stg[:, 0:c1 - c0]
                wt = [tok + "#" + eng]
                if gain is not None:
                    g = gain[:, k:k + 1]
                    if eng == "act":
                        self.act(o, i, AF.Copy, ["wst%d" % sl, "gains"], wt, scale=g)
                    else:
                        self.ts(eng, o, i, g, ALU.mult, ["wst%d" % sl, "gains"], wt)
                else:
                    self.cp(eng, o, i, ["wst%d" % sl], wt)

        def join():
            self.s.add("pe", lambda e: e.nop(), [tok + "#dve", tok + "#act"], [tok])
        if defer:
            return join
        join()
        return None

    def norm_T(self, xt, xtok, hT_ap, htok, u):
        self.norm_T_multi([(xt, xtok, hT_ap, htok, u)])

    def norm_T_multi(self, items):
        junk = self.junk
        for (xt, xtok, hT_ap, htok, u) in items:
            self.act(junk[:], xt, AF.Square, [xtok], ["junk", "ss%d" % u], accum=self.st_ss[u][:, 0:1])
        for (xt, xtok, hT_ap, htok, u) in items:
            self.act(self.st_ln[u][:, 0:1], self.st_ss[u][:, 0:1], AF.Ln, ["ss%d" % u], ["rs%d_ln" % u],
                     scale=1.0 / D, bias=self.epsc[:, 0:1])
        for (xt, xtok, hT_ap, htok, u) in items:
            self.act(self.st_rs[u][:, 0:1], self.st_ln[u][:, 0:1], AF.Exp, ["rs%d_ln" % u], ["rs%d" % u], scale=-0.5)
        b0 = self.pb16[0]
        for (xt, xtok, hT_ap, htok, u) in items:
            xu = u % len(self.xs)
            xs = self.xs[xu]
            self.ts("dve", xs[:], xt, self.st_rs[u][:, 0:1], ALU.mult, [xtok, "rs%d" % u], ["xs%d" % xu])
            for k in range(KD):
                self.tr(b0[:, k * 128:(k + 1) * 128], xs[:, k * 128:(k + 1) * 128], ["xs%d" % xu], ["b0"])
            self.cp("act", hT_ap, b0[:, 0:1024].rearrange("p (k n) -> p k n", k=KD), ["b0"], [htok])

    def build(self):
        nc, S, NT, NOWN, NG = self.nc, self.S, self.NT, self.NOWN, self.NG
        self.in_names = []
        SO = S // 2
        xall = self.din("xall", [S, D])
        xown = self.din("xown", [SO, D])
        posall = self.din("posall", [1, S], I32)
        posown = self.din("posown", [1, SO], I32)
        invf = self.din("invf", [64, 1])
        ident = self.din("ident", [128, 128])
        umat = self.din("umat", [128, 128])
        cind = self.din("cind", [128, 2])
        qmask = self.din("qmask", [128, 2])
        amask = self.din("amask", [128, 256])
        w_sh = self.din("w_sh", [D, NSH])
        w_ow = self.din("w_ow", [D, NOW])
        g_pm = self.din("g_pm", [128, 8])
        w_a2 = self.din("w_a2", [16, 512])
        b_a2 = self.din("b_a2", [1, 512])
        g_gla = self.din("g_gla", [128, 8])
        w_og = self.din("w_og", [D, D])
        g_q = self.din("g_q", [128, 3])
        w_uq = self.din("w_uq", [384, 2048])
        g_kv = self.din("g_kv", [128, 2])
        w_ukT = self.din("w_ukT", [128, 2048])
        w_uv = self.din("w_uv", [256, 1024])
        w_om = self.din("w_om", [D, D])
        w_g = self.din("w_g", [D, 2048])
        b_g = self.din("b_g", [128, 16])
        w_out = self.din("w_out", [D, D])
        g_pmix = self.din("g_pmix", [1, D])
        g_pf = self.din("g_pf", [128, 8])
        w_fg = self.din("w_fg", [D, DFF])
        w_fu = self.din("w_fu", [D, DFF])
        w_fd = self.din("w_fd", [DFF, D])
        g_pffn = self.din("g_pffn", [1, D])
        out = nc.dram_tensor("out", [SO, D], F32, kind="ExternalOutput").ap()

        rope_all = self.dscr("rope_all", [2, 64, S], F32)
        rope_own = self.dscr("rope_own", [2, 64, SO], F32)
        sc_og = self.dscr("sc_og", [NG, 128, 8, 512], BF16)
        sc_cq = self.dscr("sc_cq", [NOWN, 128, 3, 128], BF16)
        sc_om = self.dscr("sc_om", [NG, 128, 8, 512], BF16)
        sc_x1 = self.dscr("sc_x1", [SO, D], F32)
        self.dbg_out = None
        if self.dbg:
            self.dbg_out = nc.dram_tensor("dbg", [128, 4096], F32, kind="ExternalOutput").ap()

        with ExitStack() as es:
            self.identf = self.sb(es, "identf", [128, 128], F32)
            self.identb = self.sb(es, "identb", [128, 128], BF16)
            self.epsc = self.sb(es, "epsc", [128, 1], F32)
            self.onec = self.sb(es, "onec", [128, 1], F32)
            self.hpic = self.sb(es, "hpic", [128, 1], F32)
            self.junk = self.sb(es, "junk", [128, 1024], BF16)
            self.st_ss = [self.sb(es, "st_ss%d" % u, [128, 8], F32) for u in range(4)]
            self.st_ln = [self.sb(es, "st_ln%d" % u, [128, 8], F32) for u in range(4)]
            self.st_rs = [self.sb(es, "st_rs%d" % u, [128, 8], F32) for u in range(4)]
            self.xs = [self.sb(es, "xs%d" % u, [128, 1024], BF16) for u in range(2)]
            pbanks = [es.enter_context(nc.psum_tensor("pb%d" % i, [128, 512], F32)) for i in range(8)]
            self.pb = pbanks
            self.pb16 = [b[:].bitcast(BF16) for b in pbanks]
            self.wslot = 0
            self.wcb = 2048
            self.wrot = 0
            pb, pb16 = self.pb, self.pb16

            self.dma("sp", self.identf[:], ident, "misc", [], ["identf"])
            self.cp("dve", self.identb[:], self.identf[:], ["identf"], ["ident"])
            self.memset("dve", self.epsc[:], EPS, ["epsc"])
            self.memset("dve", self.onec[:], 1.0, ["onec"])
            self.memset("dve", self.hpic[:], math.pi / 2, ["hpic"])

            with ExitStack() as er:
                CH = min(2048, SO)
                invf_t = self.sb(er, "invf_t", [64, 1], F32)
                posi = self.sb(er, "posi", [64, CH], I32)
                ang = self.sb(er, "ang", [64, CH], F32)
                tq = self.sb(er, "tq", [64, CH], F32)
                rr = self.sb(er, "rr", [64, CH], F32)
                g1 = self.sb(er, "g1", [64, CH], F32)
                co = self.sb(er, "co", [64, CH], F32)
                si = self.sb(er, "si", [64, CH], F32)
                self.dma("sp", invf_t[:], invf, "misc2", [], ["invf"])
                MAGIC = 12582912.0
                C1 = 6.28125
                C2 = 2.0 * math.pi - 6.28125
                PI = math.pi
                for (pos_d, rope_d, n) in ((posall, rope_all, S), (posown, rope_own, SO)):
                    for c0 in range(0, n, CH):
                        self.dma("sp", posi[:], pos_d[0:1, c0:c0 + CH].partition_broadcast(64), "posi",
                                 [], ["posi"])
                        self.cp("dve", ang[:], posi[:], ["posi"], ["ang"])
                        self.ts("dve", ang[:], ang[:], invf_t[:, 0:1], ALU.mult, ["ang", "invf"], ["ang"])
                        self.ts("dve", tq[:], ang[:], 1.0 / (2 * PI), ALU.mult, ["ang"], ["tq"], s2=MAGIC, op1=ALU.add)
                        self.ts("dve", tq[:], tq[:], -MAGIC, ALU.add, ["tq"], ["tq"])
                        self.stt(rr[:], tq[:], -C1, ang[:], ALU.mult, ALU.add, ["tq", "ang"], ["rr"])
                        self.stt(rr[:], tq[:], -C2, rr[:], ALU.mult, ALU.add, ["tq", "rr"], ["rr"])
                        self.ts("dve", g1[:], rr[:], PI, ALU.is_gt, ["rr"], ["g1"], s2=-2 * PI, op1=ALU.mult)
                        self.tt("dve", rr[:], rr[:], g1[:], ALU.add, ["rr", "g1"], ["rr"])
                        self.ts("dve", g1[:], rr[:], -PI, ALU.is_lt, ["rr"], ["g1"], s2=2 * PI, op1=ALU.mult)
                        self.tt("dve", rr[:], rr[:], g1[:], ALU.add, ["rr", "g1"], ["rr"])
                        self.ts("dve", rr[:], rr[:], -3.1415925, ALU.max, ["rr"], ["rr"], s2=3.1415925, op1=ALU.min)
                        self.act(si[0:32, :], rr[0:32, :], AF.Sin, ["rr"], ["si"], scale=-1.0)
                        self.act(si[32:64, :], rr[32:64, :], AF.Sin, ["rr"], ["si"])
                        self.act(g1[:], rr[:], AF.Abs, ["rr"], ["g1"])
                        self.act(co[:], g1[:], AF.Sin, ["g1"], ["co"], scale=-1.0, bias=self.hpic[0:64, 0:1])
                        self.dma("sp", rope_d[0, :, c0:c0 + CH], co[:], "ropest0", ["co"], ["rope_dram"])
                        self.dma("sp", rope_d[1, :, c0:c0 + CH], si[:], "ropest1", ["si"], ["rope_dram"])

            self.s.barrier()
            if self.upto >= 1:
                self.pass_A(es, locals())
            self.finish(es, out)
        return nc

    def finish(self, es, out):
        nc = self.nc
        self.s.add("sp", lambda e: e.nop(), ["out_dram", "dbg_dram"], [])
        self.s.finalize(nc, es)
        with nc.Block() as block:
            @block.sync
            def _(e):
                self.s.emit("sp", e)

            @block.tensor
            def _(e):
                self.s.emit("pe", e)

            @block.scalar
            def _(e):
                self.s.emit("act", e)

            @block.vector
            def _(e):
                self.s.emit("dve", e)

            @block.gpsimd
            def _(e):
                self.s.emit("pool", e)

    def pass_A(self, es_outer, L):
        nc, S, NT, NOWN, NG = self.nc, self.S, self.NT, self.NOWN, self.NG
        pb, pb16 = self.pb, self.pb16
        xall, xown = L["xall"], L["xown"]
        with ExitStack() as es:
            vaug = self.sb(es, "vaug", [128, NT, 258], BF16)
            kpeT = self.sb(es, "kpeT", [64, S], BF16)
            self.vaug, self.kpeT = vaug, kpeT
            self.memset("pool", vaug[:, :, 256:258], 1.0, ["vaug_ones"])
            with ExitStack() as ea:
                WS = self.sb(ea, "WS", [128, KD, NSH], BF16)
                WO = self.sb(ea, "WO", [128, KD, NOW], BF16)
                gpm = self.sb(ea, "gpm", [128, 8], F32)
                wa2a = self.sb(ea, "wa2a", [32, 512], BF16)
                umb = self.sb(ea, "umb", [128, 128], BF16)
                cif = self.sb(ea, "cif", [128, 2], F32)
                cib = self.sb(ea, "cib", [128, 2], BF16)
                qmf = self.sb(ea, "qmf", [128, 2], F32)
                ew = ExitStack()
                wa2f = self.sb(ew, "wa2f", [32, 512], F32)
                umf = self.sb(ew, "umf", [128, 128], F32)
                self.dma("sp", gpm[:], L["g_pm"], "misc", [], ["gains"])
                self.memset("dve", wa2f[:], 0.0, ["wa2f"])
                self.dma("sp", wa2f[0:16, :], L["w_a2"], "misc2", ["wa2f"], ["wa2f"])
                self.dma("sp", wa2f[16:17, :], L["b_a2"], "misc3", ["wa2f"], ["wa2f"])
                self.cp("dve", wa2a[:], wa2f[:], ["wa2f"], ["wa2a"])
                self.dma("sp", umf[:], L["umat"], "misc4", [], ["umf"])
                self.cp("dve", umb[:], umf[:], ["umf"], ["umb"])
                self.dma("sp", cif[:], L["cind"], "misc5", [], ["cif"])
                self.cp("dve", cib[:], cif[:], ["cif"], ["cib"])
                self.dma("sp", qmf[:], L["qmask"], "misc6", [], ["qmf"])
                self.ts("dve", qmf[:], qmf[:], 128.0 ** -0.5, ALU.mult, ["qmf"], ["qmf"])
                with ew:
                    self.wstage = [self.sb(ew, "wstg%d" % i, [128, 2048], F32) for i in range(2)]
                    self.load_w(WS, L["w_sh"], KD, NSH, gpm, "WS")
                    self.load_w(WO, L["w_ow"], KD, NOW, gpm, "WO")
                self.s.barrier()
                xin1 = [self.sb(ea, "xin_%d" % i, [128, 1024], F32) for i in range(3)]
                xin = [xin1, xin1]
                hTa = self.sb(ea, "hTa", [128, KD, 128], BF16)
                hT = [[hTa] + [self.sb(ea, "hT%d_%d" % (sl, i), [128, KD, 128], BF16) for i in (1, 2)] for sl in range(2)]
                hTn = [["hTa", "hT%d_1" % sl, "hT%d_2" % sl] for sl in range(2)]
                haT = [self.sb(ea, "haT%d" % i, [32, 128], BF16) for i in range(2)]
                ksb = [self.sb(ea, "ksb%d" % i, [128, 512], F32) for i in range(2)]
                e1s = self.sb(ea, "e1s", [128, 512], F32)
                e1 = [e1s, e1s]
                nl = [self.sb(ea, "nl%d" % i, [128, 512], BF16) for i in range(2)]
                dfac = [self.sb(ea, "dfac%d" % i, [128, 512], F32) for i in range(2)]
                dc = [self.sb(ea, "dc%d" % i, [128, 8], F32) for i in range(2)]
                kdec = [self.sb(ea, "kdec%d" % i, [128, 512], BF16) for i in range(2)]
                vb = [self.sb(ea, "vb%d" % i, [128, 1024], BF16) for i in range(2)]
                stc = [[self.sb(ea, "stc%d_%d" % (i, q), [128, 4], F32) for q in range(3)] for i in range(3)]
                Sst = self.sb(ea, "Sst", [128, 4, 256], F32)
                Sbf = self.sb(ea, "Sbf", [128, 4, 4, 256], BF16)
                qpad = self.sb(ea, "qpad", [128, 4, 4, 128], BF16)
                eg = self.sb(ea, "eg", [128, 1024], F32)
                sg = self.sb(ea, "sg", [128, 1024], F32)
                osb = eg
                og = self.sb(ea, "og", [128, 1024], BF16)
                ogT = self.sb(ea, "ogT", [128, KD, 128], BF16)
                cqn = self.sb(ea, "cqn", [128, 384], BF16)
                cqnT = self.sb(ea, "cqnT", [128, 3, 128], BF16)
                rcs1 = self.sb(ea, "rcs", [64, 2, 128], F32)
                t11 = self.sb(ea, "t1", [64, 128], F32)
                t21 = self.sb(ea, "t2", [64, 128], F32)
                rcs, t1, t2 = [rcs1, rcs1], [t11, t11], [t21, t21]
                sso = self.sb(ea, "sso", [128, 4], F32)
                lno = self.sb(ea, "lno", [128, 4], F32)
                rso = self.sb(ea, "rso", [128, 4], F32)
                for i in range(2):
                    self.memset("dve", haT[i][:], 1.0, ["haT%d" % i])
                self.memset("dve", Sst[:], 0.0, ["Sst"])
                self.memset("pool", qpad[:], 0.0, ["qpad"])

                rope_all = L["rope_all"]
                NP = NT // 2

                def S1(j):
                    sl = j % 2
                    items = []
                    for i in range(3):
                        src = xall[(2 * j + i) * 128:(2 * j + i + 1) * 128, :] if i < 2 else xown[j * 128:(j + 1) * 128, :]
                        tk = "xin_%d" % i
                        self.dma("sp", xin[sl][i][:], src, "xin", [], [tk])
                        items.append((xin[sl][i][:], tk, hT[sl][i][:], hTn[sl][i], i))
                    self.norm_T_multi(items)

                def proj(j, ab):
                    sl = j % 2
                    t = 2 * j + ab
                    h, hk = hT[sl][ab], hTn[sl][ab]
                    cb = 5 + ab
                    for k in range(KD):
                        self.mm(pb[cb][0:16, 0:128], WS[:, k, 1920:1936], h[:, k, :], k == 0, k == KD - 1,
                                [hk, "WS"], ["b%d" % cb])
                    self.cp("dve", haT[ab][0:16, :], pb[cb][0:16, 0:128], ["b%d" % cb], ["haT%d" % ab])
                    for (bank, c0, c1) in ((1, 0, 512), (2, 512, 1024), (3, 1024, 1536)):
                        for k in range(KD):
                            self.mm(pb[bank][:, 0:512], h[:, k, :], WS[:, k, c0:c1], k == 0, k == KD - 1,
                                    [hk, "WS"], ["b%d" % bank])
                        if bank == 1:
                            self.cp("act", ksb[ab][:], pb[1][:, 0:512], ["b1"], ["ksb%d" % ab])
                            self.mm(pb[cb][:, 0:512], haT[ab][:, :], wa2a[:, :], True, True, ["haT%d" % ab, "wa2a"],
                                    ["b%d" % cb])
                            self.act(e1[ab][:], pb[cb][:, 0:512], AF.Exp, ["b%d" % cb], ["e1s"], scale=-1.0)
                            self.act(nl[ab][:], e1[ab][:], AF.Ln, ["e1s"], ["nl%d" % ab], bias=self.onec[:, 0:1])
                        elif bank == 2:
                            self.cp("dve", vb[ab][:, 0:512], pb[2][:, 0:512], ["b2"], ["vb%d" % ab])
                        else:
                            self.cp("act", vb[ab][:, 512:1024], pb[3][:, 0:512], ["b3"], ["vb%d" % ab])
                    for k in range(KD):
                        self.mm(pb[4][:, 0:256], h[:, k, :], WS[:, k, 1536:1792], k == 0, k == KD - 1,
                                [hk, "WS"], ["b4"])
                    for (o0, c0) in ((256, 1792), (384, 1856)):
                        for k in range(KD):
                            self.mm(pb[4][0:64, o0:o0 + 128], WS[:, k, c0:c0 + 64], h[:, k, :], k == 0, k == KD - 1,
                                    [hk, "WS"], ["b4"])
                    ssc, lnc, rsc = stc[ab]
                    self.act(self.junk[:, 0:256], pb[4][:, 0:256], AF.Square, ["b4"], ["junk", "ssc%d" % ab],
                             accum=ssc[:, 0:1])
                    self.rstd_from_ss(ssc[:, 0:1], lnc[:, 0:1], rsc[:, 0:1], 256, ["ssc%d" % ab], "rsc%d" % ab)
                    self.ts("dve", vaug[:, t, 0:256], pb[4][:, 0:256], rsc[:, 0:1], ALU.mult, ["b4", "rsc%d" % ab],
                            ["vaug%d" % t])
                    self.dma("sp", rcs[ab][:, :, :], rope_all[:, :, t * 128:(t + 1) * 128].rearrange("a p n -> p a n"),
                             "rcs", ["rope_dram"], ["rcs_"])
                    self.tt("dve", t1[ab][:], pb[4][0:64, 256:384], rcs[ab][:, 0, :], ALU.mult, ["b4", "rcs_"],
                            ["t1_"])
                    self.tt("dve", t2[ab][:], pb[4][0:64, 384:512], rcs[ab][:, 1, :], ALU.mult, ["b4", "rcs_"],
                            ["t2_"])
                    self.tt("dve", kpeT[:, t * 128:(t + 1) * 128], t1[ab][:], t2[ab][:], ALU.add,
                            ["t1_", "t2_"], ["kpeT%d" % t])

                def chain(ab):
                    cb = 5 + ab
                    self.mm(pb[cb][:, 0:512], umb[:, :], nl[ab][:, :], True, True, ["umb", "nl%d" % ab], ["b%d" % cb])
                    self.act(dfac[ab][:], pb[cb][:, 0:512], AF.Exp, ["b%d" % cb], ["dfac%d" % ab], scale=-1.0 / 16.0)
                    for hh in range(4):
                        self.mm(pb[cb][:, 2 * hh:2 * hh + 2], nl[ab][:, hh * 128:(hh + 1) * 128], cib[:, :], True, True,
                                ["nl%d" % ab, "cib"], ["b%d" % cb])
                    self.act(dc[ab][:], pb[cb][:, 0:8], AF.Exp, ["b%d" % cb], ["dc%d" % ab], scale=-1.0 / 16.0)
                    self.tt("dve", kdec[ab][:], ksb[ab][:], dfac[ab][:], ALU.mult, ["ksb%d" % ab, "dfac%d" % ab],
                            ["kdec%d" % ab])

                def state(ab):
                    for c in range(2):
                        pc = 2 * ab + c
                        for half in range(2):
                            bank = 1 + 2 * c + half
                            for q in range(2):
                                hh = 2 * half + q
                                self.mm(pb[bank][:, q * 256:q * 256 + 256],
                                        kdec[ab][c * 64:(c + 1) * 64, hh * 128:(hh + 1) * 128],
                                        vb[ab][c * 64:(c + 1) * 64, hh * 256:(hh + 1) * 256], True, True,
                                        ["kdec%d" % ab, "vb%d" % ab], ["b%d" % bank])
                        for half in range(2):
                            bank = 1 + 2 * c + half
                            for q in range(2):
                                hh = 2 * half + q
                                self.stt(Sst[:, hh, :], Sst[:, hh, :], dc[ab][:, 2 * hh + c:2 * hh + c + 1],
                                         pb[bank][:, q * 256:q * 256 + 256], ALU.mult, ALU.add,
                                         ["Sst", "dc%d" % ab, "b%d" % bank], ["Sst"])
                        self.cp("act", Sbf[:, pc, :, :], Sst[:, :, :], ["Sst"], ["Sbf%d" % pc])

                def own_proj(j):
                    sl = j % 2
                    h, hk = hT[sl][2], hTn[sl][2]
                    for hh in range(4):
                        for k in range(KD):
                            self.mm(pb[1][:, hh * 128:(hh + 1) * 128], WO[:, k, hh * 128:(hh + 1) * 128], h[:, k, :],
                                    k == 0, k == KD - 1, [hk, "WO"], ["b1"])
                    qv = pb[1][:, 0:512].rearrange("p (h n) -> p h n", h=4)
                    for m in range(2):
                        for cc in range(2):
                            c = 2 * m + cc
                            self.ts("dve", qpad[:, c, :, cc * 64:(cc + 1) * 64], qv[:, :, cc * 64:(cc + 1) * 64],
                                    qmf[:, m:m + 1], ALU.mult, ["b1", "qmf"], ["qpad"])
                    for (bank, c0) in ((2, 512), (3, 1024)):
                        for k in range(KD):
                            self.mm(pb[bank][:, 0:512], h[:, k, :], WO[:, k, c0:c0 + 512], k == 0, k == KD - 1,
                                    [hk, "WO"], ["b%d" % bank])
                        o0 = c0 - 512
                        self.act(eg[:, o0:o0 + 512], pb[bank][:, 0:512], AF.Exp, ["b%d" % bank], ["eg%d" % bank], scale=-1.0)
                        self.ts("dve", eg[:, o0:o0 + 512], eg[:, o0:o0 + 512], 1.0, ALU.add, ["eg%d" % bank], ["eg%d" % bank])
                        self.recip(eg[:, o0:o0 + 512], eg[:, o0:o0 + 512], ["eg%d" % bank], ["eg%d" % bank])
                        self.tt("dve", sg[:, o0:o0 + 512], eg[:, o0:o0 + 512], pb[bank][:, 0:512], ALU.mult,
                                ["eg%d" % bank, "b%d" % bank], ["sg%d" % bank])
                    for k in range(KD):
                        self.mm(pb[4][:, 0:384], h[:, k, :], WO[:, k, 1536:1920], k == 0, k == KD - 1,
                                [hk, "WO"], ["b4"])
                    ssq, lnq, rsq = stc[2]
                    self.act(self.junk[:, 0:384], pb[4][:, 0:384], AF.Square, ["b4"], ["junk", "ssq"], accum=ssq[:, 0:1])
                    self.rstd_from_ss(ssq[:, 0:1], lnq[:, 0:1], rsq[:, 0:1], 384, ["ssq"], "rsq")
                    self.ts("dve", cqn[:], pb[4][:, 0:384], rsq[:, 0:1], ALU.mult, ["b4", "rsq"], ["cqn"])

                def own_out(j):
                    for hh in range(4):
                        bank = 5 + hh // 2
                        for c in range(4):
                            self.mm(pb[bank][:, (hh % 2) * 256:(hh % 2) * 256 + 256], qpad[:, c, hh, :],
                                    Sbf[:, c, hh, :], c == 0, c == 3, ["qpad", "Sbf%d" % c], ["b%d" % bank])
                    for hf in range(2):
                        self.cp("act", osb[:, hf * 512:(hf + 1) * 512], pb[5 + hf][:, 0:512], ["b%d" % (5 + hf)],
                                ["eg%d" % (2 + hf)])
                    for hh in range(4):
                        self.act(self.junk[:, 0:256], osb[:, hh * 256:(hh + 1) * 256], AF.Square,
                                 ["eg%d" % (2 + hh // 2)], ["junk", "sso"], accum=sso[:, hh:hh + 1])
                    self.act(lno[:], sso[:], AF.Ln, ["sso"], ["lno"], scale=1.0 / 256, bias=self.epsc[:, 0:1])
                    self.act(rso[:], lno[:], AF.Exp, ["lno"], ["rso"], scale=-0.5)
                    for hh in range(4):
                        self.stt(og[:, hh * 256:(hh + 1) * 256], osb[:, hh * 256:(hh + 1) * 256],
                                 rso[:, hh:hh + 1], sg[:, hh * 256:(hh + 1) * 256], ALU.mult, ALU.mult,
                                 ["eg%d" % (2 + hh // 2), "rso", "sg%d" % (2 + hh // 2)], ["og"])
                    for k in range(KD):
                        self.tr(pb16[0][:, k * 128:(k + 1) * 128], og[:, k * 128:(k + 1) * 128], ["og"], ["b0"])
                    self.cp("act", ogT[:], pb16[0][:, 0:1024].rearrange("p (k n) -> p k n", k=KD), ["b0"], ["ogT"])
                    gi, ii = j // GT, j % GT
                    self.dma("sp", L["sc_og"][gi, :, :, ii * 128:(ii + 1) * 128], ogT[:], "ogst", ["ogT"], ["sc_og"])
                    for k in range(3):
                        self.tr(pb16[0][:, k * 128:(k + 1) * 128], cqn[:, k * 128:(k + 1) * 128], ["cqn"], ["b0"])
                    self.cp("dve", cqnT[:], pb16[0][:, 0:384].rearrange("p (k n) -> p k n", k=3), ["b0"], ["cqnT"])
                    self.dma("sp", L["sc_cq"][j], cqnT[:], "cqst", ["cqnT"], ["sc_cq"])

                self.probe('A')
                S1(0)
                for j in range(NP):
                    proj(j, 0)
                    if j + 1 < NP:
                        S1(j + 1)
                    proj(j, 1)
                    chain(0)
                    own_proj(j)
                    chain(1)
                    state(0)
                    state(1)
                    own_out(j)
            self.s.barrier()
            if self.upto >= 2:
                self.pass_B1(es, L)
                self.s.barrier()
        if self.upto >= 3:
            self.pass_B2(es_outer, L)
            self.s.barrier()
        if self.upto >= 4:
            self.pass_C(es_outer, L)

    def dbg_dump_A(self, L):
        pass

    def pass_B1(self, es_outer, L):
        nc, S, NT, NOWN, NG = self.nc, self.S, self.NT, self.NOWN, self.NG
        pb, pb16 = self.pb, self.pb16
        vaug, kpeT = self.vaug, self.kpeT
        SCALE = 192.0 ** -0.5
        with ExitStack() as es:
            ckvT = self.sb(es, "ckvT", [128, 2, S], BF16)
            WUQ = self.sb(es, "WUQ", [128, 3, 2048], BF16)
            WUKT = self.sb(es, "WUKT", [128, 1, 2048], BF16)
            WUV = self.sb(es, "WUV", [128, 2, 1024], BF16)
            gq = self.sb(es, "gq", [128, 3], F32)
            gkv = self.sb(es, "gkv", [128, 2], F32)
            amf = self.sb(es, "amf", [128, 256], F32)
            amb = self.sb(es, "amb", [128, 2, 128], BF16)
            self.dma("sp", gq[:], L["g_q"], "misc", [], ["gains"])
            self.dma("sp", gkv[:], L["g_kv"], "misc2", [], ["gains"])
            self.dma("sp", amf[:], L["amask"], "misc3", [], ["amf"])
            self.cp("dve", amb[:].rearrange("p a n -> p (a n)"), amf[:], ["amf"], ["amb"])
            with ExitStack() as ew:
                self.wstage = [self.sb(ew, "wstgb%d" % i, [128, 2048], F32) for i in range(2)]
                self.load_w(WUQ, L["w_uq"], 3, 2048, gq, "WUQ")
                self.load_w(WUKT, L["w_ukT"], 1, 2048, None, "WUKT")
                self.load_w(WUV, L["w_uv"], 2, 1024, gkv, "WUV")
            self.s.barrier()
            for t in range(NT):
                for lc in range(2):
                    self.tr(pb16[0][:, lc * 128:(lc + 1) * 128], vaug[:, t, lc * 128:(lc + 1) * 128],
                            ["vaug%d" % t], ["b0"])
                self.cp("act" if t % 2 else "dve", ckvT[:, :, t * 128:(t + 1) * 128],
                        pb16[0][:, 0:256].rearrange("p (a n) -> p a n", a=2), ["b0"], ["ckvT%d" % t])
            import os
            B1STOP = int(os.environ.get("B1STOP", "9"))
            if B1STOP < 1:
                return
            cqT = [self.sb(es, "cqT%d" % i, [128, 3, 128], BF16) for i in range(2)]
            rco = [self.sb(es, "rco%d" % i, [64, 2, 128], F32) for i in range(2)]
            qn = [self.sb(es, "qn%d" % i, [128, 128], BF16) for i in range(2)]
            qpe = self.sb(es, "qpe", [64, 8, 128], BF16)
            qabs = self.sb(es, "qabs", [128, 2, 8, 128], BF16)
            t1 = self.sb(es, "bt1", [64, 128], F32)
            t2 = self.sb(es, "bt2", [64, 128], F32)
            PT = [self.sb(es, "PT%d" % i, [128, 4, 128], BF16) for i in range(3)]
            rsum = self.sb(es, "rsum", [128, 8], F32)
            olat = self.sb(es, "olat", [128, 8, 256], BF16)
            olatT = self.sb(es, "olatT", [128, 8, 2, 128], BF16)
            omT = self.sb(es, "omT", [128, 8, 128], BF16)
            self.probe('B1')
            for j in range(NOWN):
                sl = j % 2
                XV = int(os.environ.get("XV", "3"))
                if XV & 1:
                    self.dma("sp", cqT[sl][:], L["sc_cq"][j], "cqT%d" % sl, ["sc_cq"], ["cqT%d" % sl])
                if XV & 2:
                    self.dma("sp", rco[sl][:], L["rope_own"][:, :, j * 128:(j + 1) * 128].rearrange("a p n -> p a n"),
                         "rco%d" % sl, ["rope_dram"], ["rco%d" % sl])
                cq = cqT[sl]
                QV = int(os.environ.get("QV", "9"))
                for h in range(8):
                    if QV < 2:
                        continue
                    qs = h % 2
                    qb = 1 + qs
                    qbt = "b%d" % qb
                    for k in range(3):
                        self.mm(pb[qb][:, 0:128], WUQ[:, k, h * 256:h * 256 + 128], cq[:, k, :], k == 0, k == 2,
                                ["cqT%d" % sl, "WUQ"], [qbt])
                    for (o0, c0) in ((128, 128), (256, 192)):
                        for k in range(3):
                            self.mm(pb[qb][0:64, o0:o0 + 128], WUQ[:, k, h * 256 + c0:h * 256 + c0 + 64], cq[:, k, :],
                                    k == 0, k == 2, ["cqT%d" % sl, "WUQ"], [qbt])
                    YV = int(os.environ.get("YV", "9"))
                    if YV < 1:
                        continue
                    self.cp("act", qn[qs][:], pb[qb][:, 0:128], [qbt], ["qn%d" % qs])
                    if YV < 2:
                        continue
                    self.tt("dve", t1[:], pb[qb][0:64, 128:256], rco[sl][:, 0, :], ALU.mult, [qbt, "rco%d" % sl], ["bt1"])
                    self.tt("dve", t2[:], pb[qb][0:64, 256:384], rco[sl][:, 1, :], ALU.mult, [qbt, "rco%d" % sl], ["bt2"])
                    self.tt("dve", qpe[:, h, :], t1[:], t2[:], ALU.add, ["bt1", "bt2"], ["qpe"])
                    bank = 3 + qs
                    if QV < 3:
                        continue
                    for lc in range(2):
                        self.mm(pb[bank][:, lc * 128:(lc + 1) * 128], WUKT[:, 0, h * 256 + lc * 128:h * 256 + (lc + 1) * 128],
                                qn[qs][:], True, True, ["qn%d" % qs, "WUKT"], ["b%d" % bank])
                    for lc in range(2):
                        self.ts("dve", qabs[:, lc, h, :], pb[bank][:, lc * 128:(lc + 1) * 128], gkv[:, lc:lc + 1],
                                ALU.mult, ["b%d" % bank, "gains"], ["qabs"])
                nkt = 2 * j + 2
                if B1STOP < 2:
                    continue
                for gi in range(2):
                    def scores(kt):
                        sbk = 5 + (kt % 3)
                        for lc in range(2):
                            self.mm(pb[sbk][:, 0:512], ckvT[:, lc, kt * 128:(kt + 1) * 128],
                                    qabs[:, lc, 4 * gi:4 * gi + 4, :], lc == 0, False,
                                    ["ckvT%d" % kt, "qabs"], ["b%d" % sbk])
                        self.mm(pb[sbk][:, 0:512], kpeT[:, kt * 128:(kt + 1) * 128], qpe[:, 4 * gi:4 * gi + 4, :],
                                False, True, ["kpeT%d" % kt, "qpe"], ["b%d" % sbk])

                    scores(0)
                    scores(1)
                    for kt in range(nkt):
                        sbk = 5 + (kt % 3)
                        ps = kt % 3
                        self.act(PT[ps][:], pb[sbk][:, 0:512].rearrange("p (h n) -> p h n", h=4), AF.Exp,
                                 ["b%d" % sbk], ["PT%d" % ps], scale=SCALE)
                        if kt >= nkt - 2:
                            r = kt - (nkt - 2)
                            for hh in range(4):
                                self.tt("dve", PT[ps][:, hh, :], PT[ps][:, hh, :], amb[:, r, :], ALU.mult,
                                        ["PT%d" % ps, "amb"], ["PT%d" % ps])
                        if kt + 2 < nkt:
                            scores(kt + 2)
                        for hh in range(4):
                            self.mm(pb[1 + hh][:, 0:258], PT[ps][:, hh, :], vaug[:, kt, 0:258], kt == 0, kt == nkt - 1,
                                    ["PT%d" % ps, "vaug%d" % kt, "vaug_ones"], ["b%d" % (1 + hh)])
                    for hh in range(4):
                        h = 4 * gi + hh
                        self.recip(rsum[:, h:h + 1], pb[1 + hh][:, 256:257], ["b%d" % (1 + hh)], ["rsum"])
                        self.ts("dve", olat[:, h, :], pb[1 + hh][:, 0:256], rsum[:, h:h + 1], ALU.mult,
                                ["b%d" % (1 + hh), "rsum"], ["olat"])
                if B1STOP < 3:
                    continue
                for half in range(2):
                    for hh in range(4):
                        h = 4 * half + hh
                        for lc in range(2):
                            self.tr(pb16[0][:, (hh * 2 + lc) * 128:(hh * 2 + lc + 1) * 128],
                                    olat[:, h, lc * 128:(lc + 1) * 128], ["olat"], ["b0"])
                    self.cp("act", olatT[:, 4 * half:4 * half + 4, :, :].rearrange("p h a n -> p (h a n)"),
                            pb16[0][:, 0:1024], ["b0"], ["olatT"])
                for half in range(2):
                    bank = 1 + half
                    for hh in range(4):
                        h = 4 * half + hh
                        for lc in range(2):
                            self.mm(pb[bank][:, hh * 128:(hh + 1) * 128], WUV[:, lc, h * 128:(h + 1) * 128],
                                    olatT[:, h, lc, :], lc == 0, lc == 1, ["olatT", "WUV"], ["b%d" % bank])
                    self.cp("act", omT[:, 4 * half:4 * half + 4, :].rearrange("p h n -> p (h n)"), pb[bank][:, 0:512],
                            ["b%d" % bank], ["omT"])
                gi_, ii = j // GT, j % GT
                self.dma("sp", L["sc_om"][gi_, :, :, ii * 128:(ii + 1) * 128], omT[:], "omst", ["omT"], ["sc_om"])

    def post_norm_res(self, banks, btoks, gbc, xres, xtok, outt, outtok, u):
        ss, lnv, rs = self.st_ss[u], self.st_ln[u], self.st_rs[u]
        for hf in range(2):
            self.act(self.junk[:, 0:512], banks[hf][:, 0:512], AF.Square, [btoks[hf]], ["junk", "pss%d" % u],
                     accum=ss[:, 2 + hf:3 + hf])
        self.tt("dve", ss[:, 4:5], ss[:, 2:3], ss[:, 3:4], ALU.add, ["pss%d" % u], ["pss2%d" % u])
        self.rstd_from_ss(ss[:, 4:5], lnv[:, 4:5], rs[:, 4:5], D, ["pss2%d" % u], "prs%d" % u)
        for hf in range(2):
            self.stt(self.ptmp[:, hf * 512:(hf + 1) * 512], banks[hf][:, 0:512], rs[:, 4:5],
                     gbc[:, hf * 512:(hf + 1) * 512], ALU.mult, ALU.mult, [btoks[hf], "prs%d" % u, "gbc"], ["ptmp"])
        self.tt("dve", outt, self.ptmp[:], xres, ALU.add, ["ptmp", xtok], [outtok])

    def pass_B2(self, es_outer, L):
        nc, S, NT, NOWN, NG = self.nc, self.S, self.NT, self.NOWN, self.NG
        pb, pb16 = self.pb, self.pb16
        with ExitStack() as es:
            WG = self.sb(es, "WG", [128, KD, 2048], BF16)
            WOG = self.sb(es, "WOG", [128, KD, 1024], BF16)
            WOM = self.sb(es, "WOM", [128, KD, 1024], BF16)
            WOUT = self.sb(es, "WOUT", [128, KD, 1024], BF16)
            gpm = self.sb(es, "gpm2", [128, 8], F32)
            ggl = self.sb(es, "ggl", [128, 8], F32)
            bg = self.sb(es, "bg", [128, 16], F32)
            gbc = self.sb(es, "gbc", [128, 1024], F32)
            self.dma("sp", gpm[:], L["g_pm"], "misc", [], ["gains"])
            self.dma("sp", ggl[:], L["g_gla"], "misc2", [], ["gains"])
            self.dma("sp", bg[:], L["b_g"], "misc3", [], ["bg"])
            self.dma("sp", gbc[:], L["g_pmix"][0:1, :].partition_broadcast(128), "misc4", [], ["gbc"])
            self.wstage = [self.sb(es, "wstgc%d" % i, [128, 2048], F32) for i in range(2)]
            self.load_w(WG, L["w_g"], KD, 2048, gpm, "WG")
            self.load_w(WOG, L["w_og"], KD, 1024, ggl, "WOG")
            self.load_w(WOM, L["w_om"], KD, 1024, None, "WOM")
            join_wout = self.load_w(WOUT, L["w_out"], KD, 1024, None, "WOUT", defer=True)
            xg = [self.sb(es, "xg%d" % i, [128, 1024], F32) for i in range(GT)]
            hTg = self.sb(es, "hTg", [128, KD, 512], BF16)
            ogg = self.sb(es, "ogg", [128, KD, 512], BF16)
            omg = self.sb(es, "omg", [128, KD, 512], BF16)
            ga = [self.sb(es, "ga%d" % i, [128, 512], F32) for i in range(2)]
            gb = [self.sb(es, "gb%d" % i, [128, 512], F32) for i in range(2)]
            m1 = [self.sb(es, "m1%d" % i, [128, 512], F32) for i in range(2)]
            m2 = [self.sb(es, "m2%d" % i, [128, 512], F32) for i in range(2)]
            mixT = self.sb(es, "mixT", [128, KD, 512], BF16)
            self.ptmp = self.sb(es, "ptmp", [128, 1024], F32)
            self.probe('B2')
            for g in range(NG):
                self.dma("sp", ogg[:], L["sc_og"][g], "ogg", ["sc_og"], ["ogg"])
                self.dma("sp", omg[:], L["sc_om"][g], "omg", ["sc_om"], ["omg"])
                items = []
                for i in range(GT):
                    j = g * GT + i
                    self.dma("sp", xg[i][:], L["xown"][j * 128:(j + 1) * 128, :], "xg%d" % i, [], ["xg%d" % i])
                    items.append((xg[i][:], "xg%d" % i, hTg[:, :, i * 128:(i + 1) * 128], "hTg", i))
                self.norm_T_multi(items)
                for fc in range(KD):
                    st = fc % 2
                    bs = (1, 2, 3, 4) if st == 0 else (5, 6, 7, 0)
                    srcs = ((WG, 0, hTg, "hTg", "WG"), (WG, 1024, hTg, "hTg", "WG"),
                            (WOG, 0, ogg, "ogg", "WOG"), (WOM, 0, omg, "omg", "WOM"))
                    for bi, (Wt, off, rhs, rtok, wtok) in enumerate(srcs):
                        for k in range(KD):
                            self.mm(pb[bs[bi]][:, 0:512], Wt[:, k, off + fc * 128:off + (fc + 1) * 128], rhs[:, k, :],
                                    k == 0, k == KD - 1, [rtok, wtok], ["b%d" % bs[bi]])
                    self.act(ga[st][:], pb[bs[0]][:, 0:512], AF.Sigmoid, ["b%d" % bs[0], "bg"], ["ga%d" % st],
                             bias=bg[:, fc:fc + 1])
                    self.act(gb[st][:], pb[bs[1]][:, 0:512], AF.Sigmoid, ["b%d" % bs[1], "bg"], ["gb%d" % st],
                             bias=bg[:, 8 + fc:9 + fc])
                    self.tt("dve", m1[st][:], ga[st][:], pb[bs[2]][:, 0:512], ALU.mult, ["ga%d" % st, "b%d" % bs[2]],
                            ["m1%d" % st])
                    self.tt("dve", m2[st][:], gb[st][:], pb[bs[3]][:, 0:512], ALU.mult, ["gb%d" % st, "b%d" % bs[3]],
                            ["m2%d" % st])
                    self.tt("dve", mixT[:, fc, :], m1[st][:], m2[st][:], ALU.add, ["m1%d" % st, "m2%d" % st], ["mixT"])
                if g == 0:
                    join_wout()
                for i in range(GT):
                    j = g * GT + i
                    bs = (1, 2) if i % 2 == 0 else (3, 4)
                    for hf in range(2):
                        for k in range(KD):
                            self.mm(pb[bs[hf]][:, 0:512], mixT[:, k, i * 128:(i + 1) * 128], WOUT[:, k, hf * 512:(hf + 1) * 512],
                                    k == 0, k == KD - 1, ["mixT", "WOUT"], ["b%d" % bs[hf]])
                    self.post_norm_res([pb[bs[0]], pb[bs[1]]], ["b%d" % bs[0], "b%d" % bs[1]], gbc, xg[i][:], "xg%d" % i,
                                       xg[i][:], "xg%d" % i, 2 + i % 2)
                    self.dma("sp", L["sc_x1"][j * 128:(j + 1) * 128, :], xg[i][:], "x1st%d" % i, ["xg%d" % i], ["sc_x1"])

    def pass_C(self, es_outer, L):
        nc, S, NT, NOWN, NG = self.nc, self.S, self.NT, self.NOWN, self.NG
        pb, pb16 = self.pb, self.pb16
        with ExitStack() as es:
            self.probe('Cstart')
            WFG = self.sb(es, "WFG", [128, KD, DFF], BF16)
            WFU = self.sb(es, "WFU", [128, KD, DFF], BF16)
            WFD = self.sb(es, "WFD", [128, KF, 1024], BF16)
            gpf = self.sb(es, "gpf", [128, 8], F32)
            gbc = self.sb(es, "gbc2", [128, 1024], F32)
            self.dma("sp", gpf[:], L["g_pf"], "misc", [], ["gains"])
            self.dma("sp", gbc[:], L["g_pffn"][0:1, :].partition_broadcast(128), "misc4", [], ["gbc"])
            with ExitStack() as ew:
                self.wstage = [self.sb(ew, "wstgd%d" % i, [128, 2048], F32) for i in range(2)]
                self.load_w(WFG, L["w_fg"], KD, DFF, gpf, "WFG")
                self.load_w(WFU, L["w_fu"], KD, DFF, gpf, "WFU")
                self.load_w(WFD, L["w_fd"], KF, 1024, None, "WFD")
            self.s.barrier()
            self.probe('C0')
            xg = [self.sb(es, "xc%d" % i, [128, 1024], F32) for i in range(GT)]
            hTg = self.sb(es, "hTc", [128, KD, 512], BF16)
            actT = self.sb(es, "actT", [128, KF, 512], BF16)
            sl = [self.sb(es, "sl%d" % i, [128, 512], F32) for i in range(2)]
            self.ptmp = self.sb(es, "ptmp2", [128, 1024], F32)
            self.probe('C')
            for g in range(NG):
                items = []
                for i in range(GT):
                    j = g * GT + i
                    self.dma("sp", xg[i][:], L["sc_x1"][j * 128:(j + 1) * 128, :], "xg%d" % i, ["sc_x1"], ["xg%d" % i])
                    items.append((xg[i][:], "xg%d" % i, hTg[:, :, i * 128:(i + 1) * 128], "hTg", i))
                self.norm_T_multi(items)
                for fc in range(KF):
                    st = fc % 2
                    bg_, bu_ = (1 + 2 * (fc % 3), 2 + 2 * (fc % 3))
                    for (Wt, bank, wtok) in ((WFG, bg_, "WFG"), (WFU, bu_, "WFU")):
                        for k in range(KD):
                            self.mm(pb[bank][:, 0:512], Wt[:, k, fc * 128:(fc + 1) * 128], hTg[:, k, :], k == 0, k == KD - 1,
                                    ["hTg", wtok], ["b%d" % bank])
                    self.act(sl[st][:], pb[bg_][:, 0:512], AF.Silu, ["b%d" % bg_], ["sl%d" % st])
                    self.tt("dve", actT[:, fc, :], sl[st][:], pb[bu_][:, 0:512], ALU.mult, ["sl%d" % st, "b%d" % bu_],
                            ["actT"])
                for i in range(GT):
                    j = g * GT + i
                    bs = (1, 2) if i % 2 == 0 else (3, 4)
                    for hf in range(2):
                        for k in range(KF):
                            self.mm(pb[bs[hf]][:, 0:512], actT[:, k, i * 128:(i + 1) * 128], WFD[:, k, hf * 512:(hf + 1) * 512],
                                    k == 0, k == KF - 1, ["actT", "WFD"], ["b%d" % bs[hf]])
                    self.post_norm_res([pb[bs[0]], pb[bs[1]]], ["b%d" % bs[0], "b%d" % bs[1]], gbc, xg[i][:], "xg%d" % i,
                                       xg[i][:], "xg%d" % i, 2 + i % 2)
                    self.dma("sp", L["out"][j * 128:(j + 1) * 128, :], xg[i][:], "outst%d" % i, ["xg%d" % i], ["out_dram"])


def _consts(parity):
    inv = (1.0 / (10000.0 ** (np.arange(0, 64, 2, dtype=np.float32) / np.float32(64)))).astype(np.float32)
    invf = np.concatenate([inv, inv]).reshape(64, 1).astype(np.float32)
    ident = np.eye(128, dtype=np.float32)
    s = np.arange(128)[:, None]
    t = np.arange(128)[None, :]
    umat = ((s // 64 == t // 64) & (s > t)).astype(np.float32)
    cind = (s // 64 == np.arange(2)[None, :]).astype(np.float32)
    qmask = np.zeros((128, 2), np.float32)
    qmask[:, parity] = 1.0
    diag = ((s // 64) <= (t // 64)).astype(np.float32)
    amask = np.zeros((128, 2, 128), np.float32)
    if parity == 0:
        amask[:, 0, :] = diag
    else:
        amask[:, 0, :] = 1.0
        amask[:, 1, :] = diag
    return dict(invf=invf, ident=ident, umat=umat, cind=cind, qmask=qmask,
                amask=np.ascontiguousarray(amask.reshape(128, 256)))


def _pk(v, n):
    return np.ascontiguousarray(v.reshape(n, 128).T).astype(np.float32)


def _weights(inp):
    w_in = inp["w_in"][0]
    sp = np.cumsum([0, 512, 512, 1024, 1024, 16, 384, 256, 64])
    q, k, v, g, ha, cq, ckv, kpe = [w_in[:, sp[i]:sp[i + 1]] for i in range(8)]
    kpesw = np.concatenate([kpe[:, 32:], kpe[:, :32]], axis=1)
    w_sh = np.ascontiguousarray(np.concatenate([k, v, ckv, kpe, kpesw, ha], axis=1))
    w_ow = np.ascontiguousarray(np.concatenate([q, g, cq], axis=1))
    w_uq = inp["w_uq"][0].reshape(384, 8, 192)
    nope, rope = w_uq[:, :, :128], w_uq[:, :, 128:]
    ropesw = np.concatenate([rope[:, :, 32:], rope[:, :, :32]], axis=2)
    w_uq2 = np.ascontiguousarray(np.concatenate([nope, rope, ropesw], axis=2).reshape(384, 2048))
    w_ukv = inp["w_ukv"][0].reshape(256, 8, 256)
    w_ukT = np.ascontiguousarray(w_ukv[:, :, :128].transpose(2, 1, 0).reshape(128, 2048))
    w_uv = np.ascontiguousarray(w_ukv[:, :, 128:].reshape(256, 1024))
    gla = inp["gla_norm"][0]
    return dict(
        w_sh=w_sh, w_ow=w_ow, g_pm=_pk(inp["pre_mix_norm"][0], 8),
        w_a2=np.ascontiguousarray(inp["w_a2"][0]), b_a2=np.ascontiguousarray(inp["b_a2"][0].reshape(1, 512)),
        g_gla=_pk(np.tile(gla, 4), 8), w_og=np.ascontiguousarray(inp["w_o_gla"][0]),
        g_q=_pk(inp["q_norm"][0], 3), w_uq=w_uq2, g_kv=_pk(inp["kv_norm"][0], 2),
        w_ukT=w_ukT, w_uv=w_uv, w_om=np.ascontiguousarray(inp["w_o_mla"][0]),
        w_g=np.ascontiguousarray(inp["w_gate"][0]), b_g=_pk(inp["b_gate"][0], 16),
        w_out=np.ascontiguousarray(inp["w_out"][0]),
        g_pmix=np.ascontiguousarray(inp["post_mix_norm"][0].reshape(1, D)),
        g_pf=_pk(inp["pre_ffn_norm"][0], 8),
        w_fg=np.ascontiguousarray(inp["w_ffn_gate"][0]), w_fu=np.ascontiguousarray(inp["w_ffn_up"][0]),
        w_fd=np.ascontiguousarray(inp["w_ffn_down"][0]),
        g_pffn=np.ascontiguousarray(inp["post_ffn_norm"][0].reshape(1, D)),
    )


def make_in_maps(inp, S):
    x = np.asarray(inp["x"], dtype=np.float32)
    pos = np.asarray(inp["positions"], dtype=np.int32)
    W = _weights({k: np.asarray(v) for k, v in inp.items()})
    maps = []
    NT = S // 128
    for core in range(8):
        b, par = core // 2, core % 2
        xb = x[b, :S]
        xo = xb.reshape(NT // 2, 2, 128, D)[:, par].reshape(S // 2, D)
        pb = pos[b, :S]
        po = pb.reshape(NT // 2, 2, 128)[:, par].reshape(1, S // 2)
        m = dict(xall=np.ascontiguousarray(xb), xown=np.ascontiguousarray(xo),
                 posall=np.ascontiguousarray(pb.reshape(1, S)), posown=np.ascontiguousarray(po))
        m.update(_consts(par))
        m.update(W)
        maps.append(m)
    return maps


def run(inp, S, upto=99, dbg=False):
    kb = K(S, upto=upto, dbg=dbg)
    nc = kb.build()
    maps = make_in_maps(inp, S)
    maps = [{k: v for k, v in m.items() if k in kb.in_names} for m in maps]
    res = run_bass_kernel_spmd(nc, maps, core_ids=list(range(8)))
    B = 4
    NT = S // 128
    full = np.zeros((B, NT // 2, 2, 128, D), np.float32)
    for core in range(8):
        b, par = core // 2, core % 2
        full[b, :, par] = np.asarray(res.results[core]["out"]).reshape(NT // 2, 128, D)
    return full.reshape(B, S, D), res


def kernel(**inputs):
    o, _ = run(inputs, 8192)
    return o
```

```python
import math
from contextlib import ExitStack

import numpy as np
import concourse.bass as bass
import concourse.mybir as mybir
from concourse.bass_utils import run_bass_kernel_spmd

F32 = mybir.dt.float32
BF16 = mybir.dt.bfloat16
I32 = mybir.dt.int32
AF = mybir.ActivationFunctionType
ALU = mybir.AluOpType

D = 1024
KD = 8
DFF = 2816
KF = 22
EPS = 1e-6
NSH = 1936
NOW = 1920
GT = 4


PSUM_TOKENS = frozenset("b%d" % i for i in range(8))


class Op:
    __slots__ = ("eng", "fn", "deps", "signal", "sem", "count", "key", "idx", "label")

    def __init__(self, eng, fn, key):
        self.eng, self.fn, self.key = eng, fn, key
        self.deps, self.signal, self.sem, self.count = [], False, None, 0


class Sched:
    ENGS = ("pe", "act", "dve", "pool", "sp")

    def __init__(self):
        self.ops = {e: [] for e in self.ENGS}
        self.last_w = {}
        self.readers = {}
        self.nops = 0

    def add(self, eng, fn, r=(), w=(), key=None):
        op = Op(eng, fn, key)
        op.label = getattr(self, "label", "")
        if key is not None:
            op.signal = True
        op.idx = self.nops
        self.nops += 1
        raw = set()
        deps = set()
        for t in r:
            lw = self.last_w.get(t)
            if lw is not None:
                deps.add(lw)
                raw.add(lw)
            if t in PSUM_TOKENS:
                for rd in self.readers.get(t, {}).values():
                    if rd.eng != eng:
                        deps.add(rd)
        for t in w:
            lw = self.last_w.get(t)
            if lw is not None:
                deps.add(lw)
            for rd in self.readers.get(t, {}).values():
                deps.add(rd)
        for d in deps:
            if d is op:
                continue
            if d.key is None and key is None and d.eng == eng:
                if eng == "pe":
                    continue
            op.deps.append(d)
            d.signal = True
        for t in r:
            rk = eng if key is None else ("dma", op.idx)
            self.readers.setdefault(t, {})[rk] = op
        for t in w:
            self.last_w[t] = op
            self.readers[t] = {}
        self.ops[eng].append(op)
        return op

    def barrier(self):
        lasts = []
        for e in self.ENGS:
            last_eng = None
            last_dma = {}
            for op in self.ops[e]:
                if op.key is None:
                    last_eng = op
                else:
                    last_dma[op.key] = op
            if last_eng is not None:
                lasts.append(last_eng)
            lasts.extend(last_dma.values())
        for e in self.ENGS:
            op = Op(e, lambda eng: eng.nop(), None)
            op.label = "barrier"
            op.idx = self.nops
            self.nops += 1
            for d in lasts:
                if d.key is None and d.eng == e:
                    continue
                op.deps.append(d)
                d.signal = True
            self.ops[e].append(op)

    def finalize(self, nc, es):
        self.sems = {}
        cnt = {}
        for e in self.ENGS:
            for op in self.ops[e]:
                if not op.signal:
                    continue
                if op.key is not None:
                    k = ("dma", op.key)
                    inc = 16
                else:
                    k = ("eng", e)
                    inc = 1
                if k not in self.sems:
                    self.sems[k] = es.enter_context(nc.semaphore("s_%s_%s" % (k[0], str(k[1]))))
                    cnt[k] = 0
                cnt[k] += inc
                op.sem = k
                op.count = cnt[k]

    def emit(self, eng_name, engine):
        waited = {}
        for op in self.ops[eng_name]:
            need = {}
            for d in op.deps:
                if need.get(d.sem, 0) < d.count:
                    need[d.sem] = d.count
            for k, v in need.items():
                if waited.get(k, 0) < v:
                    engine.wait_ge(self.sems[k], v)
                    waited[k] = v
            ins = op.fn(engine)
            if op.signal:
                ins.then_inc(self.sems[op.sem], 16 if op.key is not None else 1)


class K:
    def __init__(self, S, upto=99, dbg=False):
        self.S = S
        self.NT = S // 128
        self.NOWN = self.NT // 2
        self.NG = self.NOWN // GT
        assert self.NG * GT * 2 * 128 == S
        self.upto = upto
        self.dbg = dbg
        self.nc = bass.Bass("TRN2", target_bir_lowering=False)
        self.s = Sched()
        self.dkeys = {}

    def mm(self, out, lhsT, rhs, start, stop, r, w):
        return self.s.add("pe", lambda e: e.matmul(out, lhsT, rhs, start=start, stop=stop), r, w)

    def tr(self, out, in_, r, w):
        ident = self.identb[:]
        return self.s.add("pe", lambda e: e.transpose(out, in_, ident), r + ["ident"], w)

    def act(self, out, in_, func, r, w, scale=None, bias=None, accum=None):
        kw = {}
        if scale is not None:
            kw["scale"] = scale
        if bias is not None:
            kw["bias"] = bias
        if accum is not None:
            kw["accum_out"] = accum
        return self.s.add("act", lambda e: e.activation(out, in_, func, **kw), r, w)

    def ts(self, eng, out, in0, s1, op0, r, w, s2=None, op1=None):
        if op1 is None:
            return self.s.add(eng, lambda e: e.tensor_scalar(out, in0, s1, None, op0), r, w)
        return self.s.add(eng, lambda e: e.tensor_scalar(out, in0, s1, s2, op0, op1), r, w)

    def tt(self, eng, out, in0, in1, op, r, w):
        return self.s.add(eng, lambda e: e.tensor_tensor(out, in0, in1, op), r, w)

    def stt(self, out, in0, scalar, in1, op0, op1, r, w):
        return self.s.add("dve", lambda e: e.scalar_tensor_tensor(out, in0, scalar, in1, op0, op1), r, w)

    def cp(self, eng, out, in_, r, w):
        if eng == "act":
            return self.s.add("act", lambda e: e.activation(out, in_, AF.Copy), r, w)
        return self.s.add(eng, lambda e: e.tensor_copy(out, in_), r, w)

    def memset(self, eng, ap, val, w):
        return self.s.add(eng, lambda e: e.memset(ap, val), [], w)

    def recip(self, out, in_, r, w):
        return self.s.add("dve", lambda e: e.reciprocal(out, in_), r, w)

    def dma(self, q, out, in_, key, r, w):
        base = key.rstrip("0123456789")
        ch = "chain_" + base
        return self.s.add(q, lambda e: e.dma_start(out, in_), list(r) + [ch], list(w) + [ch], key=base)

    def sb(self, es, name, shape, dt):
        return es.enter_context(self.nc.sbuf_tensor(name, list(shape), dt))

    def din(self, name, shape, dt=F32):
        t = self.nc.dram_tensor(name, list(shape), dt, kind="ExternalInput").ap()
        self.in_names.append(name)
        return t

    def dscr(self, name, shape, dt):
        return self.nc.dram_tensor(name, list(shape), dt, kind="Internal").ap()

    def probe(self, tag):
        import os
        if not os.environ.get("SBPROBE"):
            return
        try:
            with self.nc.sbuf_tensor("probe_" + tag, [128, 100000], F32):
                pass
        except AssertionError as e:
            msg = str(e)
            i = msg.find("(base=")
            print("SBUF", tag, msg[i:i + 40])

    def rstd_from_ss(self, ss, lnv, rstd, n, rt, wt):
        self.act(lnv, ss, AF.Ln, rt, [wt + "_ln"], scale=1.0 / n, bias=self.epsc[:, 0:1])
        self.act(rstd, lnv, AF.Exp, [wt + "_ln"], [wt], scale=-0.5)

    def load_w(self, dst, src, nk, ncol, gain, tok, defer=False):
        engs = ("dve", "act")
        CB = self.wcb
        for k in range(nk):
            for c0 in range(0, ncol, CB):
                c1 = min(ncol, c0 + CB)
                sl = self.wslot % 2
                self.wslot += 1
                stg = self.wstage[sl]
                self.dma("sp", stg[:, 0:c1 - c0], src[k * 128:(k + 1) * 128, c0:c1], "wst" + "AB"[sl],
                         [], ["wst%d" % sl])
                eng = engs[self.wrot % 2]
                self.wrot += 1
                o = dst[:, k, c0:c1]
                i = stg[:, 0:c1 - c0]
                wt = [tok + "#" + eng]
                if gain is not None:
                    g = gain[:, k:k + 1]
                    if eng == "act":
                        self.act(o, i, AF.Copy, ["wst%d" % sl, "gains"], wt, scale=g)
                    else:
                        self.ts(eng, o, i, g, ALU.mult, ["wst%d" % sl, "gains"], wt)
                else:
                    self.cp(eng, o, i, ["wst%d" % sl], wt)

        def join():
            lab = getattr(self.s, "label", "")
            self.s.label = "join_" + tok
            self.s.add("pe", lambda e: e.nop(), [tok + "#dve", tok + "#act"], [tok])
            self.s.label = lab
        if defer:
            return join
        join()
        return None

    def norm_T(self, xt, xtok, hT_ap, htok, u):
        self.norm_T_multi([(xt, xtok, hT_ap, htok, u)])

    def norm_T_multi(self, items):
        junk = self.junk
        for (xt, xtok, hT_ap, htok, u) in items:
            self.act(junk[:], xt, AF.Square, [xtok], ["junk", "ss%d" % u], accum=self.st_ss[u][:, 0:1])
        for (xt, xtok, hT_ap, htok, u) in items:
            self.act(self.st_ln[u][:, 0:1], self.st_ss[u][:, 0:1], AF.Ln, ["ss%d" % u], ["rs%d_ln" % u],
                     scale=1.0 / D, bias=self.epsc[:, 0:1])
        for (xt, xtok, hT_ap, htok, u) in items:
            self.act(self.st_rs[u][:, 0:1], self.st_ln[u][:, 0:1], AF.Exp, ["rs%d_ln" % u], ["rs%d" % u], scale=-0.5)
        b0 = self.pb16[0]
        for (xt, xtok, hT_ap, htok, u) in items:
            xu = u % len(self.xs)
            xs = self.xs[xu]
            self.ts("dve", xs[:], xt, self.st_rs[u][:, 0:1], ALU.mult, [xtok, "rs%d" % u], ["xs%d" % xu])
            for k in range(KD):
                self.tr(b0[:, k * 128:(k + 1) * 128], xs[:, k * 128:(k + 1) * 128], ["xs%d" % xu], ["b0"])
            self.cp("act", hT_ap, b0[:, 0:1024].rearrange("p (k n) -> p k n", k=KD), ["b0"], [htok])

    def build(self):
        nc, S, NT, NOWN, NG = self.nc, self.S, self.NT, self.NOWN, self.NG
        self.in_names = []
        SO = S // 2
        xall = self.din("xall", [S, D])
        xown = self.din("xown", [SO, D])
        posall = self.din("posall", [1, S], I32)
        posown = self.din("posown", [1, SO], I32)
        invf = self.din("invf", [64, 1])
        ident = self.din("ident", [128, 128])
        umat = self.din("umat", [128, 128])
        cind = self.din("cind", [128, 2])
        qmask = self.din("qmask", [128, 2])
        amask = self.din("amask", [128, 256])
        w_sh = self.din("w_sh", [D, NSH])
        w_ow = self.din("w_ow", [D, NOW])
        g_pm = self.din("g_pm", [128, 8])
        w_a2 = self.din("w_a2", [16, 512])
        b_a2 = self.din("b_a2", [1, 512])
        g_gla = self.din("g_gla", [128, 8])
        w_og = self.din("w_og", [D, D])
        g_q = self.din("g_q", [128, 3])
        w_uq = self.din("w_uq", [384, 2048])
        g_kv = self.din("g_kv", [128, 2])
        w_ukT = self.din("w_ukT", [128, 2048])
        w_uv = self.din("w_uv", [256, 1024])
        w_om = self.din("w_om", [D, D])
        w_g = self.din("w_g", [D, 2048])
        b_g = self.din("b_g", [128, 16])
        w_out = self.din("w_out", [D, D])
        g_pmix = self.din("g_pmix", [1, D])
        g_pf = self.din("g_pf", [128, 8])
        w_fg = self.din("w_fg", [D, DFF])
        w_fu = self.din("w_fu", [D, DFF])
        w_fd = self.din("w_fd", [DFF, D])
        g_pffn = self.din("g_pffn", [1, D])
        out = nc.dram_tensor("out", [SO, D], F32, kind="ExternalOutput").ap()

        rope_all = self.dscr("rope_all", [2, 64, S], F32)
        rope_own = self.dscr("rope_own", [2, 64, SO], F32)
        sc_og = self.dscr("sc_og", [NG, 128, 8, 512], BF16)
        sc_cq = self.dscr("sc_cq", [NOWN, 128, 3, 128], BF16)
        sc_om = self.dscr("sc_om", [NG, 128, 8, 512], BF16)
        sc_x1 = self.dscr("sc_x1", [SO, D], F32)
        self.dbg_out = None
        if self.dbg:
            self.dbg_out = nc.dram_tensor("dbg", [128, 4096], F32, kind="ExternalOutput").ap()

        with ExitStack() as es:
            self.identf = self.sb(es, "identf", [128, 128], F32)
            self.identb = self.sb(es, "identb", [128, 128], BF16)
            self.epsc = self.sb(es, "epsc", [128, 1], F32)
            self.onec = self.sb(es, "onec", [128, 1], F32)
            self.hpic = self.sb(es, "hpic", [128, 1], F32)
            self.junk = self.sb(es, "junk", [128, 1024], BF16)
            self.st_ss = [self.sb(es, "st_ss%d" % u, [128, 8], F32) for u in range(4)]
            self.st_ln = [self.sb(es, "st_ln%d" % u, [128, 8], F32) for u in range(4)]
            self.st_rs = [self.sb(es, "st_rs%d" % u, [128, 8], F32) for u in range(4)]
            self.xs = [self.sb(es, "xs%d" % u, [128, 1024], BF16) for u in range(2)]
            pbanks = [es.enter_context(nc.psum_tensor("pb%d" % i, [128, 512], F32)) for i in range(8)]
            self.pb = pbanks
            self.pb16 = [b[:].bitcast(BF16) for b in pbanks]
            self.wslot = 0
            self.wcb = 2048
            self.wrot = 0
            pb, pb16 = self.pb, self.pb16

            self.dma("sp", self.identf[:], ident, "misc", [], ["identf"])
            self.cp("dve", self.identb[:], self.identf[:], ["identf"], ["ident"])
            self.memset("dve", self.epsc[:], EPS, ["epsc"])
            self.memset("dve", self.onec[:], 1.0, ["onec"])
            self.memset("dve", self.hpic[:], math.pi / 2, ["hpic"])

            with ExitStack() as er:
                CH = min(2048, SO)
                invf_t = self.sb(er, "invf_t", [64, 1], F32)
                posi = self.sb(er, "posi", [64, CH], I32)
                ang = self.sb(er, "ang", [64, CH], F32)
                tq = self.sb(er, "tq", [64, CH], F32)
                rr = self.sb(er, "rr", [64, CH], F32)
                g1 = self.sb(er, "g1", [64, CH], F32)
                co = self.sb(er, "co", [64, CH], F32)
                si = self.sb(er, "si", [64, CH], F32)
                self.dma("sp", invf_t[:], invf, "misc2", [], ["invf"])
                MAGIC = 12582912.0
                C1 = 6.28125
                C2 = 2.0 * math.pi - 6.28125
                PI = math.pi
                for (pos_d, rope_d, n) in ((posall, rope_all, S), (posown, rope_own, SO)):
                    for c0 in range(0, n, CH):
                        self.dma("sp", posi[:], pos_d[0:1, c0:c0 + CH].partition_broadcast(64), "posi",
                                 [], ["posi"])
                        self.cp("dve", ang[:], posi[:], ["posi"], ["ang"])
                        self.ts("dve", ang[:], ang[:], invf_t[:, 0:1], ALU.mult, ["ang", "invf"], ["ang"])
                        self.ts("dve", tq[:], ang[:], 1.0 / (2 * PI), ALU.mult, ["ang"], ["tq"], s2=MAGIC, op1=ALU.add)
                        self.ts("dve", tq[:], tq[:], -MAGIC, ALU.add, ["tq"], ["tq"])
                        self.stt(rr[:], tq[:], -C1, ang[:], ALU.mult, ALU.add, ["tq", "ang"], ["rr"])
                        self.stt(rr[:], tq[:], -C2, rr[:], ALU.mult, ALU.add, ["tq", "rr"], ["rr"])
                        self.ts("dve", g1[:], rr[:], PI, ALU.is_gt, ["rr"], ["g1"], s2=-2 * PI, op1=ALU.mult)
                        self.tt("dve", rr[:], rr[:], g1[:], ALU.add, ["rr", "g1"], ["rr"])
                        self.ts("dve", g1[:], rr[:], -PI, ALU.is_lt, ["rr"], ["g1"], s2=2 * PI, op1=ALU.mult)
                        self.tt("dve", rr[:], rr[:], g1[:], ALU.add, ["rr", "g1"], ["rr"])
                        self.ts("dve", rr[:], rr[:], -3.1415925, ALU.max, ["rr"], ["rr"], s2=3.1415925, op1=ALU.min)
                        self.act(si[0:32, :], rr[0:32, :], AF.Sin, ["rr"], ["si"], scale=-1.0)
                        self.act(si[32:64, :], rr[32:64, :], AF.Sin, ["rr"], ["si"])
                        self.act(g1[:], rr[:], AF.Abs, ["rr"], ["g1"])
                        self.act(co[:], g1[:], AF.Sin, ["g1"], ["co"], scale=-1.0, bias=self.hpic[0:64, 0:1])
                        self.dma("sp", rope_d[0, :, c0:c0 + CH], co[:], "ropest0", ["co"], ["rope_dram"])
                        self.dma("sp", rope_d[1, :, c0:c0 + CH], si[:], "ropest1", ["si"], ["rope_dram"])

            self.s.barrier()
            if self.upto >= 1:
                self.pass_A(es, locals())
            self.finish(es, out)
        return nc

    def finish(self, es, out):
        nc = self.nc
        self.s.add("sp", lambda e: e.nop(), ["out_dram", "dbg_dram"], [])
        self.s.finalize(nc, es)
        with nc.Block() as block:
            @block.sync
            def _(e):
                self.s.emit("sp", e)

            @block.tensor
            def _(e):
                self.s.emit("pe", e)

            @block.scalar
            def _(e):
                self.s.emit("act", e)

            @block.vector
            def _(e):
                self.s.emit("dve", e)

            @block.gpsimd
            def _(e):
                self.s.emit("pool", e)

    def pass_A(self, es_outer, L):
        nc, S, NT, NOWN, NG = self.nc, self.S, self.NT, self.NOWN, self.NG
        pb, pb16 = self.pb, self.pb16
        xall, xown = L["xall"], L["xown"]
        with ExitStack() as es:
            vaug = self.sb(es, "vaug", [128, NT, 258], BF16)
            kpeT = self.sb(es, "kpeT", [64, S], BF16)
            self.vaug, self.kpeT = vaug, kpeT
            self.memset("pool", vaug[:, :, 256:258], 1.0, ["vaug_ones"])
            with ExitStack() as ea:
                WS = self.sb(ea, "WS", [128, KD, NSH], BF16)
                WO = self.sb(ea, "WO", [128, KD, NOW], BF16)
                gpm = self.sb(ea, "gpm", [128, 8], F32)
                wa2a = self.sb(ea, "wa2a", [32, 512], BF16)
                umb = self.sb(ea, "umb", [128, 128], BF16)
                cif = self.sb(ea, "cif", [128, 2], F32)
                cib = self.sb(ea, "cib", [128, 2], BF16)
                qmf = self.sb(ea, "qmf", [128, 2], F32)
                ew = ExitStack()
                wa2f = self.sb(ew, "wa2f", [32, 512], F32)
                umf = self.sb(ew, "umf", [128, 128], F32)
                self.dma("sp", gpm[:], L["g_pm"], "misc", [], ["gains"])
                self.memset("dve", wa2f[:], 0.0, ["wa2f"])
                self.dma("sp", wa2f[0:16, :], L["w_a2"], "misc2", ["wa2f"], ["wa2f"])
                self.dma("sp", wa2f[16:17, :], L["b_a2"], "misc3", ["wa2f"], ["wa2f"])
                self.cp("dve", wa2a[:], wa2f[:], ["wa2f"], ["wa2a"])
                self.dma("sp", umf[:], L["umat"], "misc4", [], ["umf"])
                self.cp("dve", umb[:], umf[:], ["umf"], ["umb"])
                self.dma("sp", cif[:], L["cind"], "misc5", [], ["cif"])
                self.cp("dve", cib[:], cif[:], ["cif"], ["cib"])
                self.dma("sp", qmf[:], L["qmask"], "misc6", [], ["qmf"])
                self.ts("dve", qmf[:], qmf[:], 128.0 ** -0.5, ALU.mult, ["qmf"], ["qmf"])
                with ew:
                    self.wstage = [self.sb(ew, "wstg%d" % i, [128, 2048], F32) for i in range(2)]
                    self.load_w(WS, L["w_sh"], KD, NSH, gpm, "WS")
                    self.load_w(WO, L["w_ow"], KD, NOW, gpm, "WO")
                self.s.barrier()
                xin1 = [self.sb(ea, "xin_%d" % i, [128, 1024], F32) for i in range(3)]
                xin = [xin1, xin1]
                hTa = self.sb(ea, "hTa", [128, KD, 128], BF16)
                hT = [[hTa] + [self.sb(ea, "hT%d_%d" % (sl, i), [128, KD, 128], BF16) for i in (1, 2)] for sl in range(2)]
                hTn = [["hTa", "hT%d_1" % sl, "hT%d_2" % sl] for sl in range(2)]
                haT = [self.sb(ea, "haT%d" % i, [32, 128], BF16) for i in range(2)]
                ksb = [self.sb(ea, "ksb%d" % i, [128, 512], F32) for i in range(2)]
                e1s = self.sb(ea, "e1s", [128, 512], F32)
                e1 = [e1s, e1s]
                nl = [self.sb(ea, "nl%d" % i, [128, 512], BF16) for i in range(2)]
                dfac = [self.sb(ea, "dfac%d" % i, [128, 512], F32) for i in range(2)]
                dc = [self.sb(ea, "dc%d" % i, [128, 8], F32) for i in range(2)]
                kdec = [self.sb(ea, "kdec%d" % i, [128, 512], BF16) for i in range(2)]
                vbt = [self.sb(ea, "vb%d" % i, [128, 1024], BF16) for i in range(3)]

                def vbuf(j, ab):
                    i = 1 if ab == 1 else (0 if j % 2 == 0 else 2)
                    return vbt[i], "vb%d" % i
                stc = [[self.sb(ea, "stc%d_%d" % (i, q), [128, 4], F32) for q in range(3)] for i in range(3)]
                Sst = self.sb(ea, "Sst", [128, 4, 256], F32)
                Sbf = self.sb(ea, "Sbf", [128, 4, 4, 256], BF16)
                qpad = self.sb(ea, "qpad", [128, 4, 4, 128], BF16)
                eg = self.sb(ea, "eg", [128, 1024], F32)
                sg = self.sb(ea, "sg", [128, 1024], F32)
                osb = eg
                og = self.sb(ea, "og", [128, 1024], BF16)
                ogT = self.sb(ea, "ogT", [128, KD, 128], BF16)
                cqn = self.sb(ea, "cqn", [128, 384], BF16)
                cqnT = self.sb(ea, "cqnT", [128, 3, 128], BF16)
                rcs1 = self.sb(ea, "rcs", [64, 2, 128], F32)
                t11 = self.sb(ea, "t1", [64, 128], F32)
                t21 = self.sb(ea, "t2", [64, 128], F32)
                rcs, t1, t2 = [rcs1, rcs1], [t11, t11], [t21, t21]
                sso = self.sb(ea, "sso", [128, 4], F32)
                lno = self.sb(ea, "lno", [128, 4], F32)
                rso = self.sb(ea, "rso", [128, 4], F32)
                for i in range(2):
                    self.memset("dve", haT[i][:], 1.0, ["haT%d" % i])
                self.memset("dve", Sst[:], 0.0, ["Sst"])
                self.memset("pool", qpad[:], 0.0, ["qpad"])

                rope_all = L["rope_all"]
                NP = NT // 2

                def S1(j):
                    self.s.label = "S1" + (str(ab) if "ab" in "S1(j)" else "")
                    sl = j % 2
                    items = []
                    for i in range(3):
                        src = xall[(2 * j + i) * 128:(2 * j + i + 1) * 128, :] if i < 2 else xown[j * 128:(j + 1) * 128, :]
                        tk = "xin_%d" % i
                        self.dma("sp", xin[sl][i][:], src, "xin", [], [tk])
                        items.append((xin[sl][i][:], tk, hT[sl][i][:], hTn[sl][i], i))
                    self.norm_T_multi(items)

                def proj(j, ab):
                    self.s.label = "proj" + (str(ab) if "ab" in "proj(j, ab)" else "")
                    sl = j % 2
                    t = 2 * j + ab
                    h, hk = hT[sl][ab], hTn[sl][ab]
                    cb = 5 + ab
                    for k in range(KD):
                        self.mm(pb[cb][0:16, 0:128], WS[:, k, 1920:1936], h[:, k, :], k == 0, k == KD - 1,
                                [hk, "WS"], ["b%d" % cb])
                    self.cp("dve", haT[ab][0:16, :], pb[cb][0:16, 0:128], ["b%d" % cb], ["haT%d" % ab])
                    for (bank, c0, c1) in ((1, 0, 512), (2, 512, 1024), (3, 1024, 1536)):
                        for k in range(KD):
                            self.mm(pb[bank][:, 0:512], h[:, k, :], WS[:, k, c0:c1], k == 0, k == KD - 1,
                                    [hk, "WS"], ["b%d" % bank])
                        if bank == 1:
                            self.cp("act", ksb[ab][:], pb[1][:, 0:512], ["b1"], ["ksb%d" % ab])
                            self.mm(pb[cb][:, 0:512], haT[ab][:, :], wa2a[:, :], True, True, ["haT%d" % ab, "wa2a"],
                                    ["b%d" % cb])
                            self.act(e1[ab][:], pb[cb][:, 0:512], AF.Exp, ["b%d" % cb], ["e1s"], scale=-1.0)
                            self.act(nl[ab][:], e1[ab][:], AF.Ln, ["e1s"], ["nl%d" % ab], bias=self.onec[:, 0:1])
                        elif bank == 2:
                            self.cp("dve", vbuf(j, ab)[0][:, 0:512], pb[2][:, 0:512], ["b2"], [vbuf(j, ab)[1]])
                        else:
                            self.cp("act", vbuf(j, ab)[0][:, 512:1024], pb[3][:, 0:512], ["b3"], [vbuf(j, ab)[1]])
                    for k in range(KD):
                        self.mm(pb[4][:, 0:256], h[:, k, :], WS[:, k, 1536:1792], k == 0, k == KD - 1,
                                [hk, "WS"], ["b4"])
                    for (o0, c0) in ((256, 1792), (384, 1856)):
                        for k in range(KD):
                            self.mm(pb[4][0:64, o0:o0 + 128], WS[:, k, c0:c0 + 64], h[:, k, :], k == 0, k == KD - 1,
                                    [hk, "WS"], ["b4"])
                    ssc, lnc, rsc = stc[ab]
                    self.act(self.junk[:, 0:256], pb[4][:, 0:256], AF.Square, ["b4"], ["junk", "ssc%d" % ab],
                             accum=ssc[:, 0:1])
                    self.rstd_from_ss(ssc[:, 0:1], lnc[:, 0:1], rsc[:, 0:1], 256, ["ssc%d" % ab], "rsc%d" % ab)
                    self.ts("dve", vaug[:, t, 0:256], pb[4][:, 0:256], rsc[:, 0:1], ALU.mult, ["b4", "rsc%d" % ab],
                            ["vaug%d" % t])
                    self.dma("sp", rcs[ab][:, :, :], rope_all[:, :, t * 128:(t + 1) * 128].rearrange("a p n -> p a n"),
                             "rcs", ["rope_dram"], ["rcs_"])
                    self.tt("dve", t1[ab][:], pb[4][0:64, 256:384], rcs[ab][:, 0, :], ALU.mult, ["b4", "rcs_"],
                            ["t1_"])
                    self.tt("dve", t2[ab][:], pb[4][0:64, 384:512], rcs[ab][:, 1, :], ALU.mult, ["b4", "rcs_"],
                            ["t2_"])
                    self.tt("dve", kpeT[:, t * 128:(t + 1) * 128], t1[ab][:], t2[ab][:], ALU.add,
                            ["t1_", "t2_"], ["kpeT%d" % t])

                def chain(ab):
                    self.s.label = "chain" + (str(ab) if "ab" in "chain(ab)" else "")
                    cb = 5 + ab
                    self.mm(pb[cb][:, 0:512], umb[:, :], nl[ab][:, :], True, True, ["umb", "nl%d" % ab], ["b%d" % cb])
                    self.act(dfac[ab][:], pb[cb][:, 0:512], AF.Exp, ["b%d" % cb], ["dfac%d" % ab], scale=-1.0 / 16.0)
                    for hh in range(4):
                        self.mm(pb[cb][:, 2 * hh:2 * hh + 2], nl[ab][:, hh * 128:(hh + 1) * 128], cib[:, :], True, True,
                                ["nl%d" % ab, "cib"], ["b%d" % cb])
                    self.act(dc[ab][:], pb[cb][:, 0:8], AF.Exp, ["b%d" % cb], ["dc%d" % ab], scale=-1.0 / 16.0)
                    self.tt("dve", kdec[ab][:], ksb[ab][:], dfac[ab][:], ALU.mult, ["ksb%d" % ab, "dfac%d" % ab],
                            ["kdec%d" % ab])

                def state(j, ab):
                    self.s.label = "state%d" % ab
                    vb_, vbk = vbuf(j, ab)
                    for c in range(2):
                        pc = 2 * ab + c
                        for half in range(2):
                            bank = 1 + 2 * c + half
                            for q in range(2):
                                hh = 2 * half + q
                                self.mm(pb[bank][:, q * 256:q * 256 + 256],
                                        kdec[ab][c * 64:(c + 1) * 64, hh * 128:(hh + 1) * 128],
                                        vb_[c * 64:(c + 1) * 64, hh * 256:(hh + 1) * 256], True, True,
                                        ["kdec%d" % ab, vbk], ["b%d" % bank])
                        for half in range(2):
                            bank = 1 + 2 * c + half
                            for q in range(2):
                                hh = 2 * half + q
                                self.stt(Sst[:, hh, :], Sst[:, hh, :], dc[ab][:, 2 * hh + c:2 * hh + c + 1],
                                         pb[bank][:, q * 256:q * 256 + 256], ALU.mult, ALU.add,
                                         ["Sst", "dc%d" % ab, "b%d" % bank], ["Sst"])
                        self.cp("act", Sbf[:, pc, :, :], Sst[:, :, :], ["Sst"], ["Sbf%d" % pc])

                def own_proj(j):
                    self.s.label = "own_proj" + (str(ab) if "ab" in "own_proj(j)" else "")
                    sl = j % 2
                    h, hk = hT[sl][2], hTn[sl][2]
                    for hh in range(4):
                        for k in range(KD):
                            self.mm(pb[1][:, hh * 128:(hh + 1) * 128], WO[:, k, hh * 128:(hh + 1) * 128], h[:, k, :],
                                    k == 0, k == KD - 1, [hk, "WO"], ["b1"])
                    qv = pb[1][:, 0:512].rearrange("p (h n) -> p h n", h=4)
                    for m in range(2):
                        for cc in range(2):
                            c = 2 * m + cc
                            self.ts("dve", qpad[:, c, :, cc * 64:(cc + 1) * 64], qv[:, :, cc * 64:(cc + 1) * 64],
                                    qmf[:, m:m + 1], ALU.mult, ["b1", "qmf"], ["qpad"])
                    for (bank, c0) in ((2, 512), (3, 1024)):
                        for k in range(KD):
                            self.mm(pb[bank][:, 0:512], h[:, k, :], WO[:, k, c0:c0 + 512], k == 0, k == KD - 1,
                                    [hk, "WO"], ["b%d" % bank])
                        o0 = c0 - 512
                        self.act(eg[:, o0:o0 + 512], pb[bank][:, 0:512], AF.Exp, ["b%d" % bank], ["eg%d" % bank], scale=-1.0)
                        self.ts("dve", eg[:, o0:o0 + 512], eg[:, o0:o0 + 512], 1.0, ALU.add, ["eg%d" % bank], ["eg%d" % bank])
                        self.recip(eg[:, o0:o0 + 512], eg[:, o0:o0 + 512], ["eg%d" % bank], ["eg%d" % bank])
                        self.tt("dve", sg[:, o0:o0 + 512], eg[:, o0:o0 + 512], pb[bank][:, 0:512], ALU.mult,
                                ["eg%d" % bank, "b%d" % bank], ["sg%d" % bank])
                    for k in range(KD):
                        self.mm(pb[4][:, 0:384], h[:, k, :], WO[:, k, 1536:1920], k == 0, k == KD - 1,
                                [hk, "WO"], ["b4"])
                    ssq, lnq, rsq = stc[2]
                    self.act(self.junk[:, 0:384], pb[4][:, 0:384], AF.Square, ["b4"], ["junk", "ssq"], accum=ssq[:, 0:1])
                    self.rstd_from_ss(ssq[:, 0:1], lnq[:, 0:1], rsq[:, 0:1], 384, ["ssq"], "rsq")
                    self.ts("dve", cqn[:], pb[4][:, 0:384], rsq[:, 0:1], ALU.mult, ["b4", "rsq"], ["cqn"])

                def own_out(j):
                    self.s.label = "own_out" + (str(ab) if "ab" in "own_out(j)" else "")
                    for hh in range(4):
                        bank = 5 + hh // 2
                        for c in range(4):
                            self.mm(pb[bank][:, (hh % 2) * 256:(hh % 2) * 256 + 256], qpad[:, c, hh, :],
                                    Sbf[:, c, hh, :], c == 0, c == 3, ["qpad", "Sbf%d" % c], ["b%d" % bank])
                    for hf in range(2):
                        self.cp("act", osb[:, hf * 512:(hf + 1) * 512], pb[5 + hf][:, 0:512], ["b%d" % (5 + hf)],
                                ["eg%d" % (2 + hf)])
                    for hh in range(4):
                        self.act(self.junk[:, 0:256], osb[:, hh * 256:(hh + 1) * 256], AF.Square,
                                 ["eg%d" % (2 + hh // 2)], ["junk", "sso"], accum=sso[:, hh:hh + 1])
                    self.act(lno[:], sso[:], AF.Ln, ["sso"], ["lno"], scale=1.0 / 256, bias=self.epsc[:, 0:1])
                    self.act(rso[:], lno[:], AF.Exp, ["lno"], ["rso"], scale=-0.5)
                    for hh in range(4):
                        self.stt(og[:, hh * 256:(hh + 1) * 256], osb[:, hh * 256:(hh + 1) * 256],
                                 rso[:, hh:hh + 1], sg[:, hh * 256:(hh + 1) * 256], ALU.mult, ALU.mult,
                                 ["eg%d" % (2 + hh // 2), "rso", "sg%d" % (2 + hh // 2)], ["og"])
                    for k in range(KD):
                        self.tr(pb16[0][:, k * 128:(k + 1) * 128], og[:, k * 128:(k + 1) * 128], ["og"], ["b0"])
                    self.cp("act", ogT[:], pb16[0][:, 0:1024].rearrange("p (k n) -> p k n", k=KD), ["b0"], ["ogT"])
                    gi, ii = j // GT, j % GT
                    self.dma("sp", L["sc_og"][gi, :, :, ii * 128:(ii + 1) * 128], ogT[:], "ogst", ["ogT"], ["sc_og"])
                    for k in range(3):
                        self.tr(pb16[0][:, k * 128:(k + 1) * 128], cqn[:, k * 128:(k + 1) * 128], ["cqn"], ["b0"])
                    self.cp("dve", cqnT[:], pb16[0][:, 0:384].rearrange("p (k n) -> p k n", k=3), ["b0"], ["cqnT"])
                    self.dma("sp", L["sc_cq"][j], cqnT[:], "cqst", ["cqnT"], ["sc_cq"])

                self.probe('A')
                S1(0)
                for j in range(NP):
                    proj(j, 0)
                    if j > 0:
                        state(j - 1, 0)
                        state(j - 1, 1)
                    if j + 1 < NP:
                        S1(j + 1)
                    proj(j, 1)
                    if j > 0:
                        own_out(j - 1)
                    chain(0)
                    own_proj(j)
                    chain(1)
                state(NP - 1, 0)
                state(NP - 1, 1)
                own_out(NP - 1)
            self.s.barrier()
            if self.upto >= 2:
                self.pass_B1(es, L)
                self.s.barrier()
        if self.upto >= 3:
            self.pass_B2(es_outer, L)
            self.s.barrier()
        if self.upto >= 4:
            self.pass_C(es_outer, L)

    def dbg_dump_A(self, L):
        pass

    def pass_B1(self, es_outer, L):
        nc, S, NT, NOWN, NG = self.nc, self.S, self.NT, self.NOWN, self.NG
        pb, pb16 = self.pb, self.pb16
        vaug, kpeT = self.vaug, self.kpeT
        SCALE = 192.0 ** -0.5
        with ExitStack() as es:
            ckvT = self.sb(es, "ckvT", [128, 2, S], BF16)
            WUQ = self.sb(es, "WUQ", [128, 3, 2048], BF16)
            WUKT = self.sb(es, "WUKT", [128, 1, 2048], BF16)
            WUV = self.sb(es, "WUV", [128, 2, 1024], BF16)
            gq = self.sb(es, "gq", [128, 3], F32)
            gkv = self.sb(es, "gkv", [128, 2], F32)
            amf = self.sb(es, "amf", [128, 256], F32)
            amb = self.sb(es, "amb", [128, 2, 128], BF16)
            self.dma("sp", gq[:], L["g_q"], "misc", [], ["gains"])
            self.dma("sp", gkv[:], L["g_kv"], "misc2", [], ["gains"])
            self.dma("sp", amf[:], L["amask"], "misc3", [], ["amf"])
            self.cp("dve", amb[:].rearrange("p a n -> p (a n)"), amf[:], ["amf"], ["amb"])
            with ExitStack() as ew:
                self.wstage = [self.sb(ew, "wstgb%d" % i, [128, 2048], F32) for i in range(2)]
                self.load_w(WUQ, L["w_uq"], 3, 2048, gq, "WUQ")
                self.load_w(WUKT, L["w_ukT"], 1, 2048, None, "WUKT")
                self.load_w(WUV, L["w_uv"], 2, 1024, gkv, "WUV")
            self.s.barrier()
            for t in range(NT):
                for lc in range(2):
                    self.tr(pb16[0][:, lc * 128:(lc + 1) * 128], vaug[:, t, lc * 128:(lc + 1) * 128],
                            ["vaug%d" % t], ["b0"])
                self.cp("act" if t % 2 else "dve", ckvT[:, :, t * 128:(t + 1) * 128],
                        pb16[0][:, 0:256].rearrange("p (a n) -> p a n", a=2), ["b0"], ["ckvT%d" % t])
            import os
            B1STOP = int(os.environ.get("B1STOP", "9"))
            if B1STOP < 1:
                return
            cqT = [self.sb(es, "cqT%d" % i, [128, 3, 128], BF16) for i in range(2)]
            rco = [self.sb(es, "rco%d" % i, [64, 2, 128], F32) for i in range(2)]
            qn = [self.sb(es, "qn%d" % i, [128, 128], BF16) for i in range(2)]
            qpe = self.sb(es, "qpe", [64, 8, 128], BF16)
            qabs = self.sb(es, "qabs", [128, 2, 8, 128], BF16)
            t1 = self.sb(es, "bt1", [64, 128], F32)
            t2 = self.sb(es, "bt2", [64, 128], F32)
            PT = [self.sb(es, "PT%d" % i, [128, 4, 128], BF16) for i in range(3)]
            rsum = self.sb(es, "rsum", [128, 8], F32)
            olat = self.sb(es, "olat", [128, 8, 256], BF16)
            olatT = self.sb(es, "olatT", [128, 8, 2, 128], BF16)
            omT = self.sb(es, "omT", [128, 8, 128], BF16)
            self.probe('B1')
            for j in range(NOWN):
                sl = j % 2
                XV = int(os.environ.get("XV", "3"))
                if XV & 1:
                    self.dma("sp", cqT[sl][:], L["sc_cq"][j], "cqT%d" % sl, ["sc_cq"], ["cqT%d" % sl])
                if XV & 2:
                    self.dma("sp", rco[sl][:], L["rope_own"][:, :, j * 128:(j + 1) * 128].rearrange("a p n -> p a n"),
                         "rco%d" % sl, ["rope_dram"], ["rco%d" % sl])
                cq = cqT[sl]
                QV = int(os.environ.get("QV", "9"))
                for h in range(8):
                    if QV < 2:
                        continue
                    qs = h % 2
                    qb = 1 + qs
                    qbt = "b%d" % qb
                    for k in range(3):
                        self.mm(pb[qb][:, 0:128], WUQ[:, k, h * 256:h * 256 + 128], cq[:, k, :], k == 0, k == 2,
                                ["cqT%d" % sl, "WUQ"], [qbt])
                    for (o0, c0) in ((128, 128), (256, 192)):
                        for k in range(3):
                            self.mm(pb[qb][0:64, o0:o0 + 128], WUQ[:, k, h * 256 + c0:h * 256 + c0 + 64], cq[:, k, :],
                                    k == 0, k == 2, ["cqT%d" % sl, "WUQ"], [qbt])
                    YV = int(os.environ.get("YV", "9"))
                    if YV < 1:
                        continue
                    self.cp("act", qn[qs][:], pb[qb][:, 0:128], [qbt], ["qn%d" % qs])
                    if YV < 2:
                        continue
                    self.tt("dve", t1[:], pb[qb][0:64, 128:256], rco[sl][:, 0, :], ALU.mult, [qbt, "rco%d" % sl], ["bt1"])
                    self.tt("dve", t2[:], pb[qb][0:64, 256:384], rco[sl][:, 1, :], ALU.mult, [qbt, "rco%d" % sl], ["bt2"])
                    self.tt("dve", qpe[:, h, :], t1[:], t2[:], ALU.add, ["bt1", "bt2"], ["qpe"])
                    bank = 3 + qs
                    if QV < 3:
                        continue
                    for lc in range(2):
                        self.mm(pb[bank][:, lc * 128:(lc + 1) * 128], WUKT[:, 0, h * 256 + lc * 128:h * 256 + (lc + 1) * 128],
                                qn[qs][:], True, True, ["qn%d" % qs, "WUKT"], ["b%d" % bank])
                    for lc in range(2):
                        self.ts("dve", qabs[:, lc, h, :], pb[bank][:, lc * 128:(lc + 1) * 128], gkv[:, lc:lc + 1],
                                ALU.mult, ["b%d" % bank, "gains"], ["qabs"])
                nkt = 2 * j + 2
                if B1STOP < 2:
                    continue
                for gi in range(2):
                    def scores(kt):
                        sbk = 5 + (kt % 3)
                        for lc in range(2):
                            self.mm(pb[sbk][:, 0:512], ckvT[:, lc, kt * 128:(kt + 1) * 128],
                                    qabs[:, lc, 4 * gi:4 * gi + 4, :], lc == 0, False,
                                    ["ckvT%d" % kt, "qabs"], ["b%d" % sbk])
                        self.mm(pb[sbk][:, 0:512], kpeT[:, kt * 128:(kt + 1) * 128], qpe[:, 4 * gi:4 * gi + 4, :],
                                False, True, ["kpeT%d" % kt, "qpe"], ["b%d" % sbk])

                    scores(0)
                    scores(1)
                    for kt in range(nkt):
                        sbk = 5 + (kt % 3)
                        ps = kt % 3
                        self.act(PT[ps][:], pb[sbk][:, 0:512].rearrange("p (h n) -> p h n", h=4), AF.Exp,
                                 ["b%d" % sbk], ["PT%d" % ps], scale=SCALE)
                        if kt >= nkt - 2:
                            r = kt - (nkt - 2)
                            for hh in range(4):
                                self.tt("dve", PT[ps][:, hh, :], PT[ps][:, hh, :], amb[:, r, :], ALU.mult,
                                        ["PT%d" % ps, "amb"], ["PT%d" % ps])
                        if kt + 2 < nkt:
                            scores(kt + 2)
                        for hh in range(4):
                            self.mm(pb[1 + hh][:, 0:258], PT[ps][:, hh, :], vaug[:, kt, 0:258], kt == 0, kt == nkt - 1,
                                    ["PT%d" % ps, "vaug%d" % kt, "vaug_ones"], ["b%d" % (1 + hh)])
                    for hh in range(4):
                        h = 4 * gi + hh
                        self.recip(rsum[:, h:h + 1], pb[1 + hh][:, 256:257], ["b%d" % (1 + hh)], ["rsum"])
                        self.ts("dve", olat[:, h, :], pb[1 + hh][:, 0:256], rsum[:, h:h + 1], ALU.mult,
                                ["b%d" % (1 + hh), "rsum"], ["olat"])
                if B1STOP < 3:
                    continue
                for half in range(2):
                    for hh in range(4):
                        h = 4 * half + hh
                        for lc in range(2):
                            self.tr(pb16[0][:, (hh * 2 + lc) * 128:(hh * 2 + lc + 1) * 128],
                                    olat[:, h, lc * 128:(lc + 1) * 128], ["olat"], ["b0"])
                    self.cp("act", olatT[:, 4 * half:4 * half + 4, :, :].rearrange("p h a n -> p (h a n)"),
                            pb16[0][:, 0:1024], ["b0"], ["olatT"])
                for half in range(2):
                    bank = 1 + half
                    for hh in range(4):
                        h = 4 * half + hh
                        for lc in range(2):
                            self.mm(pb[bank][:, hh * 128:(hh + 1) * 128], WUV[:, lc, h * 128:(h + 1) * 128],
                                    olatT[:, h, lc, :], lc == 0, lc == 1, ["olatT", "WUV"], ["b%d" % bank])
                    self.cp("act", omT[:, 4 * half:4 * half + 4, :].rearrange("p h n -> p (h n)"), pb[bank][:, 0:512],
                            ["b%d" % bank], ["omT"])
                gi_, ii = j // GT, j % GT
                self.dma("sp", L["sc_om"][gi_, :, :, ii * 128:(ii + 1) * 128], omT[:], "omst", ["omT"], ["sc_om"])

    def post_norm_res(self, banks, btoks, gbc, xres, xtok, outt, outtok, u):
        ss, lnv, rs = self.st_ss[u], self.st_ln[u], self.st_rs[u]
        for hf in range(2):
            self.act(self.junk[:, 0:512], banks[hf][:, 0:512], AF.Square, [btoks[hf]], ["junk", "pss%d" % u],
                     accum=ss[:, 2 + hf:3 + hf])
        self.tt("dve", ss[:, 4:5], ss[:, 2:3], ss[:, 3:4], ALU.add, ["pss%d" % u], ["pss2%d" % u])
        self.rstd_from_ss(ss[:, 4:5], lnv[:, 4:5], rs[:, 4:5], D, ["pss2%d" % u], "prs%d" % u)
        for hf in range(2):
            self.stt(self.ptmp[:, hf * 512:(hf + 1) * 512], banks[hf][:, 0:512], rs[:, 4:5],
                     gbc[:, hf * 512:(hf + 1) * 512], ALU.mult, ALU.mult, [btoks[hf], "prs%d" % u, "gbc"], ["ptmp"])
        self.tt("dve", outt, self.ptmp[:], xres, ALU.add, ["ptmp", xtok], [outtok])

    def pass_B2(self, es_outer, L):
        nc, S, NT, NOWN, NG = self.nc, self.S, self.NT, self.NOWN, self.NG
        pb, pb16 = self.pb, self.pb16
        with ExitStack() as es:
            WG = self.sb(es, "WG", [128, KD, 2048], BF16)
            WOG = self.sb(es, "WOG", [128, KD, 1024], BF16)
            WOM = self.sb(es, "WOM", [128, KD, 1024], BF16)
            WOUT = self.sb(es, "WOUT", [128, KD, 1024], BF16)
            gpm = self.sb(es, "gpm2", [128, 8], F32)
            ggl = self.sb(es, "ggl", [128, 8], F32)
            bg = self.sb(es, "bg", [128, 16], F32)
            gbc = self.sb(es, "gbc", [128, 1024], F32)
            self.dma("sp", gpm[:], L["g_pm"], "misc", [], ["gains"])
            self.dma("sp", ggl[:], L["g_gla"], "misc2", [], ["gains"])
            self.dma("sp", bg[:], L["b_g"], "misc3", [], ["bg"])
            self.dma("sp", gbc[:], L["g_pmix"][0:1, :].partition_broadcast(128), "misc4", [], ["gbc"])
            self.wstage = [self.sb(es, "wstgc%d" % i, [128, 2048], F32) for i in range(2)]
            self.load_w(WG, L["w_g"], KD, 2048, gpm, "WG")
            self.load_w(WOG, L["w_og"], KD, 1024, ggl, "WOG")
            self.load_w(WOM, L["w_om"], KD, 1024, None, "WOM")
            join_wout = self.load_w(WOUT, L["w_out"], KD, 1024, None, "WOUT", defer=True)
            xg = [self.sb(es, "xg%d" % i, [128, 1024], F32) for i in range(GT)]
            hTg = self.sb(es, "hTg", [128, KD, 512], BF16)
            ogg = self.sb(es, "ogg", [128, KD, 512], BF16)
            omg = self.sb(es, "omg", [128, KD, 512], BF16)
            ga = [self.sb(es, "ga%d" % i, [128, 512], F32) for i in range(2)]
            gb = [self.sb(es, "gb%d" % i, [128, 512], F32) for i in range(2)]
            m1 = [self.sb(es, "m1%d" % i, [128, 512], F32) for i in range(2)]
            m2 = [self.sb(es, "m2%d" % i, [128, 512], F32) for i in range(2)]
            mixT = self.sb(es, "mixT", [128, KD, 512], BF16)
            self.ptmp = self.sb(es, "ptmp", [128, 1024], F32)
            self.probe('B2')
            for g in range(NG):
                self.dma("sp", ogg[:], L["sc_og"][g], "ogg", ["sc_og"], ["ogg"])
                self.dma("sp", omg[:], L["sc_om"][g], "omg", ["sc_om"], ["omg"])
                items = []
                for i in range(GT):
                    j = g * GT + i
                    self.dma("sp", xg[i][:], L["xown"][j * 128:(j + 1) * 128, :], "xg%d" % i, [], ["xg%d" % i])
                    items.append((xg[i][:], "xg%d" % i, hTg[:, :, i * 128:(i + 1) * 128], "hTg", i))
                self.norm_T_multi(items)
                for fc in range(KD):
                    st = fc % 2
                    bs = (1, 2, 3, 4) if st == 0 else (5, 6, 7, 0)
                    srcs = ((WG, 0, hTg, "hTg", "WG"), (WG, 1024, hTg, "hTg", "WG"),
                            (WOG, 0, ogg, "ogg", "WOG"), (WOM, 0, omg, "omg", "WOM"))
                    for bi, (Wt, off, rhs, rtok, wtok) in enumerate(srcs):
                        for k in range(KD):
                            self.mm(pb[bs[bi]][:, 0:512], Wt[:, k, off + fc * 128:off + (fc + 1) * 128], rhs[:, k, :],
                                    k == 0, k == KD - 1, [rtok, wtok], ["b%d" % bs[bi]])
                    self.act(ga[st][:], pb[bs[0]][:, 0:512], AF.Sigmoid, ["b%d" % bs[0], "bg"], ["ga%d" % st],
                             bias=bg[:, fc:fc + 1])
                    self.act(gb[st][:], pb[bs[1]][:, 0:512], AF.Sigmoid, ["b%d" % bs[1], "bg"], ["gb%d" % st],
                             bias=bg[:, 8 + fc:9 + fc])
                    self.tt("dve", m1[st][:], ga[st][:], pb[bs[2]][:, 0:512], ALU.mult, ["ga%d" % st, "b%d" % bs[2]],
                            ["m1%d" % st])
                    self.tt("dve", m2[st][:], gb[st][:], pb[bs[3]][:, 0:512], ALU.mult, ["gb%d" % st, "b%d" % bs[3]],
                            ["m2%d" % st])
                    self.tt("dve", mixT[:, fc, :], m1[st][:], m2[st][:], ALU.add, ["m1%d" % st, "m2%d" % st], ["mixT"])
                if g == 0:
                    join_wout()
                for i in range(GT):
                    j = g * GT + i
                    bs = (1, 2) if i % 2 == 0 else (3, 4)
                    for hf in range(2):
                        for k in range(KD):
                            self.mm(pb[bs[hf]][:, 0:512], mixT[:, k, i * 128:(i + 1) * 128], WOUT[:, k, hf * 512:(hf + 1) * 512],
                                    k == 0, k == KD - 1, ["mixT", "WOUT"], ["b%d" % bs[hf]])
                    self.post_norm_res([pb[bs[0]], pb[bs[1]]], ["b%d" % bs[0], "b%d" % bs[1]], gbc, xg[i][:], "xg%d" % i,
                                       xg[i][:], "xg%d" % i, 2 + i % 2)
                    self.dma("sp", L["sc_x1"][j * 128:(j + 1) * 128, :], xg[i][:], "x1st%d" % i, ["xg%d" % i], ["sc_x1"])

    def pass_C(self, es_outer, L):
        nc, S, NT, NOWN, NG = self.nc, self.S, self.NT, self.NOWN, self.NG
        pb, pb16 = self.pb, self.pb16
        with ExitStack() as es:
            self.probe('Cstart')
            WFG = self.sb(es, "WFG", [128, KD, DFF], BF16)
            WFU = self.sb(es, "WFU", [128, KD, DFF], BF16)
            WFD = self.sb(es, "WFD", [128, KF, 1024], BF16)
            gpf = self.sb(es, "gpf", [128, 8], F32)
            gbc = self.sb(es, "gbc2", [128, 1024], F32)
            self.dma("sp", gpf[:], L["g_pf"], "misc", [], ["gains"])
            self.dma("sp", gbc[:], L["g_pffn"][0:1, :].partition_broadcast(128), "misc4", [], ["gbc"])
            with ExitStack() as ew:
                self.wstage = [self.sb(ew, "wstgd%d" % i, [128, 2048], F32) for i in range(2)]
                self.load_w(WFG, L["w_fg"], KD, DFF, gpf, "WFG")
                self.load_w(WFU, L["w_fu"], KD, DFF, gpf, "WFU")
                self.load_w(WFD, L["w_fd"], KF, 1024, None, "WFD")
            self.s.barrier()
            self.probe('C0')
            xg = [self.sb(es, "xc%d" % i, [128, 1024], F32) for i in range(GT)]
            hTg = self.sb(es, "hTc", [128, KD, 512], BF16)
            actT = self.sb(es, "actT", [128, KF, 512], BF16)
            sl = [self.sb(es, "sl%d" % i, [128, 512], F32) for i in range(2)]
            self.ptmp = self.sb(es, "ptmp2", [128, 1024], F32)
            self.probe('C')
            for g in range(NG):
                items = []
                for i in range(GT):
                    j = g * GT + i
                    self.dma("sp", xg[i][:], L["sc_x1"][j * 128:(j + 1) * 128, :], "xg%d" % i, ["sc_x1"], ["xg%d" % i])
                    items.append((xg[i][:], "xg%d" % i, hTg[:, :, i * 128:(i + 1) * 128], "hTg", i))
                self.norm_T_multi(items)
                for fc in range(KF):
                    st = fc % 2
                    bg_, bu_ = (1 + 2 * (fc % 3), 2 + 2 * (fc % 3))
                    for (Wt, bank, wtok) in ((WFG, bg_, "WFG"), (WFU, bu_, "WFU")):
                        for k in range(KD):
                            self.mm(pb[bank][:, 0:512], Wt[:, k, fc * 128:(fc + 1) * 128], hTg[:, k, :], k == 0, k == KD - 1,
                                    ["hTg", wtok], ["b%d" % bank])
                    self.act(sl[st][:], pb[bg_][:, 0:512], AF.Silu, ["b%d" % bg_], ["sl%d" % st])
                    self.tt("dve", actT[:, fc, :], sl[st][:], pb[bu_][:, 0:512], ALU.mult, ["sl%d" % st, "b%d" % bu_],
                            ["actT"])
                for i in range(GT):
                    j = g * GT + i
                    bs = (1, 2) if i % 2 == 0 else (3, 4)
                    for hf in range(2):
                        for k in range(KF):
                            self.mm(pb[bs[hf]][:, 0:512], actT[:, k, i * 128:(i + 1) * 128], WFD[:, k, hf * 512:(hf + 1) * 512],
                                    k == 0, k == KF - 1, ["actT", "WFD"], ["b%d" % bs[hf]])
                    self.post_norm_res([pb[bs[0]], pb[bs[1]]], ["b%d" % bs[0], "b%d" % bs[1]], gbc, xg[i][:], "xg%d" % i,
                                       xg[i][:], "xg%d" % i, 2 + i % 2)
                    self.dma("sp", L["out"][j * 128:(j + 1) * 128, :], xg[i][:], "outst%d" % i, ["xg%d" % i], ["out_dram"])


def _consts(parity):
    inv = (1.0 / (10000.0 ** (np.arange(0, 64, 2, dtype=np.float32) / np.float32(64)))).astype(np.float32)
    invf = np.concatenate([inv, inv]).reshape(64, 1).astype(np.float32)
    ident = np.eye(128, dtype=np.float32)
    s = np.arange(128)[:, None]
    t = np.arange(128)[None, :]
    umat = ((s // 64 == t // 64) & (s > t)).astype(np.float32)
    cind = (s // 64 == np.arange(2)[None, :]).astype(np.float32)
    qmask = np.zeros((128, 2), np.float32)
    qmask[:, parity] = 1.0
    diag = ((s // 64) <= (t // 64)).astype(np.float32)
    amask = np.zeros((128, 2, 128), np.float32)
    if parity == 0:
        amask[:, 0, :] = diag
    else:
        amask[:, 0, :] = 1.0
        amask[:, 1, :] = diag
    return dict(invf=invf, ident=ident, umat=umat, cind=cind, qmask=qmask,
                amask=np.ascontiguousarray(amask.reshape(128, 256)))


def _pk(v, n):
    return np.ascontiguousarray(v.reshape(n, 128).T).astype(np.float32)


def _weights(inp):
    w_in = inp["w_in"][0]
    sp = np.cumsum([0, 512, 512, 1024, 1024, 16, 384, 256, 64])
    q, k, v, g, ha, cq, ckv, kpe = [w_in[:, sp[i]:sp[i + 1]] for i in range(8)]
    kpesw = np.concatenate([kpe[:, 32:], kpe[:, :32]], axis=1)
    w_sh = np.ascontiguousarray(np.concatenate([k, v, ckv, kpe, kpesw, ha], axis=1))
    w_ow = np.ascontiguousarray(np.concatenate([q, g, cq], axis=1))
    w_uq = inp["w_uq"][0].reshape(384, 8, 192)
    nope, rope = w_uq[:, :, :128], w_uq[:, :, 128:]
    ropesw = np.concatenate([rope[:, :, 32:], rope[:, :, :32]], axis=2)
    w_uq2 = np.ascontiguousarray(np.concatenate([nope, rope, ropesw], axis=2).reshape(384, 2048))
    w_ukv = inp["w_ukv"][0].reshape(256, 8, 256)
    w_ukT = np.ascontiguousarray(w_ukv[:, :, :128].transpose(2, 1, 0).reshape(128, 2048))
    w_uv = np.ascontiguousarray(w_ukv[:, :, 128:].reshape(256, 1024))
    gla = inp["gla_norm"][0]
    return dict(
        w_sh=w_sh, w_ow=w_ow, g_pm=_pk(inp["pre_mix_norm"][0], 8),
        w_a2=np.ascontiguousarray(inp["w_a2"][0]), b_a2=np.ascontiguousarray(inp["b_a2"][0].reshape(1, 512)),
        g_gla=_pk(np.tile(gla, 4), 8), w_og=np.ascontiguousarray(inp["w_o_gla"][0]),
        g_q=_pk(inp["q_norm"][0], 3), w_uq=w_uq2, g_kv=_pk(inp["kv_norm"][0], 2),
        w_ukT=w_ukT, w_uv=w_uv, w_om=np.ascontiguousarray(inp["w_o_mla"][0]),
        w_g=np.ascontiguousarray(inp["w_gate"][0]), b_g=_pk(inp["b_gate"][0], 16),
        w_out=np.ascontiguousarray(inp["w_out"][0]),
        g_pmix=np.ascontiguousarray(inp["post_mix_norm"][0].reshape(1, D)),
        g_pf=_pk(inp["pre_ffn_norm"][0], 8),
        w_fg=np.ascontiguousarray(inp["w_ffn_gate"][0]), w_fu=np.ascontiguousarray(inp["w_ffn_up"][0]),
        w_fd=np.ascontiguousarray(inp["w_ffn_down"][0]),
        g_pffn=np.ascontiguousarray(inp["post_ffn_norm"][0].reshape(1, D)),
    )


def make_in_maps(inp, S):
    x = np.asarray(inp["x"], dtype=np.float32)
    pos = np.asarray(inp["positions"], dtype=np.int32)
    W = _weights({k: np.asarray(v) for k, v in inp.items()})
    maps = []
    NT = S // 128
    for core in range(8):
        b, par = core // 2, core % 2
        xb = x[b, :S]
        xo = xb.reshape(NT // 2, 2, 128, D)[:, par].reshape(S // 2, D)
        pb = pos[b, :S]
        po = pb.reshape(NT // 2, 2, 128)[:, par].reshape(1, S // 2)
        m = dict(xall=np.ascontiguousarray(xb), xown=np.ascontiguousarray(xo),
                 posall=np.ascontiguousarray(pb.reshape(1, S)), posown=np.ascontiguousarray(po))
        m.update(_consts(par))
        m.update(W)
        maps.append(m)
    return maps


def run(inp, S, upto=99, dbg=False):
    kb = K(S, upto=upto, dbg=dbg)
    nc = kb.build()
    maps = make_in_maps(inp, S)
    maps = [{k: v for k, v in m.items() if k in kb.in_names} for m in maps]
    res = run_bass_kernel_spmd(nc, maps, core_ids=list(range(8)))
    B = 4
    NT = S // 128
    full = np.zeros((B, NT // 2, 2, 128, D), np.float32)
    for core in range(8):
        b, par = core // 2, core % 2
        full[b, :, par] = np.asarray(res.results[core]["out"]).reshape(NT // 2, 128, D)
    return full.reshape(B, S, D), res


def kernel(**inputs):
    o, _ = run(inputs, 8192)
    return o
```

```python
import math
from contextlib import ExitStack

import numpy as np
import concourse.bass as bass
import concourse.mybir as mybir
from concourse.bass_utils import run_bass_kernel_spmd

F32 = mybir.dt.float32
BF16 = mybir.dt.bfloat16
I32 = mybir.dt.int32
AF = mybir.ActivationFunctionType
ALU = mybir.AluOpType

D = 1024
KD = 8
DFF = 2816
KF = 22
EPS = 1e-6
NSH = 1936
NOW = 1920
GT = 4


PSUM_TOKENS = frozenset("b%d" % i for i in range(8))


class Op:
    __slots__ = ("eng", "fn", "deps", "signal", "sem", "count", "key", "idx")

    def __init__(self, eng, fn, key):
        self.eng, self.fn, self.key = eng, fn, key
        self.deps, self.signal, self.sem, self.count = [], False, None, 0


class Sched:
    ENGS = ("pe", "act", "dve", "pool", "sp")

    def __init__(self):
        self.ops = {e: [] for e in self.ENGS}
        self.last_w = {}
        self.readers = {}
        self.nops = 0

    def add(self, eng, fn, r=(), w=(), key=None):
        op = Op(eng, fn, key)
        if key is not None:
            op.signal = True
        op.idx = self.nops
        self.nops += 1
        raw = set()
        deps = set()
        for t in r:
            lw = self.last_w.get(t)
            if lw is not None:
                deps.add(lw)
                raw.add(lw)
            if t in PSUM_TOKENS:
                for rd in self.readers.get(t, {}).values():
                    if rd.eng != eng:
                        deps.add(rd)
        for t in w:
            lw = self.last_w.get(t)
            if lw is not None:
                deps.add(lw)
            for rd in self.readers.get(t, {}).values():
                deps.add(rd)
        for d in deps:
            if d is op:
                continue
            if d.key is None and key is None and d.eng == eng:
                if eng == "pe":
                    continue
            op.deps.append(d)
            d.signal = True
        for t in r:
            rk = eng if key is None else ("dma", op.idx)
            self.readers.setdefault(t, {})[rk] = op
        for t in w:
            self.last_w[t] = op
            self.readers[t] = {}
        self.ops[eng].append(op)
        return op

    def barrier(self):
        lasts = []
        for e in self.ENGS:
            last_eng = None
            last_dma = {}
            for op in self.ops[e]:
                if op.key is None:
                    last_eng = op
                else:
                    last_dma[op.key] = op
            if last_eng is not None:
                lasts.append(last_eng)
            lasts.extend(last_dma.values())
        for e in self.ENGS:
            op = Op(e, lambda eng: eng.nop(), None)
            op.idx = self.nops
            self.nops += 1
            for d in lasts:
                if d.key is None and d.eng == e:
                    continue
                op.deps.append(d)
                d.signal = True
            self.ops[e].append(op)

    def finalize(self, nc, es):
        self.sems = {}
        cnt = {}
        for e in self.ENGS:
            for op in self.ops[e]:
                if not op.signal:
                    continue
                if op.key is not None:
                    k = ("dma", op.key)
                    inc = 16
                else:
                    k = ("eng", e)
                    inc = 1
                if k not in self.sems:
                    self.sems[k] = es.enter_context(nc.semaphore("s_%s_%s" % (k[0], str(k[1]))))
                    cnt[k] = 0
                cnt[k] += inc
                op.sem = k
                op.count = cnt[k]

    def emit(self, eng_name, engine):
        waited = {}
        for op in self.ops[eng_name]:
            need = {}
            for d in op.deps:
                if need.get(d.sem, 0) < d.count:
                    need[d.sem] = d.count
            for k, v in need.items():
                if waited.get(k, 0) < v:
                    engine.wait_ge(self.sems[k], v)
                    waited[k] = v
            ins = op.fn(engine)
            if op.signal:
                ins.then_inc(self.sems[op.sem], 16 if op.key is not None else 1)


class K:
    def __init__(self, S, upto=99, dbg=False):
        self.S = S
        self.NT = S // 128
        self.NOWN = self.NT // 2
        self.NG = self.NOWN // GT
        assert self.NG * GT * 2 * 128 == S
        self.upto = upto
        self.dbg = dbg
        self.nc = bass.Bass("TRN2", target_bir_lowering=False)
        self.s = Sched()
        self.dkeys = {}

    def mm(self, out, lhsT, rhs, start, stop, r, w):
        return self.s.add("pe", lambda e: e.matmul(out, lhsT, rhs, start=start, stop=stop), r, w)

    def tr(self, out, in_, r, w):
        ident = self.identb[:]
        return self.s.add("pe", lambda e: e.transpose(out, in_, ident), r + ["ident"], w)

    def act(self, out, in_, func, r, w, scale=None, bias=None, accum=None):
        kw = {}
        if scale is not None:
            kw["scale"] = scale
        if bias is not None:
            kw["bias"] = bias
        if accum is not None:
            kw["accum_out"] = accum
        return self.s.add("act", lambda e: e.activation(out, in_, func, **kw), r, w)

    def ts(self, eng, out, in0, s1, op0, r, w, s2=None, op1=None):
        if op1 is None:
            return self.s.add(eng, lambda e: e.tensor_scalar(out, in0, s1, None, op0), r, w)
        return self.s.add(eng, lambda e: e.tensor_scalar(out, in0, s1, s2, op0, op1), r, w)

    def tt(self, eng, out, in0, in1, op, r, w):
        return self.s.add(eng, lambda e: e.tensor_tensor(out, in0, in1, op), r, w)

    def stt(self, out, in0, scalar, in1, op0, op1, r, w):
        return self.s.add("dve", lambda e: e.scalar_tensor_tensor(out, in0, scalar, in1, op0, op1), r, w)

    def cp(self, eng, out, in_, r, w):
        if eng == "act":
            return self.s.add("act", lambda e: e.activation(out, in_, AF.Copy), r, w)
        return self.s.add(eng, lambda e: e.tensor_copy(out, in_), r, w)

    def memset(self, eng, ap, val, w):
        return self.s.add(eng, lambda e: e.memset(ap, val), [], w)

    def recip(self, out, in_, r, w):
        return self.s.add("dve", lambda e: e.reciprocal(out, in_), r, w)

    def dma(self, q, out, in_, key, r, w):
        base = key.rstrip("0123456789")
        ch = "chain_" + base
        return self.s.add(q, lambda e: e.dma_start(out, in_), list(r) + [ch], list(w) + [ch], key=base)

    def sb(self, es, name, shape, dt):
        return es.enter_context(self.nc.sbuf_tensor(name, list(shape), dt))

    def din(self, name, shape, dt=F32):
        t = self.nc.dram_tensor(name, list(shape), dt, kind="ExternalInput").ap()
        self.in_names.append(name)
        return t

    def dscr(self, name, shape, dt):
        return self.nc.dram_tensor(name, list(shape), dt, kind="Internal").ap()

    def probe(self, tag):
        import os
        if not os.environ.get("SBPROBE"):
            return
        try:
            with self.nc.sbuf_tensor("probe_" + tag, [128, 100000], F32):
                pass
        except AssertionError as e:
            msg = str(e)
            i = msg.find("(base=")
            print("SBUF", tag, msg[i:i + 40])

    def rstd_from_ss(self, ss, lnv, rstd, n, rt, wt):
        self.act(lnv, ss, AF.Ln, rt, [wt + "_ln"], scale=1.0 / n, bias=self.epsc[:, 0:1])
        self.act(rstd, lnv, AF.Exp, [wt + "_ln"], [wt], scale=-0.5)

    def load_w(self, dst, src, nk, ncol, gain, tok, defer=False):
        engs = ("dve", "act")
        CB = self.wcb
        for k in range(nk):
            for c0 in range(0, ncol, CB):
                c1 = min(ncol, c0 + CB)
                sl = self.wslot % 2
                self.wslot += 1
                stg = self.wstage[sl]
                self.dma("sp", stg[:, 0:c1 - c0], src[k * 128:(k + 1) * 128, c0:c1], "wst" + "AB"[sl],
                         [], ["wst%d" % sl])
                eng = engs[self.wrot % 2]
                self.wrot += 1
                o = dst[:, k, c0:c1]
                i = stg[:, 0:c1 - c0]
                wt = [tok + "#" + eng]
                if gain is not None:
                    g = gain[:, k:k + 1]
                    if eng == "act":
                        self.act(o, i, AF.Copy, ["wst%d" % sl, "gains"], wt, scale=g)
                    else:
                        self.ts(eng, o, i, g, ALU.mult, ["wst%d" % sl, "gains"], wt)
                else:
                    self.cp(eng, o, i, ["wst%d" % sl], wt)

        def join():
            self.s.add("pe", lambda e: e.nop(), [tok + "#dve", tok + "#act"], [tok])
        if defer:
            return join
        join()
        return None

    def norm_T(self, xt, xtok, hT_ap, htok, u):
        self.norm_T_multi([(xt, xtok, hT_ap, htok, u)])

    def norm_T_multi(self, items):
        junk = self.junk
        for (xt, xtok, hT_ap, htok, u) in items:
            self.act(junk[:], xt, AF.Square, [xtok], ["junk", "ss%d" % u], accum=self.st_ss[u][:, 0:1])
        for (xt, xtok, hT_ap, htok, u) in items:
            self.act(self.st_ln[u][:, 0:1], self.st_ss[u][:, 0:1], AF.Ln, ["ss%d" % u], ["rs%d_ln" % u],
                     scale=1.0 / D, bias=self.epsc[:, 0:1])
        for (xt, xtok, hT_ap, htok, u) in items:
            self.act(self.st_rs[u][:, 0:1], self.st_ln[u][:, 0:1], AF.Exp, ["rs%d_ln" % u], ["rs%d" % u], scale=-0.5)
        b0 = self.pb16[0]
        for (xt, xtok, hT_ap, htok, u) in items:
            xu = u % len(self.xs)
            xs = self.xs[xu]
            self.ts("dve", xs[:], xt, self.st_rs[u][:, 0:1], ALU.mult, [xtok, "rs%d" % u], ["xs%d" % xu])
            for k in range(KD):
                self.tr(b0[:, k * 128:(k + 1) * 128], xs[:, k * 128:(k + 1) * 128], ["xs%d" % xu], ["b0"])
            self.cp("act", hT_ap, b0[:, 0:1024].rearrange("p (k n) -> p k n", k=KD), ["b0"], [htok])

    def build(self):
        nc, S, NT, NOWN, NG = self.nc, self.S, self.NT, self.NOWN, self.NG
        self.in_names = []
        SO = S // 2
        xall = self.din("xall", [S, D])
        xown = self.din("xown", [SO, D])
        posall = self.din("posall", [1, S], I32)
        posown = self.din("posown", [1, SO], I32)
        invf = self.din("invf", [64, 1])
        ident = self.din("ident", [128, 128])
        umat = self.din("umat", [128, 128])
        cind = self.din("cind", [128, 2])
        qmask = self.din("qmask", [128, 2])
        amask = self.din("amask", [128, 256])
        w_sh = self.din("w_sh", [D, NSH])
        w_ow = self.din("w_ow", [D, NOW])
        g_pm = self.din("g_pm", [128, 8])
        w_a2 = self.din("w_a2", [16, 512])
        b_a2 = self.din("b_a2", [1, 512])
        g_gla = self.din("g_gla", [128, 8])
        w_og = self.din("w_og", [D, D])
        g_q = self.din("g_q", [128, 3])
        w_uq = self.din("w_uq", [384, 2048])
        g_kv = self.din("g_kv", [128, 2])
        w_ukT = self.din("w_ukT", [128, 2048])
        w_uv = self.din("w_uv", [256, 1024])
        w_om = self.din("w_om", [D, D])
        w_g = self.din("w_g", [D, 2048])
        b_g = self.din("b_g", [128, 16])
        w_out = self.din("w_out", [D, D])
        g_pmix = self.din("g_pmix", [1, D])
        g_pf = self.din("g_pf", [128, 8])
        w_fg = self.din("w_fg", [D, DFF])
        w_fu = self.din("w_fu", [D, DFF])
        w_fd = self.din("w_fd", [DFF, D])
        g_pffn = self.din("g_pffn", [1, D])
        out = nc.dram_tensor("out", [SO, D], F32, kind="ExternalOutput").ap()

        rope_all = self.dscr("rope_all", [2, 64, S], F32)
        rope_own = self.dscr("rope_own", [2, 64, SO], F32)
        sc_og = self.dscr("sc_og", [NG, 128, 8, 512], BF16)
        sc_cq = self.dscr("sc_cq", [NOWN, 128, 3, 128], BF16)
        sc_om = self.dscr("sc_om", [NG, 128, 8, 512], BF16)
        sc_x1 = self.dscr("sc_x1", [SO, D], F32)
        self.dbg_out = None
        if self.dbg:
            self.dbg_out = nc.dram_tensor("dbg", [128, 4096], F32, kind="ExternalOutput").ap()

        with ExitStack() as es:
            self.identf = self.sb(es, "identf", [128, 128], F32)
            self.identb = self.sb(es, "identb", [128, 128], BF16)
            self.epsc = self.sb(es, "epsc", [128, 1], F32)
            self.onec = self.sb(es, "onec", [128, 1], F32)
            self.hpic = self.sb(es, "hpic", [128, 1], F32)
            self.junk = self.sb(es, "junk", [128, 1024], BF16)
            self.st_ss = [self.sb(es, "st_ss%d" % u, [128, 8], F32) for u in range(4)]
            self.st_ln = [self.sb(es, "st_ln%d" % u, [128, 8], F32) for u in range(4)]
            self.st_rs = [self.sb(es, "st_rs%d" % u, [128, 8], F32) for u in range(4)]
            self.xs = [self.sb(es, "xs%d" % u, [128, 1024], BF16) for u in range(2)]
            pbanks = [es.enter_context(nc.psum_tensor("pb%d" % i, [128, 512], F32)) for i in range(8)]
            self.pb = pbanks
            self.pb16 = [b[:].bitcast(BF16) for b in pbanks]
            self.wslot = 0
            self.wcb = 2048
            self.wrot = 0
            pb, pb16 = self.pb, self.pb16

            self.dma("sp", self.identf[:], ident, "misc", [], ["identf"])
            self.cp("dve", self.identb[:], self.identf[:], ["identf"], ["ident"])
            self.memset("dve", self.epsc[:], EPS, ["epsc"])
            self.memset("dve", self.onec[:], 1.0, ["onec"])
            self.memset("dve", self.hpic[:], math.pi / 2, ["hpic"])

            with ExitStack() as er:
                CH = min(2048, SO)
                invf_t = self.sb(er, "invf_t", [64, 1], F32)
                posi = self.sb(er, "posi", [64, CH], I32)
                ang = self.sb(er, "ang", [64, CH], F32)
                tq = self.sb(er, "tq", [64, CH], F32)
                rr = self.sb(er, "rr", [64, CH], F32)
                g1 = self.sb(er, "g1", [64, CH], F32)
                co = self.sb(er, "co", [64, CH], F32)
                si = self.sb(er, "si", [64, CH], F32)
                self.dma("sp", invf_t[:], invf, "misc2", [], ["invf"])
                MAGIC = 12582912.0
                C1 = 6.28125
                C2 = 2.0 * math.pi - 6.28125
                PI = math.pi
                for (pos_d, rope_d, n) in ((posall, rope_all, S), (posown, rope_own, SO)):
                    for c0 in range(0, n, CH):
                        self.dma("sp", posi[:], pos_d[0:1, c0:c0 + CH].partition_broadcast(64), "posi",
                                 [], ["posi"])
                        self.cp("dve", ang[:], posi[:], ["posi"], ["ang"])
                        self.ts("dve", ang[:], ang[:], invf_t[:, 0:1], ALU.mult, ["ang", "invf"], ["ang"])
                        self.ts("dve", tq[:], ang[:], 1.0 / (2 * PI), ALU.mult, ["ang"], ["tq"], s2=MAGIC, op1=ALU.add)
                        self.ts("dve", tq[:], tq[:], -MAGIC, ALU.add, ["tq"], ["tq"])
                        self.stt(rr[:], tq[:], -C1, ang[:], ALU.mult, ALU.add, ["tq", "ang"], ["rr"])
                        self.stt(rr[:], tq[:], -C2, rr[:], ALU.mult, ALU.add, ["tq", "rr"], ["rr"])
                        self.ts("dve", g1[:], rr[:], PI, ALU.is_gt, ["rr"], ["g1"], s2=-2 * PI, op1=ALU.mult)
                        self.tt("dve", rr[:], rr[:], g1[:], ALU.add, ["rr", "g1"], ["rr"])
                        self.ts("dve", g1[:], rr[:], -PI, ALU.is_lt, ["rr"], ["g1"], s2=2 * PI, op1=ALU.mult)
                        self.tt("dve", rr[:], rr[:], g1[:], ALU.add, ["rr", "g1"], ["rr"])
                        self.ts("dve", rr[:], rr[:], -3.1415925, ALU.max, ["rr"], ["rr"], s2=3.1415925, op1=ALU.min)
                        self.act(si[0:32, :], rr[0:32, :], AF.Sin, ["rr"], ["si"], scale=-1.0)
                        self.act(si[32:64, :], rr[32:64, :], AF.Sin, ["rr"], ["si"])
                        self.act(g1[:], rr[:], AF.Abs, ["rr"], ["g1"])
                        self.act(co[:], g1[:], AF.Sin, ["g1"], ["co"], scale=-1.0, bias=self.hpic[0:64, 0:1])
                        self.dma("sp", rope_d[0, :, c0:c0 + CH], co[:], "ropest0", ["co"], ["rope_dram"])
                        self.dma("sp", rope_d[1, :, c0:c0 + CH], si[:], "ropest1", ["si"], ["rope_dram"])

            self.s.barrier()
            if self.upto >= 1:
                self.pass_A(es, locals())
            self.finish(es, out)
        return nc

    def finish(self, es, out):
        nc = self.nc
        self.s.add("sp", lambda e: e.nop(), ["out_dram", "dbg_dram"], [])
        self.s.finalize(nc, es)
        with nc.Block() as block:
            @block.sync
            def _(e):
                self.s.emit("sp", e)

            @block.tensor
            def _(e):
                self.s.emit("pe", e)

            @block.scalar
            def _(e):
                self.s.emit("act", e)

            @block.vector
            def _(e):
                self.s.emit("dve", e)

            @block.gpsimd
            def _(e):
                self.s.emit("pool", e)

    def pass_A(self, es_outer, L):
        nc, S, NT, NOWN, NG = self.nc, self.S, self.NT, self.NOWN, self.NG
        pb, pb16 = self.pb, self.pb16
        xall, xown = L["xall"], L["xown"]
        with ExitStack() as es:
            vaug = self.sb(es, "vaug", [128, NT, 258], BF16)
            kpeT = self.sb(es, "kpeT", [64, S], BF16)
            self.vaug, self.kpeT = vaug, kpeT
            self.memset("pool", vaug[:, :, 256:258], 1.0, ["vaug_ones"])
            with ExitStack() as ea:
                WS = self.sb(ea, "WS", [128, KD, NSH], BF16)
                WO = self.sb(ea, "WO", [128, KD, NOW], BF16)
                gpm = self.sb(ea, "gpm", [128, 8], F32)
                wa2a = self.sb(ea, "wa2a", [32, 512], BF16)
                umb = self.sb(ea, "umb", [128, 128], BF16)
                cif = self.sb(ea, "cif", [128, 2], F32)
                cib = self.sb(ea, "cib", [128, 2], BF16)
                qmf = self.sb(ea, "qmf", [128, 2], F32)
                ew = ExitStack()
                wa2f = self.sb(ew, "wa2f", [32, 512], F32)
                umf = self.sb(ew, "umf", [128, 128], F32)
                self.dma("sp", gpm[:], L["g_pm"], "misc", [], ["gains"])
                self.memset("dve", wa2f[:], 0.0, ["wa2f"])
                self.dma("sp", wa2f[0:16, :], L["w_a2"], "misc2", ["wa2f"], ["wa2f"])
                self.dma("sp", wa2f[16:17, :], L["b_a2"], "misc3", ["wa2f"], ["wa2f"])
                self.cp("dve", wa2a[:], wa2f[:], ["wa2f"], ["wa2a"])
                self.dma("sp", umf[:], L["umat"], "misc4", [], ["umf"])
                self.cp("dve", umb[:], umf[:], ["umf"], ["umb"])
                self.dma("sp", cif[:], L["cind"], "misc5", [], ["cif"])
                self.cp("dve", cib[:], cif[:], ["cif"], ["cib"])
                self.dma("sp", qmf[:], L["qmask"], "misc6", [], ["qmf"])
                self.ts("dve", qmf[:], qmf[:], 128.0 ** -0.5, ALU.mult, ["qmf"], ["qmf"])
                with ew:
                    self.wstage = [self.sb(ew, "wstg%d" % i, [128, 2048], F32) for i in range(2)]
                    self.load_w(WS, L["w_sh"], KD, NSH, gpm, "WS")
                    self.load_w(WO, L["w_ow"], KD, NOW, gpm, "WO")
                self.s.barrier()
                xin1 = [self.sb(ea, "xin_%d" % i, [128, 1024], F32) for i in range(3)]
                xin = [xin1, xin1]
                hTa = self.sb(ea, "hTa", [128, KD, 128], BF16)
                hT = [[hTa] + [self.sb(ea, "hT%d_%d" % (sl, i), [128, KD, 128], BF16) for i in (1, 2)] for sl in range(2)]
                hTn = [["hTa", "hT%d_1" % sl, "hT%d_2" % sl] for sl in range(2)]
                haT = [self.sb(ea, "haT%d" % i, [32, 128], BF16) for i in range(2)]
                ksb = [self.sb(ea, "ksb%d" % i, [128, 512], F32) for i in range(2)]
                e1s = self.sb(ea, "e1s", [128, 512], F32)
                e1 = [e1s, e1s]
                nl = [self.sb(ea, "nl%d" % i, [128, 512], BF16) for i in range(2)]
                dfac = [self.sb(ea, "dfac%d" % i, [128, 512], F32) for i in range(2)]
                dc = [self.sb(ea, "dc%d" % i, [128, 8], F32) for i in range(2)]
                kdec = [self.sb(ea, "kdec%d" % i, [128, 512], BF16) for i in range(2)]
                vb = [self.sb(ea, "vb%d" % i, [128, 1024], BF16) for i in range(2)]
                stc = [[self.sb(ea, "stc%d_%d" % (i, q), [128, 4], F32) for q in range(3)] for i in range(3)]
                Sst = self.sb(ea, "Sst", [128, 4, 256], F32)
                Sbf = self.sb(ea, "Sbf", [128, 4, 4, 256], BF16)
                qpad = self.sb(ea, "qpad", [128, 4, 4, 128], BF16)
                eg = self.sb(ea, "eg", [128, 1024], F32)
                sg = self.sb(ea, "sg", [128, 1024], F32)
                osb = eg
                og = self.sb(ea, "og", [128, 1024], BF16)
                ogT = self.sb(ea, "ogT", [128, KD, 128], BF16)
                cqn = self.sb(ea, "cqn", [128, 384], BF16)
                cqnT = self.sb(ea, "cqnT", [128, 3, 128], BF16)
                rcs1 = self.sb(ea, "rcs", [64, 2, 128], F32)
                t11 = self.sb(ea, "t1", [64, 128], F32)
                t21 = self.sb(ea, "t2", [64, 128], F32)
                rcs, t1, t2 = [rcs1, rcs1], [t11, t11], [t21, t21]
                sso = self.sb(ea, "sso", [128, 4], F32)
                lno = self.sb(ea, "lno", [128, 4], F32)
                rso = self.sb(ea, "rso", [128, 4], F32)
                for i in range(2):
                    self.memset("dve", haT[i][:], 1.0, ["haT%d" % i])
                self.memset("dve", Sst[:], 0.0, ["Sst"])
                self.memset("pool", qpad[:], 0.0, ["qpad"])

                rope_all = L["rope_all"]
                NP = NT // 2

                def S1(j):
                    sl = j % 2
                    items = []
                    for i in range(3):
                        src = xall[(2 * j + i) * 128:(2 * j + i + 1) * 128, :] if i < 2 else xown[j * 128:(j + 1) * 128, :]
                        tk = "xin_%d" % i
                        self.dma("sp", xin[sl][i][:], src, "xin", [], [tk])
                        items.append((xin[sl][i][:], tk, hT[sl][i][:], hTn[sl][i], i))
                    self.norm_T_multi(items)

                def proj(j, ab):
                    sl = j % 2
                    t = 2 * j + ab
                    h, hk = hT[sl][ab], hTn[sl][ab]
                    cb = 5 + ab
                    for k in range(KD):
                        self.mm(pb[cb][0:16, 0:128], WS[:, k, 1920:1936], h[:, k, :], k == 0, k == KD - 1,
                                [hk, "WS"], ["b%d" % cb])
                    self.cp("dve", haT[ab][0:16, :], pb[cb][0:16, 0:128], ["b%d" % cb], ["haT%d" % ab])
                    for (bank, c0, c1) in ((1, 0, 512), (2, 512, 1024), (3, 1024, 1536)):
                        for k in range(KD):
                            self.mm(pb[bank][:, 0:512], h[:, k, :], WS[:, k, c0:c1], k == 0, k == KD - 1,
                                    [hk, "WS"], ["b%d" % bank])
                        if bank == 1:
                            self.cp("act", ksb[ab][:], pb[1][:, 0:512], ["b1"], ["ksb%d" % ab])
                            self.mm(pb[cb][:, 0:512], haT[ab][:, :], wa2a[:, :], True, True, ["haT%d" % ab, "wa2a"],
                                    ["b%d" % cb])
                            self.act(e1[ab][:], pb[cb][:, 0:512], AF.Exp, ["b%d" % cb], ["e1s"], scale=-1.0)
                            self.act(nl[ab][:], e1[ab][:], AF.Ln, ["e1s"], ["nl%d" % ab], bias=self.onec[:, 0:1])
                        elif bank == 2:
                            self.cp("dve", vb[ab][:, 0:512], pb[2][:, 0:512], ["b2"], ["vb%d" % ab])
                        else:
                            self.cp("act", vb[ab][:, 512:1024], pb[3][:, 0:512], ["b3"], ["vb%d" % ab])
                    for k in range(KD):
                        self.mm(pb[4][:, 0:256], h[:, k, :], WS[:, k, 1536:1792], k == 0, k == KD - 1,
                                [hk, "WS"], ["b4"])
                    for (o0, c0) in ((256, 1792), (384, 1856)):
                        for k in range(KD):
                            self.mm(pb[4][0:64, o0:o0 + 128], WS[:, k, c0:c0 + 64], h[:, k, :], k == 0, k == KD - 1,
                                    [hk, "WS"], ["b4"])
                    ssc, lnc, rsc = stc[ab]
                    self.act(self.junk[:, 0:256], pb[4][:, 0:256], AF.Square, ["b4"], ["junk", "ssc%d" % ab],
                             accum=ssc[:, 0:1])
                    self.rstd_from_ss(ssc[:, 0:1], lnc[:, 0:1], rsc[:, 0:1], 256, ["ssc%d" % ab], "rsc%d" % ab)
                    self.ts("dve", vaug[:, t, 0:256], pb[4][:, 0:256], rsc[:, 0:1], ALU.mult, ["b4", "rsc%d" % ab],
                            ["vaug%d" % t])
                    self.dma("sp", rcs[ab][:, :, :], rope_all[:, :, t * 128:(t + 1) * 128].rearrange("a p n -> p a n"),
                             "rcs", ["rope_dram"], ["rcs_"])
                    self.tt("dve", t1[ab][:], pb[4][0:64, 256:384], rcs[ab][:, 0, :], ALU.mult, ["b4", "rcs_"],
                            ["t1_"])
                    self.tt("dve", t2[ab][:], pb[4][0:64, 384:512], rcs[ab][:, 1, :], ALU.mult, ["b4", "rcs_"],
                            ["t2_"])
                    self.tt("dve", kpeT[:, t * 128:(t + 1) * 128], t1[ab][:], t2[ab][:], ALU.add,
                            ["t1_", "t2_"], ["kpeT%d" % t])

                def chain(ab):
                    cb = 5 + ab
                    self.mm(pb[cb][:, 0:512], umb[:, :], nl[ab][:, :], True, True, ["umb", "nl%d" % ab], ["b%d" % cb])
                    self.act(dfac[ab][:], pb[cb][:, 0:512], AF.Exp, ["b%d" % cb], ["dfac%d" % ab], scale=-1.0 / 16.0)
                    for hh in range(4):
                        self.mm(pb[cb][:, 2 * hh:2 * hh + 2], nl[ab][:, hh * 128:(hh + 1) * 128], cib[:, :], True, True,
                                ["nl%d" % ab, "cib"], ["b%d" % cb])
                    self.act(dc[ab][:], pb[cb][:, 0:8], AF.Exp, ["b%d" % cb], ["dc%d" % ab], scale=-1.0 / 16.0)
                    self.tt("dve", kdec[ab][:], ksb[ab][:], dfac[ab][:], ALU.mult, ["ksb%d" % ab, "dfac%d" % ab],
                            ["kdec%d" % ab])

                def state(ab):
                    for c in range(2):
                        pc = 2 * ab + c
                        for half in range(2):
                            bank = 1 + 2 * c + half
                            for q in range(2):
                                hh = 2 * half + q
                                self.mm(pb[bank][:, q * 256:q * 256 + 256],
                                        kdec[ab][c * 64:(c + 1) * 64, hh * 128:(hh + 1) * 128],
                                        vb[ab][c * 64:(c + 1) * 64, hh * 256:(hh + 1) * 256], True, True,
                                        ["kdec%d" % ab, "vb%d" % ab], ["b%d" % bank])
                        for half in range(2):
                            bank = 1 + 2 * c + half
                            for q in range(2):
                                hh = 2 * half + q
                                self.stt(Sst[:, hh, :], Sst[:, hh, :], dc[ab][:, 2 * hh + c:2 * hh + c + 1],
                                         pb[bank][:, q * 256:q * 256 + 256], ALU.mult, ALU.add,
                                         ["Sst", "dc%d" % ab, "b%d" % bank], ["Sst"])
                        self.cp("act", Sbf[:, pc, :, :], Sst[:, :, :], ["Sst"], ["Sbf%d" % pc])

                def own_proj(j):
                    sl = j % 2
                    h, hk = hT[sl][2], hTn[sl][2]
                    for hh in range(4):
                        for k in range(KD):
                            self.mm(pb[1][:, hh * 128:(hh + 1) * 128], WO[:, k, hh * 128:(hh + 1) * 128], h[:, k, :],
                                    k == 0, k == KD - 1, [hk, "WO"], ["b1"])
                    qv = pb[1][:, 0:512].rearrange("p (h n) -> p h n", h=4)
                    for m in range(2):
                        for cc in range(2):
                            c = 2 * m + cc
                            self.ts("dve", qpad[:, c, :, cc * 64:(cc + 1) * 64], qv[:, :, cc * 64:(cc + 1) * 64],
                                    qmf[:, m:m + 1], ALU.mult, ["b1", "qmf"], ["qpad"])
                    for (bank, c0) in ((2, 512), (3, 1024)):
                        for k in range(KD):
                            self.mm(pb[bank][:, 0:512], h[:, k, :], WO[:, k, c0:c0 + 512], k == 0, k == KD - 1,
                                    [hk, "WO"], ["b%d" % bank])
                        o0 = c0 - 512
                        self.act(eg[:, o0:o0 + 512], pb[bank][:, 0:512], AF.Exp, ["b%d" % bank], ["eg%d" % bank], scale=-1.0)
                        self.ts("dve", eg[:, o0:o0 + 512], eg[:, o0:o0 + 512], 1.0, ALU.add, ["eg%d" % bank], ["eg%d" % bank])
                        self.recip(eg[:, o0:o0 + 512], eg[:, o0:o0 + 512], ["eg%d" % bank], ["eg%d" % bank])
                        self.tt("dve", sg[:, o0:o0 + 512], eg[:, o0:o0 + 512], pb[bank][:, 0:512], ALU.mult,
                                ["eg%d" % bank, "b%d" % bank], ["sg%d" % bank])
                    for k in range(KD):
                        self.mm(pb[4][:, 0:384], h[:, k, :], WO[:, k, 1536:1920], k == 0, k == KD - 1,
                                [hk, "WO"], ["b4"])
                    ssq, lnq, rsq = stc[2]
                    self.act(self.junk[:, 0:384], pb[4][:, 0:384], AF.Square, ["b4"], ["junk", "ssq"], accum=ssq[:, 0:1])
                    self.rstd_from_ss(ssq[:, 0:1], lnq[:, 0:1], rsq[:, 0:1], 384, ["ssq"], "rsq")
                    self.ts("dve", cqn[:], pb[4][:, 0:384], rsq[:, 0:1], ALU.mult, ["b4", "rsq"], ["cqn"])

                def own_out(j):
                    for hh in range(4):
                        bank = 5 + hh // 2
                        for c in range(4):
                            self.mm(pb[bank][:, (hh % 2) * 256:(hh % 2) * 256 + 256], qpad[:, c, hh, :],
                                    Sbf[:, c, hh, :], c == 0, c == 3, ["qpad", "Sbf%d" % c], ["b%d" % bank])
                    for hf in range(2):
                        self.cp("act", osb[:, hf * 512:(hf + 1) * 512], pb[5 + hf][:, 0:512], ["b%d" % (5 + hf)],
                                ["eg%d" % (2 + hf)])
                    for hh in range(4):
                        self.act(self.junk[:, 0:256], osb[:, hh * 256:(hh + 1) * 256], AF.Square,
                                 ["eg%d" % (2 + hh // 2)], ["junk", "sso"], accum=sso[:, hh:hh + 1])
                    self.act(lno[:], sso[:], AF.Ln, ["sso"], ["lno"], scale=1.0 / 256, bias=self.epsc[:, 0:1])
                    self.act(rso[:], lno[:], AF.Exp, ["lno"], ["rso"], scale=-0.5)
                    for hh in range(4):
                        self.stt(og[:, hh * 256:(hh + 1) * 256], osb[:, hh * 256:(hh + 1) * 256],
                                 rso[:, hh:hh + 1], sg[:, hh * 256:(hh + 1) * 256], ALU.mult, ALU.mult,
                                 ["eg%d" % (2 + hh // 2), "rso", "sg%d" % (2 + hh // 2)], ["og"])
                    for k in range(KD):
                        self.tr(pb16[0][:, k * 128:(k + 1) * 128], og[:, k * 128:(k + 1) * 128], ["og"], ["b0"])
                    self.cp("act", ogT[:], pb16[0][:, 0:1024].rearrange("p (k n) -> p k n", k=KD), ["b0"], ["ogT"])
                    gi, ii = j // GT, j % GT
                    self.dma("sp", L["sc_og"][gi, :, :, ii * 128:(ii + 1) * 128], ogT[:], "ogst", ["ogT"], ["sc_og"])
                    for k in range(3):
                        self.tr(pb16[0][:, k * 128:(k + 1) * 128], cqn[:, k * 128:(k + 1) * 128], ["cqn"], ["b0"])
                    self.cp("dve", cqnT[:], pb16[0][:, 0:384].rearrange("p (k n) -> p k n", k=3), ["b0"], ["cqnT"])
                    self.dma("sp", L["sc_cq"][j], cqnT[:], "cqst", ["cqnT"], ["sc_cq"])

                self.probe('A')
                S1(0)
                for j in range(NP):
                    proj(j, 0)
                    if j + 1 < NP:
                        S1(j + 1)
                    proj(j, 1)
                    chain(0)
                    own_proj(j)
                    chain(1)
                    state(0)
                    state(1)
                    own_out(j)
            self.s.barrier()
            if self.upto >= 2:
                self.pass_B1(es, L)
                self.s.barrier()
        if self.upto >= 3:
            self.pass_B2(es_outer, L)
            self.s.barrier()
        if self.upto >= 4:
            self.pass_C(es_outer, L)

    def dbg_dump_A(self, L):
        pass

    def pass_B1(self, es_outer, L):
        nc, S, NT, NOWN, NG = self.nc, self.S, self.NT, self.NOWN, self.NG
        pb, pb16 = self.pb, self.pb16
        vaug, kpeT = self.vaug, self.kpeT
        SCALE = 192.0 ** -0.5
        with ExitStack() as es:
            ckvT = self.sb(es, "ckvT", [128, 2, S], BF16)
            WUQ = self.sb(es, "WUQ", [128, 3, 2048], BF16)
            WUKT = self.sb(es, "WUKT", [128, 1, 2048], BF16)
            WUV = self.sb(es, "WUV", [128, 2, 1024], BF16)
            gq = self.sb(es, "gq", [128, 3], F32)
            gkv = self.sb(es, "gkv", [128, 2], F32)
            amf = self.sb(es, "amf", [128, 256], F32)
            amb = self.sb(es, "amb", [128, 2, 128], BF16)
            self.dma("sp", gq[:], L["g_q"], "misc", [], ["gains"])
            self.dma("sp", gkv[:], L["g_kv"], "misc2", [], ["gains"])
            self.dma("sp", amf[:], L["amask"], "misc3", [], ["amf"])
            self.cp("dve", amb[:].rearrange("p a n -> p (a n)"), amf[:], ["amf"], ["amb"])
            with ExitStack() as ew:
                self.wstage = [self.sb(ew, "wstgb%d" % i, [128, 2048], F32) for i in range(2)]
                self.load_w(WUQ, L["w_uq"], 3, 2048, gq, "WUQ")
                self.load_w(WUKT, L["w_ukT"], 1, 2048, None, "WUKT")
                self.load_w(WUV, L["w_uv"], 2, 1024, gkv, "WUV")
            self.s.barrier()
            for t in range(NT):
                for lc in range(2):
                    self.tr(pb16[0][:, lc * 128:(lc + 1) * 128], vaug[:, t, lc * 128:(lc + 1) * 128],
                            ["vaug%d" % t], ["b0"])
                self.cp("act" if t % 2 else "dve", ckvT[:, :, t * 128:(t + 1) * 128],
                        pb16[0][:, 0:256].rearrange("p (a n) -> p a n", a=2), ["b0"], ["ckvT%d" % t])
            import os
            B1STOP = int(os.environ.get("B1STOP", "9"))
            if B1STOP < 1:
                return
            cqT = [self.sb(es, "cqT%d" % i, [128, 3, 128], BF16) for i in range(2)]
            rco = [self.sb(es, "rco%d" % i, [64, 2, 128], F32) for i in range(2)]
            qn = [self.sb(es, "qn%d" % i, [128, 128], BF16) for i in range(2)]
            qpe = self.sb(es, "qpe", [64, 8, 128], BF16)
            qabs = self.sb(es, "qabs", [128, 2, 8, 128], BF16)
            t1 = self.sb(es, "bt1", [64, 128], F32)
            t2 = self.sb(es, "bt2", [64, 128], F32)
            PT = [self.sb(es, "PT%d" % i, [128, 4, 128], BF16) for i in range(3)]
            rsum = self.sb(es, "rsum", [128, 8], F32)
            olat = self.sb(es, "olat", [128, 8, 256], BF16)
            olatT = self.sb(es, "olatT", [128, 8, 2, 128], BF16)
            omT = self.sb(es, "omT", [128, 8, 128], BF16)
            self.probe('B1')
            def qprep(j):
                    sl = j % 2
                    XV = int(os.environ.get("XV", "3"))
                    if XV & 1:
                        self.dma("sp", cqT[sl][:], L["sc_cq"][j], "cqT%d" % sl, ["sc_cq"], ["cqT%d" % sl])
                    if XV & 2:
                        self.dma("sp", rco[sl][:], L["rope_own"][:, :, j * 128:(j + 1) * 128].rearrange("a p n -> p a n"),
                             "rco%d" % sl, ["rope_dram"], ["rco%d" % sl])
                    cq = cqT[sl]
                    QV = int(os.environ.get("QV", "9"))
                    for h in range(8):
                        if QV < 2:
                            continue
                        qs = h % 2
                        qb = 1 + qs
                        qbt = "b%d" % qb
                        for k in range(3):
                            self.mm(pb[qb][:, 0:128], WUQ[:, k, h * 256:h * 256 + 128], cq[:, k, :], k == 0, k == 2,
                                    ["cqT%d" % sl, "WUQ"], [qbt])
                        for (o0, c0) in ((128, 128), (256, 192)):
                            for k in range(3):
                                self.mm(pb[qb][0:64, o0:o0 + 128], WUQ[:, k, h * 256 + c0:h * 256 + c0 + 64], cq[:, k, :],
                                        k == 0, k == 2, ["cqT%d" % sl, "WUQ"], [qbt])
                        YV = int(os.environ.get("YV", "9"))
                        if YV < 1:
                            continue
                        self.cp("act", qn[qs][:], pb[qb][:, 0:128], [qbt], ["qn%d" % qs])
                        if YV < 2:
                            continue
                        self.tt("dve", t1[:], pb[qb][0:64, 128:256], rco[sl][:, 0, :], ALU.mult, [qbt, "rco%d" % sl], ["bt1"])
                        self.tt("dve", t2[:], pb[qb][0:64, 256:384], rco[sl][:, 1, :], ALU.mult, [qbt, "rco%d" % sl], ["bt2"])
                        self.tt("dve", qpe[:, h, :], t1[:], t2[:], ALU.add, ["bt1", "bt2"], ["qpe"])
                        bank = 3 + qs
                        if QV < 3:
                            continue
                        for lc in range(2):
                            self.mm(pb[bank][:, lc * 128:(lc + 1) * 128], WUKT[:, 0, h * 256 + lc * 128:h * 256 + (lc + 1) * 128],
                                    qn[qs][:], True, True, ["qn%d" % qs, "WUKT"], ["b%d" % bank])
                        for lc in range(2):
                            self.ts("dve", qabs[:, lc, h, :], pb[bank][:, lc * 128:(lc + 1) * 128], gkv[:, lc:lc + 1],
                                    ALU.mult, ["b%d" % bank, "gains"], ["qabs"])

            def attention(j):
                    nkt = 2 * j + 2
                    for gi in range(2):
                        def scores(kt):
                            sbk = 5 + (kt % 3)
                            for lc in range(2):
                                self.mm(pb[sbk][:, 0:512], ckvT[:, lc, kt * 128:(kt + 1) * 128],
                                        qabs[:, lc, 4 * gi:4 * gi + 4, :], lc == 0, False,
                                        ["ckvT%d" % kt, "qabs"], ["b%d" % sbk])
                            self.mm(pb[sbk][:, 0:512], kpeT[:, kt * 128:(kt + 1) * 128], qpe[:, 4 * gi:4 * gi + 4, :],
                                    False, True, ["kpeT%d" % kt, "qpe"], ["b%d" % sbk])

                        scores(0)
                        scores(1)
                        for kt in range(nkt):
                            sbk = 5 + (kt % 3)
                            ps = kt % 3
                            self.act(PT[ps][:], pb[sbk][:, 0:512].rearrange("p (h n) -> p h n", h=4), AF.Exp,
                                     ["b%d" % sbk], ["PT%d" % ps], scale=SCALE)
                            if kt >= nkt - 2:
                                r = kt - (nkt - 2)
                                for hh in range(4):
                                    self.tt("dve", PT[ps][:, hh, :], PT[ps][:, hh, :], amb[:, r, :], ALU.mult,
                                            ["PT%d" % ps, "amb"], ["PT%d" % ps])
                            if kt + 2 < nkt:
                                scores(kt + 2)
                            for hh in range(4):
                                self.mm(pb[1 + hh][:, 0:258], PT[ps][:, hh, :], vaug[:, kt, 0:258], kt == 0, kt == nkt - 1,
                                        ["PT%d" % ps, "vaug%d" % kt, "vaug_ones"], ["b%d" % (1 + hh)])
                        for hh in range(4):
                            h = 4 * gi + hh
                            self.recip(rsum[:, h:h + 1], pb[1 + hh][:, 256:257], ["b%d" % (1 + hh)], ["rsum"])
                            self.ts("dve", olat[:, h, :], pb[1 + hh][:, 0:256], rsum[:, h:h + 1], ALU.mult,
                                    ["b%d" % (1 + hh), "rsum"], ["olat"])

            def unabsorb(j):
                    for half in range(2):
                        for hh in range(4):
                            h = 4 * half + hh
                            for lc in range(2):
                                self.tr(pb16[0][:, (hh * 2 + lc) * 128:(hh * 2 + lc + 1) * 128],
                                        olat[:, h, lc * 128:(lc + 1) * 128], ["olat"], ["b0"])
                        self.cp("act", olatT[:, 4 * half:4 * half + 4, :, :].rearrange("p h a n -> p (h a n)"),
                                pb16[0][:, 0:1024], ["b0"], ["olatT"])
                    for half in range(2):
                        bank = 5 + half
                        for hh in range(4):
                            h = 4 * half + hh
                            for lc in range(2):
                                self.mm(pb[bank][:, hh * 128:(hh + 1) * 128], WUV[:, lc, h * 128:(h + 1) * 128],
                                        olatT[:, h, lc, :], lc == 0, lc == 1, ["olatT", "WUV"], ["b%d" % bank])
                        self.cp("act", omT[:, 4 * half:4 * half + 4, :].rearrange("p h n -> p (h n)"), pb[bank][:, 0:512],
                                ["b%d" % bank], ["omT"])
                    gi_, ii = j // GT, j % GT
                    self.dma("sp", L["sc_om"][gi_, :, :, ii * 128:(ii + 1) * 128], omT[:], "omst", ["omT"], ["sc_om"])

            qprep(0)
            for j in range(NOWN):
                attention(j)
                if j + 1 < NOWN:
                    qprep(j + 1)
                unabsorb(j)

    def post_norm_res(self, banks, btoks, gbc, xres, xtok, outt, outtok, u):
        ss, lnv, rs = self.st_ss[u], self.st_ln[u], self.st_rs[u]
        for hf in range(2):
            self.act(self.junk[:, 0:512], banks[hf][:, 0:512], AF.Square, [btoks[hf]], ["junk", "pss%d" % u],
                     accum=ss[:, 2 + hf:3 + hf])
        self.tt("dve", ss[:, 4:5], ss[:, 2:3], ss[:, 3:4], ALU.add, ["pss%d" % u], ["pss2%d" % u])
        self.rstd_from_ss(ss[:, 4:5], lnv[:, 4:5], rs[:, 4:5], D, ["pss2%d" % u], "prs%d" % u)
        for hf in range(2):
            self.stt(self.ptmp[:, hf * 512:(hf + 1) * 512], banks[hf][:, 0:512], rs[:, 4:5],
                     gbc[:, hf * 512:(hf + 1) * 512], ALU.mult, ALU.mult, [btoks[hf], "prs%d" % u, "gbc"], ["ptmp"])
        self.tt("dve", outt, self.ptmp[:], xres, ALU.add, ["ptmp", xtok], [outtok])

    def pass_B2(self, es_outer, L):
        nc, S, NT, NOWN, NG = self.nc, self.S, self.NT, self.NOWN, self.NG
        pb, pb16 = self.pb, self.pb16
        with ExitStack() as es:
            WG = self.sb(es, "WG", [128, KD, 2048], BF16)
            WOG = self.sb(es, "WOG", [128, KD, 1024], BF16)
            WOM = self.sb(es, "WOM", [128, KD, 1024], BF16)
            WOUT = self.sb(es, "WOUT", [128, KD, 1024], BF16)
            gpm = self.sb(es, "gpm2", [128, 8], F32)
            ggl = self.sb(es, "ggl", [128, 8], F32)
            bg = self.sb(es, "bg", [128, 16], F32)
            gbc = self.sb(es, "gbc", [128, 1024], F32)
            self.dma("sp", gpm[:], L["g_pm"], "misc", [], ["gains"])
            self.dma("sp", ggl[:], L["g_gla"], "misc2", [], ["gains"])
            self.dma("sp", bg[:], L["b_g"], "misc3", [], ["bg"])
            self.dma("sp", gbc[:], L["g_pmix"][0:1, :].partition_broadcast(128), "misc4", [], ["gbc"])
            self.wstage = [self.sb(es, "wstgc%d" % i, [128, 2048], F32) for i in range(2)]
            self.load_w(WG, L["w_g"], KD, 2048, gpm, "WG")
            self.load_w(WOG, L["w_og"], KD, 1024, ggl, "WOG")
            self.load_w(WOM, L["w_om"], KD, 1024, None, "WOM")
            join_wout = self.load_w(WOUT, L["w_out"], KD, 1024, None, "WOUT", defer=True)
            xg = [self.sb(es, "xg%d" % i, [128, 1024], F32) for i in range(GT)]
            hTg = self.sb(es, "hTg", [128, KD, 512], BF16)
            ogg = self.sb(es, "ogg", [128, KD, 512], BF16)
            omg = self.sb(es, "omg", [128, KD, 512], BF16)
            ga = [self.sb(es, "ga%d" % i, [128, 512], F32) for i in range(2)]
            gb = [self.sb(es, "gb%d" % i, [128, 512], F32) for i in range(2)]
            m1 = [self.sb(es, "m1%d" % i, [128, 512], F32) for i in range(2)]
            m2 = [self.sb(es, "m2%d" % i, [128, 512], F32) for i in range(2)]
            mixT = self.sb(es, "mixT", [128, KD, 512], BF16)
            self.ptmp = self.sb(es, "ptmp", [128, 1024], F32)
            self.probe('B2')
            for g in range(NG):
                self.dma("sp", ogg[:], L["sc_og"][g], "ogg", ["sc_og"], ["ogg"])
                self.dma("sp", omg[:], L["sc_om"][g], "omg", ["sc_om"], ["omg"])
                items = []
                for i in range(GT):
                    j = g * GT + i
                    self.dma("sp", xg[i][:], L["xown"][j * 128:(j + 1) * 128, :], "xg%d" % i, [], ["xg%d" % i])
                    items.append((xg[i][:], "xg%d" % i, hTg[:, :, i * 128:(i + 1) * 128], "hTg", i))
                self.norm_T_multi(items)
                for fc in range(KD):
                    st = fc % 2
                    bs = (1, 2, 3, 4) if st == 0 else (5, 6, 7, 0)
                    srcs = ((WG, 0, hTg, "hTg", "WG"), (WG, 1024, hTg, "hTg", "WG"),
                            (WOG, 0, ogg, "ogg", "WOG"), (WOM, 0, omg, "omg", "WOM"))
                    for bi, (Wt, off, rhs, rtok, wtok) in enumerate(srcs):
                        for k in range(KD):
                            self.mm(pb[bs[bi]][:, 0:512], Wt[:, k, off + fc * 128:off + (fc + 1) * 128], rhs[:, k, :],
                                    k == 0, k == KD - 1, [rtok, wtok], ["b%d" % bs[bi]])
                    self.act(ga[st][:], pb[bs[0]][:, 0:512], AF.Sigmoid, ["b%d" % bs[0], "bg"], ["ga%d" % st],
                             bias=bg[:, fc:fc + 1])
                    self.act(gb[st][:], pb[bs[1]][:, 0:512], AF.Sigmoid, ["b%d" % bs[1], "bg"], ["gb%d" % st],
                             bias=bg[:, 8 + fc:9 + fc])
                    self.tt("dve", m1[st][:], ga[st][:], pb[bs[2]][:, 0:512], ALU.mult, ["ga%d" % st, "b%d" % bs[2]],
                            ["m1%d" % st])
                    self.tt("dve", m2[st][:], gb[st][:], pb[bs[3]][:, 0:512], ALU.mult, ["gb%d" % st, "b%d" % bs[3]],
                            ["m2%d" % st])
                    self.tt("dve", mixT[:, fc, :], m1[st][:], m2[st][:], ALU.add, ["m1%d" % st, "m2%d" % st], ["mixT"])
                if g == 0:
                    join_wout()
                for i in range(GT):
                    j = g * GT + i
                    bs = (1, 2) if i % 2 == 0 else (3, 4)
                    for hf in range(2):
                        for k in range(KD):
                            self.mm(pb[bs[hf]][:, 0:512], mixT[:, k, i * 128:(i + 1) * 128], WOUT[:, k, hf * 512:(hf + 1) * 512],
                                    k == 0, k == KD - 1, ["mixT", "WOUT"], ["b%d" % bs[hf]])
                    self.post_norm_res([pb[bs[0]], pb[bs[1]]], ["b%d" % bs[0], "b%d" % bs[1]], gbc, xg[i][:], "xg%d" % i,
                                       xg[i][:], "xg%d" % i, 2 + i % 2)
                    self.dma("sp", L["sc_x1"][j * 128:(j + 1) * 128, :], xg[i][:], "x1st%d" % i, ["xg%d" % i], ["sc_x1"])

    def pass_C(self, es_outer, L):
        nc, S, NT, NOWN, NG = self.nc, self.S, self.NT, self.NOWN, self.NG
        pb, pb16 = self.pb, self.pb16
        with ExitStack() as es:
            self.probe('Cstart')
            WFG = self.sb(es, "WFG", [128, KD, DFF], BF16)
            WFU = self.sb(es, "WFU", [128, KD, DFF], BF16)
            WFD = self.sb(es, "WFD", [128, KF, 1024], BF16)
            gpf = self.sb(es, "gpf", [128, 8], F32)
            gbc = self.sb(es, "gbc2", [128, 1024], F32)
            self.dma("sp", gpf[:], L["g_pf"], "misc", [], ["gains"])
            self.dma("sp", gbc[:], L["g_pffn"][0:1, :].partition_broadcast(128), "misc4", [], ["gbc"])
            with ExitStack() as ew:
                self.wstage = [self.sb(ew, "wstgd%d" % i, [128, 2048], F32) for i in range(2)]
                self.load_w(WFG, L["w_fg"], KD, DFF, gpf, "WFG")
                self.load_w(WFU, L["w_fu"], KD, DFF, gpf, "WFU")
                self.load_w(WFD, L["w_fd"], KF, 1024, None, "WFD")
            self.s.barrier()
            self.probe('C0')
            xg = [self.sb(es, "xc%d" % i, [128, 1024], F32) for i in range(GT)]
            hTg = self.sb(es, "hTc", [128, KD, 512], BF16)
            actT = self.sb(es, "actT", [128, KF, 512], BF16)
            sl = [self.sb(es, "sl%d" % i, [128, 512], F32) for i in range(2)]
            self.ptmp = self.sb(es, "ptmp2", [128, 1024], F32)
            self.probe('C')
            for g in range(NG):
                items = []
                for i in range(GT):
                    j = g * GT + i
                    self.dma("sp", xg[i][:], L["sc_x1"][j * 128:(j + 1) * 128, :], "xg%d" % i, ["sc_x1"], ["xg%d" % i])
                    items.append((xg[i][:], "xg%d" % i, hTg[:, :, i * 128:(i + 1) * 128], "hTg", i))
                self.norm_T_multi(items)
                for fc in range(KF):
                    st = fc % 2
                    bg_, bu_ = (1 + 2 * (fc % 3), 2 + 2 * (fc % 3))
                    for (Wt, bank, wtok) in ((WFG, bg_, "WFG"), (WFU, bu_, "WFU")):
                        for k in range(KD):
                            self.mm(pb[bank][:, 0:512], Wt[:, k, fc * 128:(fc + 1) * 128], hTg[:, k, :], k == 0, k == KD - 1,
                                    ["hTg", wtok], ["b%d" % bank])
                    self.act(sl[st][:], pb[bg_][:, 0:512], AF.Silu, ["b%d" % bg_], ["sl%d" % st])
                    self.tt("dve", actT[:, fc, :], sl[st][:], pb[bu_][:, 0:512], ALU.mult, ["sl%d" % st, "b%d" % bu_],
                            ["actT"])
                for i in range(GT):
                    j = g * GT + i
                    bs = (1, 2) if i % 2 == 0 else (3, 4)
                    for hf in range(2):
                        for k in range(KF):
                            self.mm(pb[bs[hf]][:, 0:512], actT[:, k, i * 128:(i + 1) * 128], WFD[:, k, hf * 512:(hf + 1) * 512],
                                    k == 0, k == KF - 1, ["actT", "WFD"], ["b%d" % bs[hf]])
                    self.post_norm_res([pb[bs[0]], pb[bs[1]]], ["b%d" % bs[0], "b%d" % bs[1]], gbc, xg[i][:], "xg%d" % i,
                                       xg[i][:], "xg%d" % i, 2 + i % 2)
                    self.dma("sp", L["out"][j * 128:(j + 1) * 128, :], xg[i][:], "outst%d" % i, ["xg%d" % i], ["out_dram"])


def _consts(parity):
    inv = (1.0 / (10000.0 ** (np.arange(0, 64, 2, dtype=np.float32) / np.float32(64)))).astype(np.float32)
    invf = np.concatenate([inv, inv]).reshape(64, 1).astype(np.float32)
    ident = np.eye(128, dtype=np.float32)
    s = np.arange(128)[:, None]
    t = np.arange(128)[None, :]
    umat = ((s // 64 == t // 64) & (s > t)).astype(np.float32)
    cind = (s // 64 == np.arange(2)[None, :]).astype(np.float32)
    qmask = np.zeros((128, 2), np.float32)
    qmask[:, parity] = 1.0
    diag = ((s // 64) <= (t // 64)).astype(np.float32)
    amask = np.zeros((128, 2, 128), np.float32)
    if parity == 0:
        amask[:, 0, :] = diag
    else:
        amask[:, 0, :] = 1.0
        amask[:, 1, :] = diag
    return dict(invf=invf, ident=ident, umat=umat, cind=cind, qmask=qmask,
                amask=np.ascontiguousarray(amask.reshape(128, 256)))


def _pk(v, n):
    return np.ascontiguousarray(v.reshape(n, 128).T).astype(np.float32)


def _weights(inp):
    w_in = inp["w_in"][0]
    sp = np.cumsum([0, 512, 512, 1024, 1024, 16, 384, 256, 64])
    q, k, v, g, ha, cq, ckv, kpe = [w_in[:, sp[i]:sp[i + 1]] for i in range(8)]
    kpesw = np.concatenate([kpe[:, 32:], kpe[:, :32]], axis=1)
    w_sh = np.ascontiguousarray(np.concatenate([k, v, ckv, kpe, kpesw, ha], axis=1))
    w_ow = np.ascontiguousarray(np.concatenate([q, g, cq], axis=1))
    w_uq = inp["w_uq"][0].reshape(384, 8, 192)
    nope, rope = w_uq[:, :, :128], w_uq[:, :, 128:]
    ropesw = np.concatenate([rope[:, :, 32:], rope[:, :, :32]], axis=2)
    w_uq2 = np.ascontiguousarray(np.concatenate([nope, rope, ropesw], axis=2).reshape(384, 2048))
    w_ukv = inp["w_ukv"][0].reshape(256, 8, 256)
    w_ukT = np.ascontiguousarray(w_ukv[:, :, :128].transpose(2, 1, 0).reshape(128, 2048))
    w_uv = np.ascontiguousarray(w_ukv[:, :, 128:].reshape(256, 1024))
    gla = inp["gla_norm"][0]
    return dict(
        w_sh=w_sh, w_ow=w_ow, g_pm=_pk(inp["pre_mix_norm"][0], 8),
        w_a2=np.ascontiguousarray(inp["w_a2"][0]), b_a2=np.ascontiguousarray(inp["b_a2"][0].reshape(1, 512)),
        g_gla=_pk(np.tile(gla, 4), 8), w_og=np.ascontiguousarray(inp["w_o_gla"][0]),
        g_q=_pk(inp["q_norm"][0], 3), w_uq=w_uq2, g_kv=_pk(inp["kv_norm"][0], 2),
        w_ukT=w_ukT, w_uv=w_uv, w_om=np.ascontiguousarray(inp["w_o_mla"][0]),
        w_g=np.ascontiguousarray(inp["w_gate"][0]), b_g=_pk(inp["b_gate"][0], 16),
        w_out=np.ascontiguousarray(inp["w_out"][0]),
        g_pmix=np.ascontiguousarray(inp["post_mix_norm"][0].reshape(1, D)),
        g_pf=_pk(inp["pre_ffn_norm"][0], 8),
        w_fg=np.ascontiguousarray(inp["w_ffn_gate"][0]), w_fu=np.ascontiguousarray(inp["w_ffn_up"][0]),
        w_fd=np.ascontiguousarray(inp["w_ffn_down"][0]),
        g_pffn=np.ascontiguousarray(inp["post_ffn_norm"][0].reshape(1, D)),
    )


def make_in_maps(inp, S):
    x = np.asarray(inp["x"], dtype=np.float32)
    pos = np.asarray(inp["positions"], dtype=np.int32)
    W = _weights({k: np.asarray(v) for k, v in inp.items()})
    maps = []
    NT = S // 128
    for core in range(8):
        b, par = core // 2, core % 2
        xb = x[b, :S]
        xo = xb.reshape(NT // 2, 2, 128, D)[:, par].reshape(S // 2, D)
        pb = pos[b, :S]
        po = pb.reshape(NT // 2, 2, 128)[:, par].reshape(1, S // 2)
        m = dict(xall=np.ascontiguousarray(xb), xown=np.ascontiguousarray(xo),
                 posall=np.ascontiguousarray(pb.reshape(1, S)), posown=np.ascontiguousarray(po))
        m.update(_consts(par))
        m.update(W)
        maps.append(m)
    return maps


def run(inp, S, upto=99, dbg=False):
    kb = K(S, upto=upto, dbg=dbg)
    nc = kb.build()
    maps = make_in_maps(inp, S)
    maps = [{k: v for k, v in m.items() if k in kb.in_names} for m in maps]
    res = run_bass_kernel_spmd(nc, maps, core_ids=list(range(8)))
    B = 4
    NT = S // 128
    full = np.zeros((B, NT // 2, 2, 128, D), np.float32)
    for core in range(8):
        b, par = core // 2, core % 2
        full[b, :, par] = np.asarray(res.results[core]["out"]).reshape(NT // 2, 128, D)
    return full.reshape(B, S, D), res


def kernel(**inputs):
    o, _ = run(inputs, 8192)
    return o
```

```python
import math
from contextlib import ExitStack

import numpy as np
import concourse.bass as bass
import concourse.mybir as mybir
from concourse.bass_utils import run_bass_kernel_spmd

F32 = mybir.dt.float32
BF16 = mybir.dt.bfloat16
I32 = mybir.dt.int32
AF = mybir.ActivationFunctionType
ALU = mybir.AluOpType

D = 1024
KD = 8
DFF = 2816
KF = 22
EPS = 1e-6
NSH = 1936
NOW = 1920
GT = 4


PSUM_TOKENS = frozenset("b%d" % i for i in range(8))


class Op:
    __slots__ = ("eng", "fn", "deps", "signal", "sem", "count", "key", "idx")

    def __init__(self, eng, fn, key):
        self.eng, self.fn, self.key = eng, fn, key
        self.deps, self.signal, self.sem, self.count = [], False, None, 0


class Sched:
    ENGS = ("pe", "act", "dve", "pool", "sp")

    def __init__(self):
        self.ops = {e: [] for e in self.ENGS}
        self.last_w = {}
        self.readers = {}
        self.nops = 0

    def add(self, eng, fn, r=(), w=(), key=None):
        op = Op(eng, fn, key)
        if key is not None:
            op.signal = True
        op.idx = self.nops
        self.nops += 1
        raw = set()
        deps = set()
        for t in r:
            lw = self.last_w.get(t)
            if lw is not None:
                deps.add(lw)
                raw.add(lw)
            if t in PSUM_TOKENS:
                for rd in self.readers.get(t, {}).values():
                    if rd.eng != eng:
                        deps.add(rd)
        for t in w:
            lw = self.last_w.get(t)
            if lw is not None:
                deps.add(lw)
            for rd in self.readers.get(t, {}).values():
                deps.add(rd)
        for d in deps:
            if d is op:
                continue
            if d.key is None and key is None and d.eng == eng:
                if eng == "pe":
                    continue
            op.deps.append(d)
            d.signal = True
        for t in r:
            rk = eng if key is None else ("dma", op.idx)
            self.readers.setdefault(t, {})[rk] = op
        for t in w:
            self.last_w[t] = op
            self.readers[t] = {}
        self.ops[eng].append(op)
        return op

    def barrier(self):
        lasts = []
        for e in self.ENGS:
            last_eng = None
            last_dma = {}
            for op in self.ops[e]:
                if op.key is None:
                    last_eng = op
                else:
                    last_dma[op.key] = op
            if last_eng is not None:
                lasts.append(last_eng)
            lasts.extend(last_dma.values())
        for e in self.ENGS:
            op = Op(e, lambda eng: eng.nop(), None)
            op.idx = self.nops
            self.nops += 1
            for d in lasts:
                if d.key is None and d.eng == e:
                    continue
                op.deps.append(d)
                d.signal = True
            self.ops[e].append(op)

    def finalize(self, nc, es):
        self.sems = {}
        cnt = {}
        for e in self.ENGS:
            for op in self.ops[e]:
                if not op.signal:
                    continue
                if op.key is not None:
                    k = ("dma", op.key)
                    inc = 16
                else:
                    k = ("eng", e)
                    inc = 1
                if k not in self.sems:
                    self.sems[k] = es.enter_context(nc.semaphore("s_%s_%s" % (k[0], str(k[1]))))
                    cnt[k] = 0
                cnt[k] += inc
                op.sem = k
                op.count = cnt[k]

    def emit(self, eng_name, engine):
        waited = {}
        for op in self.ops[eng_name]:
            need = {}
            for d in op.deps:
                if need.get(d.sem, 0) < d.count:
                    need[d.sem] = d.count
            for k, v in need.items():
                if waited.get(k, 0) < v:
                    engine.wait_ge(self.sems[k], v)
                    waited[k] = v
            ins = op.fn(engine)
            if op.signal:
                ins.then_inc(self.sems[op.sem], 16 if op.key is not None else 1)


class K:
    def __init__(self, S, upto=99, dbg=False):
        self.S = S
        self.NT = S // 128
        self.NOWN = self.NT // 2
        self.NG = self.NOWN // GT
        assert self.NG * GT * 2 * 128 == S
        self.upto = upto
        self.dbg = dbg
        self.nc = bass.Bass("TRN2", target_bir_lowering=False)
        self.s = Sched()
        self.dkeys = {}

    def mm(self, out, lhsT, rhs, start, stop, r, w):
        return self.s.add("pe", lambda e: e.matmul(out, lhsT, rhs, start=start, stop=stop), r, w)

    def tr(self, out, in_, r, w):
        ident = self.identb[:]
        return self.s.add("pe", lambda e: e.transpose(out, in_, ident), r + ["ident"], w)

    def act(self, out, in_, func, r, w, scale=None, bias=None, accum=None):
        kw = {}
        if scale is not None:
            kw["scale"] = scale
        if bias is not None:
            kw["bias"] = bias
        if accum is not None:
            kw["accum_out"] = accum
        return self.s.add("act", lambda e: e.activation(out, in_, func, **kw), r, w)

    def ts(self, eng, out, in0, s1, op0, r, w, s2=None, op1=None):
        if op1 is None:
            return self.s.add(eng, lambda e: e.tensor_scalar(out, in0, s1, None, op0), r, w)
        return self.s.add(eng, lambda e: e.tensor_scalar(out, in0, s1, s2, op0, op1), r, w)

    def tt(self, eng, out, in0, in1, op, r, w):
        return self.s.add(eng, lambda e: e.tensor_tensor(out, in0, in1, op), r, w)

    def stt(self, out, in0, scalar, in1, op0, op1, r, w):
        return self.s.add("dve", lambda e: e.scalar_tensor_tensor(out, in0, scalar, in1, op0, op1), r, w)

    def cp(self, eng, out, in_, r, w):
        if eng == "act":
            return self.s.add("act", lambda e: e.activation(out, in_, AF.Copy), r, w)
        return self.s.add(eng, lambda e: e.tensor_copy(out, in_), r, w)

    def memset(self, eng, ap, val, w):
        return self.s.add(eng, lambda e: e.memset(ap, val), [], w)

    def recip(self, out, in_, r, w):
        return self.s.add("dve", lambda e: e.reciprocal(out, in_), r, w)

    def dma(self, q, out, in_, key, r, w):
        base = key.rstrip("0123456789")
        ch = "chain_" + base
        return self.s.add(q, lambda e: e.dma_start(out, in_), list(r) + [ch], list(w) + [ch], key=base)

    def sb(self, es, name, shape, dt):
        return es.enter_context(self.nc.sbuf_tensor(name, list(shape), dt))

    def din(self, name, shape, dt=F32):
        t = self.nc.dram_tensor(name, list(shape), dt, kind="ExternalInput").ap()
        self.in_names.append(name)
        return t

    def dscr(self, name, shape, dt):
        return self.nc.dram_tensor(name, list(shape), dt, kind="Internal").ap()

    def probe(self, tag):
        import os
        if not os.environ.get("SBPROBE"):
            return
        try:
            with self.nc.sbuf_tensor("probe_" + tag, [128, 100000], F32):
                pass
        except AssertionError as e:
            msg = str(e)
            i = msg.find("(base=")
            print("SBUF", tag, msg[i:i + 40])

    def rstd_from_ss(self, ss, lnv, rstd, n, rt, wt):
        self.act(lnv, ss, AF.Ln, rt, [wt + "_ln"], scale=1.0 / n, bias=self.epsc[:, 0:1])
        self.act(rstd, lnv, AF.Exp, [wt + "_ln"], [wt], scale=-0.5)

    def load_w(self, dst, src, nk, ncol, gain, tok, defer=False):
        engs = ("dve", "act")
        CB = self.wcb
        for k in range(nk):
            for c0 in range(0, ncol, CB):
                c1 = min(ncol, c0 + CB)
                sl = self.wslot % 2
                self.wslot += 1
                stg = self.wstage[sl]
                self.dma("sp", stg[:, 0:c1 - c0], src[k * 128:(k + 1) * 128, c0:c1], "wst" + "AB"[sl],
                         [], ["wst%d" % sl])
                eng = engs[self.wrot % 2]
                self.wrot += 1
                o = dst[:, k, c0:c1]
                i = stg[:, 0:c1 - c0]
                wt = [tok + "#" + eng]
                if gain is not None:
                    g = gain[:, k:k + 1]
                    if eng == "act":
                        self.act(o, i, AF.Copy, ["wst%d" % sl, "gains"], wt, scale=g)
                    else:
                        self.ts(eng, o, i, g, ALU.mult, ["wst%d" % sl, "gains"], wt)
                else:
                    self.cp(eng, o, i, ["wst%d" % sl], wt)

        def join():
            self.s.add("pe", lambda e: e.nop(), [tok + "#dve", tok + "#act"], [tok])
        if defer:
            return join
        join()
        return None

    def norm_T(self, xt, xtok, hT_ap, htok, u):
        self.norm_T_multi([(xt, xtok, hT_ap, htok, u)])

    def norm_T_multi(self, items):
        junk = self.junk
        for (xt, xtok, hT_ap, htok, u) in items:
            self.act(junk[:], xt, AF.Square, [xtok], ["junk", "ss%d" % u], accum=self.st_ss[u][:, 0:1])
        for (xt, xtok, hT_ap, htok, u) in items:
            self.act(self.st_ln[u][:, 0:1], self.st_ss[u][:, 0:1], AF.Ln, ["ss%d" % u], ["rs%d_ln" % u],
                     scale=1.0 / D, bias=self.epsc[:, 0:1])
        for (xt, xtok, hT_ap, htok, u) in items:
            self.act(self.st_rs[u][:, 0:1], self.st_ln[u][:, 0:1], AF.Exp, ["rs%d_ln" % u], ["rs%d" % u], scale=-0.5)
        b0 = self.pb16[0]
        for (xt, xtok, hT_ap, htok, u) in items:
            xu = u % len(self.xs)
            xs = self.xs[xu]
            self.ts("dve", xs[:], xt, self.st_rs[u][:, 0:1], ALU.mult, [xtok, "rs%d" % u], ["xs%d" % xu])
            for k in range(KD):
                self.tr(b0[:, k * 128:(k + 1) * 128], xs[:, k * 128:(k + 1) * 128], ["xs%d" % xu], ["b0"])
            self.cp("act", hT_ap, b0[:, 0:1024].rearrange("p (k n) -> p k n", k=KD), ["b0"], [htok])

    def build(self):
        nc, S, NT, NOWN, NG = self.nc, self.S, self.NT, self.NOWN, self.NG
        self.in_names = []
        SO = S // 2
        xall = self.din("xall", [S, D])
        xown = self.din("xown", [SO, D])
        posall = self.din("posall", [1, S], I32)
        posown = self.din("posown", [1, SO], I32)
        invf = self.din("invf", [64, 1])
        ident = self.din("ident", [128, 128])
        umat = self.din("umat", [128, 128])
        cind = self.din("cind", [128, 2])
        qmask = self.din("qmask", [128, 2])
        amask = self.din("amask", [128, 256])
        w_sh = self.din("w_sh", [D, NSH])
        w_ow = self.din("w_ow", [D, NOW])
        g_pm = self.din("g_pm", [128, 8])
        w_a2 = self.din("w_a2", [16, 512])
        b_a2 = self.din("b_a2", [1, 512])
        g_gla = self.din("g_gla", [128, 8])
        w_og = self.din("w_og", [D, D])
        g_q = self.din("g_q", [128, 3])
        w_uq = self.din("w_uq", [384, 2048])
        g_kv = self.din("g_kv", [128, 2])
        w_ukT = self.din("w_ukT", [128, 2048])
        w_uv = self.din("w_uv", [256, 1024])
        w_om = self.din("w_om", [D, D])
        w_g = self.din("w_g", [D, 2048])
        b_g = self.din("b_g", [128, 16])
        w_out = self.din("w_out", [D, D])
        g_pmix = self.din("g_pmix", [1, D])
        g_pf = self.din("g_pf", [128, 8])
        w_fg = self.din("w_fg", [D, DFF])
        w_fu = self.din("w_fu", [D, DFF])
        w_fd = self.din("w_fd", [DFF, D])
        g_pffn = self.din("g_pffn", [1, D])
        out = nc.dram_tensor("out", [SO, D], F32, kind="ExternalOutput").ap()

        rope_all = self.dscr("rope_all", [2, 64, S], F32)
        rope_own = self.dscr("rope_own", [2, 64, SO], F32)
        sc_og = self.dscr("sc_og", [NG, 128, 8, 512], BF16)
        sc_cq = self.dscr("sc_cq", [NOWN, 128, 3, 128], BF16)
        sc_om = self.dscr("sc_om", [NG, 128, 8, 512], BF16)
        sc_x1 = self.dscr("sc_x1", [SO, D], F32)
        self.dbg_out = None
        if self.dbg:
            self.dbg_out = nc.dram_tensor("dbg", [128, 4096], F32, kind="ExternalOutput").ap()

        with ExitStack() as es:
            self.identf = self.sb(es, "identf", [128, 128], F32)
            self.identb = self.sb(es, "identb", [128, 128], BF16)
            self.epsc = self.sb(es, "epsc", [128, 1], F32)
            self.onec = self.sb(es, "onec", [128, 1], F32)
            self.hpic = self.sb(es, "hpic", [128, 1], F32)
            self.junk = self.sb(es, "junk", [128, 1024], BF16)
            self.st_ss = [self.sb(es, "st_ss%d" % u, [128, 8], F32) for u in range(4)]
            self.st_ln = [self.sb(es, "st_ln%d" % u, [128, 8], F32) for u in range(4)]
            self.st_rs = [self.sb(es, "st_rs%d" % u, [128, 8], F32) for u in range(4)]
            self.xs = [self.sb(es, "xs%d" % u, [128, 1024], BF16) for u in range(2)]
            pbanks = [es.enter_context(nc.psum_tensor("pb%d" % i, [128, 512], F32)) for i in range(8)]
            self.pb = pbanks
            self.pb16 = [b[:].bitcast(BF16) for b in pbanks]
            self.wslot = 0
            self.wcb = 2048
            self.wrot = 0
            pb, pb16 = self.pb, self.pb16

            self.dma("sp", self.identf[:], ident, "misc", [], ["identf"])
            self.cp("dve", self.identb[:], self.identf[:], ["identf"], ["ident"])
            self.memset("dve", self.epsc[:], EPS, ["epsc"])
            self.memset("dve", self.onec[:], 1.0, ["onec"])
            self.memset("dve", self.hpic[:], math.pi / 2, ["hpic"])

            with ExitStack() as er:
                CH = min(2048, SO)
                invf_t = self.sb(er, "invf_t", [64, 1], F32)
                posi = self.sb(er, "posi", [64, CH], I32)
                ang = self.sb(er, "ang", [64, CH], F32)
                tq = self.sb(er, "tq", [64, CH], F32)
                rr = self.sb(er, "rr", [64, CH], F32)
                g1 = self.sb(er, "g1", [64, CH], F32)
                co = self.sb(er, "co", [64, CH], F32)
                si = self.sb(er, "si", [64, CH], F32)
                self.dma("sp", invf_t[:], invf, "misc2", [], ["invf"])
                MAGIC = 12582912.0
                C1 = 6.28125
                C2 = 2.0 * math.pi - 6.28125
                PI = math.pi
                for (pos_d, rope_d, n) in ((posall, rope_all, S), (posown, rope_own, SO)):
                    for c0 in range(0, n, CH):
                        self.dma("sp", posi[:], pos_d[0:1, c0:c0 + CH].partition_broadcast(64), "posi",
                                 [], ["posi"])
                        self.cp("dve", ang[:], posi[:], ["posi"], ["ang"])
                        self.ts("dve", ang[:], ang[:], invf_t[:, 0:1], ALU.mult, ["ang", "invf"], ["ang"])
                        self.ts("dve", tq[:], ang[:], 1.0 / (2 * PI), ALU.mult, ["ang"], ["tq"], s2=MAGIC, op1=ALU.add)
                        self.ts("dve", tq[:], tq[:], -MAGIC, ALU.add, ["tq"], ["tq"])
                        self.stt(rr[:], tq[:], -C1, ang[:], ALU.mult, ALU.add, ["tq", "ang"], ["rr"])
                        self.stt(rr[:], tq[:], -C2, rr[:], ALU.mult, ALU.add, ["tq", "rr"], ["rr"])
                        self.ts("dve", g1[:], rr[:], PI, ALU.is_gt, ["rr"], ["g1"], s2=-2 * PI, op1=ALU.mult)
                        self.tt("dve", rr[:], rr[:], g1[:], ALU.add, ["rr", "g1"], ["rr"])
                        self.ts("dve", g1[:], rr[:], -PI, ALU.is_lt, ["rr"], ["g1"], s2=2 * PI, op1=ALU.mult)
                        self.tt("dve", rr[:], rr[:], g1[:], ALU.add, ["rr", "g1"], ["rr"])
                        self.ts("dve", rr[:], rr[:], -3.1415925, ALU.max, ["rr"], ["rr"], s2=3.1415925, op1=ALU.min)
                        self.act(si[0:32, :], rr[0:32, :], AF.Sin, ["rr"], ["si"], scale=-1.0)
                        self.act(si[32:64, :], rr[32:64, :], AF.Sin, ["rr"], ["si"])
                        self.act(g1[:], rr[:], AF.Abs, ["rr"], ["g1"])
                        self.act(co[:], g1[:], AF.Sin, ["g1"], ["co"], scale=-1.0, bias=self.hpic[0:64, 0:1])
                        self.dma("sp", rope_d[0, :, c0:c0 + CH], co[:], "ropest0", ["co"], ["rope_dram"])
                        self.dma("sp", rope_d[1, :, c0:c0 + CH], si[:], "ropest1", ["si"], ["rope_dram"])

            self.s.barrier()
            if self.upto >= 1:
                self.pass_A(es, locals())
            self.finish(es, out)
        return nc

    def finish(self, es, out):
        nc = self.nc
        self.s.add("sp", lambda e: e.nop(), ["out_dram", "dbg_dram"], [])
        self.s.finalize(nc, es)
        with nc.Block() as block:
            @block.sync
            def _(e):
                self.s.emit("sp", e)

            @block.tensor
            def _(e):
                self.s.emit("pe", e)

            @block.scalar
            def _(e):
                self.s.emit("act", e)

            @block.vector
            def _(e):
                self.s.emit("dve", e)

            @block.gpsimd
            def _(e):
                self.s.emit("pool", e)

    def pass_A(self, es_outer, L):
        nc, S, NT, NOWN, NG = self.nc, self.S, self.NT, self.NOWN, self.NG
        pb, pb16 = self.pb, self.pb16
        xall, xown = L["xall"], L["xown"]
        with ExitStack() as es:
            vaug = self.sb(es, "vaug", [128, NT, 258], BF16)
            kpeT = self.sb(es, "kpeT", [64, S], BF16)
            self.vaug, self.kpeT = vaug, kpeT
            self.memset("pool", vaug[:, :, 256:258], 1.0, ["vaug_ones"])
            with ExitStack() as ea:
                WS = self.sb(ea, "WS", [128, KD, NSH], BF16)
                WO = self.sb(ea, "WO", [128, KD, NOW], BF16)
                gpm = self.sb(ea, "gpm", [128, 8], F32)
                wa2a = self.sb(ea, "wa2a", [32, 512], BF16)
                umb = self.sb(ea, "umb", [128, 128], BF16)
                cif = self.sb(ea, "cif", [128, 2], F32)
                cib = self.sb(ea, "cib", [128, 2], BF16)
                qmf = self.sb(ea, "qmf", [128, 2], F32)
                ew = ExitStack()
                wa2f = self.sb(ew, "wa2f", [32, 512], F32)
                umf = self.sb(ew, "umf", [128, 128], F32)
                self.dma("sp", gpm[:], L["g_pm"], "misc", [], ["gains"])
                self.memset("dve", wa2f[:], 0.0, ["wa2f"])
                self.dma("sp", wa2f[0:16, :], L["w_a2"], "misc2", ["wa2f"], ["wa2f"])
                self.dma("sp", wa2f[16:17, :], L["b_a2"], "misc3", ["wa2f"], ["wa2f"])
                self.cp("dve", wa2a[:], wa2f[:], ["wa2f"], ["wa2a"])
                self.dma("sp", umf[:], L["umat"], "misc4", [], ["umf"])
                self.cp("dve", umb[:], umf[:], ["umf"], ["umb"])
                self.dma("sp", cif[:], L["cind"], "misc5", [], ["cif"])
                self.cp("dve", cib[:], cif[:], ["cif"], ["cib"])
                self.dma("sp", qmf[:], L["qmask"], "misc6", [], ["qmf"])
                self.ts("dve", qmf[:], qmf[:], 128.0 ** -0.5, ALU.mult, ["qmf"], ["qmf"])
                with ew:
                    self.wstage = [self.sb(ew, "wstg%d" % i, [128, 2048], F32) for i in range(2)]
                    self.load_w(WS, L["w_sh"], KD, NSH, gpm, "WS")
                    self.load_w(WO, L["w_ow"], KD, NOW, gpm, "WO")
                self.s.barrier()
                xin1 = [self.sb(ea, "xin_%d" % i, [128, 1024], F32) for i in range(3)]
                xin = [xin1, xin1]
                hTa = self.sb(ea, "hTa", [128, KD, 128], BF16)
                hT = [[hTa] + [self.sb(ea, "hT%d_%d" % (sl, i), [128, KD, 128], BF16) for i in (1, 2)] for sl in range(2)]
                hTn = [["hTa", "hT%d_1" % sl, "hT%d_2" % sl] for sl in range(2)]
                haT = [self.sb(ea, "haT%d" % i, [32, 128], BF16) for i in range(2)]
                ksb = [self.sb(ea, "ksb%d" % i, [128, 512], F32) for i in range(2)]
                e1s = self.sb(ea, "e1s", [128, 512], F32)
                e1 = [e1s, e1s]
                nl = [self.sb(ea, "nl%d" % i, [128, 512], BF16) for i in range(2)]
                dfac = [self.sb(ea, "dfac%d" % i, [128, 512], F32) for i in range(2)]
                dc = [self.sb(ea, "dc%d" % i, [128, 8], F32) for i in range(2)]
                kdec = [self.sb(ea, "kdec%d" % i, [128, 512], BF16) for i in range(2)]
                vb = [self.sb(ea, "vb%d" % i, [128, 1024], BF16) for i in range(2)]
                stc = [[self.sb(ea, "stc%d_%d" % (i, q), [128, 4], F32) for q in range(3)] for i in range(3)]
                Sst = self.sb(ea, "Sst", [128, 4, 256], F32)
                Sbf = self.sb(ea, "Sbf", [128, 4, 4, 256], BF16)
                qpad = self.sb(ea, "qpad", [128, 4, 4, 128], BF16)
                eg = self.sb(ea, "eg", [128, 1024], F32)
                sg = self.sb(ea, "sg", [128, 1024], F32)
                osb = eg
                og = self.sb(ea, "og", [128, 1024], BF16)
                ogT = self.sb(ea, "ogT", [128, KD, 128], BF16)
                cqn = self.sb(ea, "cqn", [128, 384], BF16)
                cqnT = self.sb(ea, "cqnT", [128, 3, 128], BF16)
                rcs1 = self.sb(ea, "rcs", [64, 2, 128], F32)
                t11 = self.sb(ea, "t1", [64, 128], F32)
                t21 = self.sb(ea, "t2", [64, 128], F32)
                rcs, t1, t2 = [rcs1, rcs1], [t11, t11], [t21, t21]
                sso = self.sb(ea, "sso", [128, 4], F32)
                lno = self.sb(ea, "lno", [128, 4], F32)
                rso = self.sb(ea, "rso", [128, 4], F32)
                for i in range(2):
                    self.memset("dve", haT[i][:], 1.0, ["haT%d" % i])
                self.memset("dve", Sst[:], 0.0, ["Sst"])
                self.memset("pool", qpad[:], 0.0, ["qpad"])

                rope_all = L["rope_all"]
                NP = NT // 2

                def S1(j):
                    sl = j % 2
                    items = []
                    for i in range(3):
                        src = xall[(2 * j + i) * 128:(2 * j + i + 1) * 128, :] if i < 2 else xown[j * 128:(j + 1) * 128, :]
                        tk = "xin_%d" % i
                        self.dma("sp", xin[sl][i][:], src, "xin", [], [tk])
                        items.append((xin[sl][i][:], tk, hT[sl][i][:], hTn[sl][i], i))
                    self.norm_T_multi(items)

                def proj(j, ab):
                    sl = j % 2
                    t = 2 * j + ab
                    h, hk = hT[sl][ab], hTn[sl][ab]
                    cb = 5 + ab
                    for k in range(KD):
                        self.mm(pb[cb][0:16, 0:128], WS[:, k, 1920:1936], h[:, k, :], k == 0, k == KD - 1,
                                [hk, "WS"], ["b%d" % cb])
                    self.cp("dve", haT[ab][0:16, :], pb[cb][0:16, 0:128], ["b%d" % cb], ["haT%d" % ab])
                    for (bank, c0, c1) in ((1, 0, 512), (2, 512, 1024), (3, 1024, 1536)):
                        for k in range(KD):
                            self.mm(pb[bank][:, 0:512], h[:, k, :], WS[:, k, c0:c1], k == 0, k == KD - 1,
                                    [hk, "WS"], ["b%d" % bank])
                        if bank == 1:
                            self.cp("act", ksb[ab][:], pb[1][:, 0:512], ["b1"], ["ksb%d" % ab])
                            self.mm(pb[cb][:, 0:512], haT[ab][:, :], wa2a[:, :], True, True, ["haT%d" % ab, "wa2a"],
                                    ["b%d" % cb])
                            self.act(e1[ab][:], pb[cb][:, 0:512], AF.Exp, ["b%d" % cb], ["e1s"], scale=-1.0)
                            self.act(nl[ab][:], e1[ab][:], AF.Ln, ["e1s"], ["nl%d" % ab], bias=self.onec[:, 0:1])
                        elif bank == 2:
                            self.cp("dve", vb[ab][:, 0:512], pb[2][:, 0:512], ["b2"], ["vb%d" % ab])
                        else:
                            self.cp("act", vb[ab][:, 512:1024], pb[3][:, 0:512], ["b3"], ["vb%d" % ab])
                    for k in range(KD):
                        self.mm(pb[4][:, 0:256], h[:, k, :], WS[:, k, 1536:1792], k == 0, k == KD - 1,
                                [hk, "WS"], ["b4"])
                    for (o0, c0) in ((256, 1792), (384, 1856)):
                        for k in range(KD):
                            self.mm(pb[4][0:64, o0:o0 + 128], WS[:, k, c0:c0 + 64], h[:, k, :], k == 0, k == KD - 1,
                                    [hk, "WS"], ["b4"])
                    ssc, lnc, rsc = stc[ab]
                    self.act(self.junk[:, 0:256], pb[4][:, 0:256], AF.Square, ["b4"], ["junk", "ssc%d" % ab],
                             accum=ssc[:, 0:1])
                    self.rstd_from_ss(ssc[:, 0:1], lnc[:, 0:1], rsc[:, 0:1], 256, ["ssc%d" % ab], "rsc%d" % ab)
                    self.ts("dve", vaug[:, t, 0:256], pb[4][:, 0:256], rsc[:, 0:1], ALU.mult, ["b4", "rsc%d" % ab],
                            ["vaug%d" % t])
                    self.dma("sp", rcs[ab][:, :, :], rope_all[:, :, t * 128:(t + 1) * 128].rearrange("a p n -> p a n"),
                             "rcs", ["rope_dram"], ["rcs_"])
                    self.tt("dve", t1[ab][:], pb[4][0:64, 256:384], rcs[ab][:, 0, :], ALU.mult, ["b4", "rcs_"],
                            ["t1_"])
                    self.tt("dve", t2[ab][:], pb[4][0:64, 384:512], rcs[ab][:, 1, :], ALU.mult, ["b4", "rcs_"],
                            ["t2_"])
                    self.tt("dve", kpeT[:, t * 128:(t + 1) * 128], t1[ab][:], t2[ab][:], ALU.add,
                            ["t1_", "t2_"], ["kpeT%d" % t])

                def chain_a(ab):
                    cb = 5 + ab
                    self.mm(pb[cb][:, 0:512], umb[:, :], nl[ab][:, :], True, True, ["umb", "nl%d" % ab], ["b%d" % cb])
                    self.act(dfac[ab][:], pb[cb][:, 0:512], AF.Exp, ["b%d" % cb], ["dfac%d" % ab], scale=-1.0 / 16.0)

                def chain_b(ab):
                    cb = 5 + ab
                    for hh in range(4):
                        self.mm(pb[cb][:, 2 * hh:2 * hh + 2], nl[ab][:, hh * 128:(hh + 1) * 128], cib[:, :], True, True,
                                ["nl%d" % ab, "cib"], ["b%d" % cb])
                    self.act(dc[ab][:], pb[cb][:, 0:8], AF.Exp, ["b%d" % cb], ["dc%d" % ab], scale=-1.0 / 16.0)
                    self.tt("dve", kdec[ab][:], ksb[ab][:], dfac[ab][:], ALU.mult, ["ksb%d" % ab, "dfac%d" % ab],
                            ["kdec%d" % ab])

                def state(ab):
                    for c in range(2):
                        pc = 2 * ab + c
                        for half in range(2):
                            bank = 1 + 2 * c + half
                            for q in range(2):
                                hh = 2 * half + q
                                self.mm(pb[bank][:, q * 256:q * 256 + 256],
                                        kdec[ab][c * 64:(c + 1) * 64, hh * 128:(hh + 1) * 128],
                                        vb[ab][c * 64:(c + 1) * 64, hh * 256:(hh + 1) * 256], True, True,
                                        ["kdec%d" % ab, "vb%d" % ab], ["b%d" % bank])
                        for half in range(2):
                            bank = 1 + 2 * c + half
                            for q in range(2):
                                hh = 2 * half + q
                                self.stt(Sst[:, hh, :], Sst[:, hh, :], dc[ab][:, 2 * hh + c:2 * hh + c + 1],
                                         pb[bank][:, q * 256:q * 256 + 256], ALU.mult, ALU.add,
                                         ["Sst", "dc%d" % ab, "b%d" % bank], ["Sst"])
                        self.cp("act", Sbf[:, pc, :, :], Sst[:, :, :], ["Sst"], ["Sbf%d" % pc])

                def own_proj(j):
                    sl = j % 2
                    h, hk = hT[sl][2], hTn[sl][2]
                    for hh in range(4):
                        for k in range(KD):
                            self.mm(pb[1][:, hh * 128:(hh + 1) * 128], WO[:, k, hh * 128:(hh + 1) * 128], h[:, k, :],
                                    k == 0, k == KD - 1, [hk, "WO"], ["b1"])
                    qv = pb[1][:, 0:512].rearrange("p (h n) -> p h n", h=4)
                    for m in range(2):
                        for cc in range(2):
                            c = 2 * m + cc
                            self.ts("dve", qpad[:, c, :, cc * 64:(cc + 1) * 64], qv[:, :, cc * 64:(cc + 1) * 64],
                                    qmf[:, m:m + 1], ALU.mult, ["b1", "qmf"], ["qpad"])
                    for (bank, c0) in ((2, 512), (3, 1024)):
                        for k in range(KD):
                            self.mm(pb[bank][:, 0:512], h[:, k, :], WO[:, k, c0:c0 + 512], k == 0, k == KD - 1,
                                    [hk, "WO"], ["b%d" % bank])
                        o0 = c0 - 512
                        self.act(eg[:, o0:o0 + 512], pb[bank][:, 0:512], AF.Exp, ["b%d" % bank], ["eg%d" % bank], scale=-1.0)
                        self.ts("dve", eg[:, o0:o0 + 512], eg[:, o0:o0 + 512], 1.0, ALU.add, ["eg%d" % bank], ["eg%d" % bank])
                        self.recip(eg[:, o0:o0 + 512], eg[:, o0:o0 + 512], ["eg%d" % bank], ["eg%d" % bank])
                        self.tt("dve", sg[:, o0:o0 + 512], eg[:, o0:o0 + 512], pb[bank][:, 0:512], ALU.mult,
                                ["eg%d" % bank, "b%d" % bank], ["sg%d" % bank])
                    for k in range(KD):
                        self.mm(pb[4][:, 0:384], h[:, k, :], WO[:, k, 1536:1920], k == 0, k == KD - 1,
                                [hk, "WO"], ["b4"])
                    ssq, lnq, rsq = stc[2]
                    self.act(self.junk[:, 0:384], pb[4][:, 0:384], AF.Square, ["b4"], ["junk", "ssq"], accum=ssq[:, 0:1])
                    self.rstd_from_ss(ssq[:, 0:1], lnq[:, 0:1], rsq[:, 0:1], 384, ["ssq"], "rsq")
                    self.ts("dve", cqn[:], pb[4][:, 0:384], rsq[:, 0:1], ALU.mult, ["b4", "rsq"], ["cqn"])

                def own_out(j):
                    for hh in range(4):
                        bank = 5 + hh // 2
                        for c in range(4):
                            self.mm(pb[bank][:, (hh % 2) * 256:(hh % 2) * 256 + 256], qpad[:, c, hh, :],
                                    Sbf[:, c, hh, :], c == 0, c == 3, ["qpad", "Sbf%d" % c], ["b%d" % bank])
                    for hf in range(2):
                        self.cp("act", osb[:, hf * 512:(hf + 1) * 512], pb[5 + hf][:, 0:512], ["b%d" % (5 + hf)],
                                ["eg%d" % (2 + hf)])
                    for hh in range(4):
                        self.act(self.junk[:, 0:256], osb[:, hh * 256:(hh + 1) * 256], AF.Square,
                                 ["eg%d" % (2 + hh // 2)], ["junk", "sso"], accum=sso[:, hh:hh + 1])
                    self.act(lno[:], sso[:], AF.Ln, ["sso"], ["lno"], scale=1.0 / 256, bias=self.epsc[:, 0:1])
                    self.act(rso[:], lno[:], AF.Exp, ["lno"], ["rso"], scale=-0.5)
                    for hh in range(4):
                        self.stt(og[:, hh * 256:(hh + 1) * 256], osb[:, hh * 256:(hh + 1) * 256],
                                 rso[:, hh:hh + 1], sg[:, hh * 256:(hh + 1) * 256], ALU.mult, ALU.mult,
                                 ["eg%d" % (2 + hh // 2), "rso", "sg%d" % (2 + hh // 2)], ["og"])
                    for k in range(KD):
                        self.tr(pb16[0][:, k * 128:(k + 1) * 128], og[:, k * 128:(k + 1) * 128], ["og"], ["b0"])
                    self.cp("act", ogT[:], pb16[0][:, 0:1024].rearrange("p (k n) -> p k n", k=KD), ["b0"], ["ogT"])
                    gi, ii = j // GT, j % GT
                    self.dma("sp", L["sc_og"][gi, :, :, ii * 128:(ii + 1) * 128], ogT[:], "ogst", ["ogT"], ["sc_og"])
                    for k in range(3):
                        self.tr(pb16[0][:, k * 128:(k + 1) * 128], cqn[:, k * 128:(k + 1) * 128], ["cqn"], ["b0"])
                    self.cp("dve", cqnT[:], pb16[0][:, 0:384].rearrange("p (k n) -> p k n", k=3), ["b0"], ["cqnT"])
                    self.dma("sp", L["sc_cq"][j], cqnT[:], "cqst", ["cqnT"], ["sc_cq"])

                self.probe('A')
                S1(0)
                for j in range(NP):
                    proj(j, 0)
                    chain_a(0)
                    if j + 1 < NP:
                        S1(j + 1)
                    chain_b(0)
                    proj(j, 1)
                    chain_a(1)
                    own_proj(j)
                    chain_b(1)
                    state(0)
                    state(1)
                    own_out(j)
            self.s.barrier()
            if self.upto >= 2:
                self.pass_B1(es, L)
                self.s.barrier()
        if self.upto >= 3:
            self.pass_B2(es_outer, L)
            self.s.barrier()
        if self.upto >= 4:
            self.pass_C(es_outer, L)

    def dbg_dump_A(self, L):
        pass

    def pass_B1(self, es_outer, L):
        nc, S, NT, NOWN, NG = self.nc, self.S, self.NT, self.NOWN, self.NG
        pb, pb16 = self.pb, self.pb16
        vaug, kpeT = self.vaug, self.kpeT
        SCALE = 192.0 ** -0.5
        with ExitStack() as es:
            ckvT = self.sb(es, "ckvT", [128, 2, S], BF16)
            WUQ = self.sb(es, "WUQ", [128, 3, 2048], BF16)
            WUKT = self.sb(es, "WUKT", [128, 1, 2048], BF16)
            WUV = self.sb(es, "WUV", [128, 2, 1024], BF16)
            gq = self.sb(es, "gq", [128, 3], F32)
            gkv = self.sb(es, "gkv", [128, 2], F32)
            amf = self.sb(es, "amf", [128, 256], F32)
            amb = self.sb(es, "amb", [128, 2, 128], BF16)
            self.dma("sp", gq[:], L["g_q"], "misc", [], ["gains"])
            self.dma("sp", gkv[:], L["g_kv"], "misc2", [], ["gains"])
            self.dma("sp", amf[:], L["amask"], "misc3", [], ["amf"])
            self.cp("dve", amb[:].rearrange("p a n -> p (a n)"), amf[:], ["amf"], ["amb"])
            with ExitStack() as ew:
                self.wstage = [self.sb(ew, "wstgb%d" % i, [128, 2048], F32) for i in range(2)]
                self.load_w(WUQ, L["w_uq"], 3, 2048, gq, "WUQ")
                self.load_w(WUKT, L["w_ukT"], 1, 2048, None, "WUKT")
                self.load_w(WUV, L["w_uv"], 2, 1024, gkv, "WUV")
            self.s.barrier()
            for t in range(NT):
                for lc in range(2):
                    self.tr(pb16[0][:, lc * 128:(lc + 1) * 128], vaug[:, t, lc * 128:(lc + 1) * 128],
                            ["vaug%d" % t], ["b0"])
                self.cp("act" if t % 2 else "dve", ckvT[:, :, t * 128:(t + 1) * 128],
                        pb16[0][:, 0:256].rearrange("p (a n) -> p a n", a=2), ["b0"], ["ckvT%d" % t])
            import os
            B1STOP = int(os.environ.get("B1STOP", "9"))
            if B1STOP < 1:
                return
            cqT = [self.sb(es, "cqT%d" % i, [128, 3, 128], BF16) for i in range(2)]
            rco = [self.sb(es, "rco%d" % i, [64, 2, 128], F32) for i in range(2)]
            qn = [self.sb(es, "qn%d" % i, [128, 128], BF16) for i in range(2)]
            qpe = self.sb(es, "qpe", [64, 8, 128], BF16)
            qabs = self.sb(es, "qabs", [128, 2, 8, 128], BF16)
            t1 = self.sb(es, "bt1", [64, 128], F32)
            t2 = self.sb(es, "bt2", [64, 128], F32)
            PT = [self.sb(es, "PT%d" % i, [128, 4, 128], BF16) for i in range(3)]
            rsum = self.sb(es, "rsum", [128, 8], F32)
            olat = self.sb(es, "olat", [128, 8, 256], BF16)
            olatT = self.sb(es, "olatT", [128, 8, 2, 128], BF16)
            omT = self.sb(es, "omT", [128, 8, 128], BF16)
            self.probe('B1')
            def qprep(j):
                    sl = j % 2
                    XV = int(os.environ.get("XV", "3"))
                    if XV & 1:
                        self.dma("sp", cqT[sl][:], L["sc_cq"][j], "cqT%d" % sl, ["sc_cq"], ["cqT%d" % sl])
                    if XV & 2:
                        self.dma("sp", rco[sl][:], L["rope_own"][:, :, j * 128:(j + 1) * 128].rearrange("a p n -> p a n"),
                             "rco%d" % sl, ["rope_dram"], ["rco%d" % sl])
                    cq = cqT[sl]
                    QV = int(os.environ.get("QV", "9"))
                    for h in range(8):
                        if QV < 2:
                            continue
                        qs = h % 2
                        qb = 1 + qs
                        qbt = "b%d" % qb
                        for k in range(3):
                            self.mm(pb[qb][:, 0:128], WUQ[:, k, h * 256:h * 256 + 128], cq[:, k, :], k == 0, k == 2,
                                    ["cqT%d" % sl, "WUQ"], [qbt])
                        for (o0, c0) in ((128, 128), (256, 192)):
                            for k in range(3):
                                self.mm(pb[qb][0:64, o0:o0 + 128], WUQ[:, k, h * 256 + c0:h * 256 + c0 + 64], cq[:, k, :],
                                        k == 0, k == 2, ["cqT%d" % sl, "WUQ"], [qbt])
                        YV = int(os.environ.get("YV", "9"))
                        if YV < 1:
                            continue
                        self.cp("act", qn[qs][:], pb[qb][:, 0:128], [qbt], ["qn%d" % qs])
                        if YV < 2:
                            continue
                        self.tt("dve", t1[:], pb[qb][0:64, 128:256], rco[sl][:, 0, :], ALU.mult, [qbt, "rco%d" % sl], ["bt1"])
                        self.tt("dve", t2[:], pb[qb][0:64, 256:384], rco[sl][:, 1, :], ALU.mult, [qbt, "rco%d" % sl], ["bt2"])
                        self.tt("dve", qpe[:, h, :], t1[:], t2[:], ALU.add, ["bt1", "bt2"], ["qpe"])
                        bank = 3 + qs
                        if QV < 3:
                            continue
                        for lc in range(2):
                            self.mm(pb[bank][:, lc * 128:(lc + 1) * 128], WUKT[:, 0, h * 256 + lc * 128:h * 256 + (lc + 1) * 128],
                                    qn[qs][:], True, True, ["qn%d" % qs, "WUKT"], ["b%d" % bank])
                        for lc in range(2):
                            self.ts("dve", qabs[:, lc, h, :], pb[bank][:, lc * 128:(lc + 1) * 128], gkv[:, lc:lc + 1],
                                    ALU.mult, ["b%d" % bank, "gains"], ["qabs"])

            def attention(j):
                    nkt = 2 * j + 2
                    for gi in range(2):
                        def scores(kt):
                            sbk = 5 + (kt % 3)
                            for lc in range(2):
                                self.mm(pb[sbk][:, 0:512], ckvT[:, lc, kt * 128:(kt + 1) * 128],
                                        qabs[:, lc, 4 * gi:4 * gi + 4, :], lc == 0, False,
                                        ["ckvT%d" % kt, "qabs"], ["b%d" % sbk])
                            self.mm(pb[sbk][:, 0:512], kpeT[:, kt * 128:(kt + 1) * 128], qpe[:, 4 * gi:4 * gi + 4, :],
                                    False, True, ["kpeT%d" % kt, "qpe"], ["b%d" % sbk])

                        scores(0)
                        scores(1)
                        for kt in range(nkt):
                            sbk = 5 + (kt % 3)
                            ps = kt % 3
                            self.act(PT[ps][:], pb[sbk][:, 0:512].rearrange("p (h n) -> p h n", h=4), AF.Exp,
                                     ["b%d" % sbk], ["PT%d" % ps], scale=SCALE)
                            if kt >= nkt - 2:
                                r = kt - (nkt - 2)
                                for hh in range(4):
                                    self.tt("dve", PT[ps][:, hh, :], PT[ps][:, hh, :], amb[:, r, :], ALU.mult,
                                            ["PT%d" % ps, "amb"], ["PT%d" % ps])
                            if kt + 2 < nkt:
                                scores(kt + 2)
                            for hh in range(4):
                                self.mm(pb[1 + hh][:, 0:258], PT[ps][:, hh, :], vaug[:, kt, 0:258], kt == 0, kt == nkt - 1,
                                        ["PT%d" % ps, "vaug%d" % kt, "vaug_ones"], ["b%d" % (1 + hh)])
                        for hh in range(4):
                            h = 4 * gi + hh
                            self.recip(rsum[:, h:h + 1], pb[1 + hh][:, 256:257], ["b%d" % (1 + hh)], ["rsum"])
                            self.ts("dve", olat[:, h, :], pb[1 + hh][:, 0:256], rsum[:, h:h + 1], ALU.mult,
                                    ["b%d" % (1 + hh), "rsum"], ["olat"])

            def unabsorb(j):
                    for half in range(2):
                        for hh in range(4):
                            h = 4 * half + hh
                            for lc in range(2):
                                self.tr(pb16[0][:, (hh * 2 + lc) * 128:(hh * 2 + lc + 1) * 128],
                                        olat[:, h, lc * 128:(lc + 1) * 128], ["olat"], ["b0"])
                        self.cp("act", olatT[:, 4 * half:4 * half + 4, :, :].rearrange("p h a n -> p (h a n)"),
                                pb16[0][:, 0:1024], ["b0"], ["olatT"])
                    for half in range(2):
                        bank = 5 + half
                        for hh in range(4):
                            h = 4 * half + hh
                            for lc in range(2):
                                self.mm(pb[bank][:, hh * 128:(hh + 1) * 128], WUV[:, lc, h * 128:(h + 1) * 128],
                                        olatT[:, h, lc, :], lc == 0, lc == 1, ["olatT", "WUV"], ["b%d" % bank])
                        self.cp("act", omT[:, 4 * half:4 * half + 4, :].rearrange("p h n -> p (h n)"), pb[bank][:, 0:512],
                                ["b%d" % bank], ["omT"])
                    gi_, ii = j // GT, j % GT
                    self.dma("sp", L["sc_om"][gi_, :, :, ii * 128:(ii + 1) * 128], omT[:], "omst", ["omT"], ["sc_om"])

            qprep(0)
            for j in range(NOWN):
                attention(j)
                if j + 1 < NOWN:
                    qprep(j + 1)
                unabsorb(j)

    def post_norm_res(self, banks, btoks, gbc, xres, xtok, outt, outtok, u):
        ss, lnv, rs = self.st_ss[u], self.st_ln[u], self.st_rs[u]
        for hf in range(2):
            self.act(self.junk[:, 0:512], banks[hf][:, 0:512], AF.Square, [btoks[hf]], ["junk", "pss%d" % u],
                     accum=ss[:, 2 + hf:3 + hf])
        self.tt("dve", ss[:, 4:5], ss[:, 2:3], ss[:, 3:4], ALU.add, ["pss%d" % u], ["pss2%d" % u])
        self.rstd_from_ss(ss[:, 4:5], lnv[:, 4:5], rs[:, 4:5], D, ["pss2%d" % u], "prs%d" % u)
        for hf in range(2):
            self.stt(self.ptmp[:, hf * 512:(hf + 1) * 512], banks[hf][:, 0:512], rs[:, 4:5],
                     gbc[:, hf * 512:(hf + 1) * 512], ALU.mult, ALU.mult, [btoks[hf], "prs%d" % u, "gbc"], ["ptmp"])
        self.tt("dve", outt, self.ptmp[:], xres, ALU.add, ["ptmp", xtok], [outtok])

    def pass_B2(self, es_outer, L):
        nc, S, NT, NOWN, NG = self.nc, self.S, self.NT, self.NOWN, self.NG
        pb, pb16 = self.pb, self.pb16
        with ExitStack() as es:
            WG = self.sb(es, "WG", [128, KD, 2048], BF16)
            WOG = self.sb(es, "WOG", [128, KD, 1024], BF16)
            WOM = self.sb(es, "WOM", [128, KD, 1024], BF16)
            WOUT = self.sb(es, "WOUT", [128, KD, 1024], BF16)
            gpm = self.sb(es, "gpm2", [128, 8], F32)
            ggl = self.sb(es, "ggl", [128, 8], F32)
            bg = self.sb(es, "bg", [128, 16], F32)
            gbc = self.sb(es, "gbc", [128, 1024], F32)
            self.dma("sp", gpm[:], L["g_pm"], "misc", [], ["gains"])
            self.dma("sp", ggl[:], L["g_gla"], "misc2", [], ["gains"])
            self.dma("sp", bg[:], L["b_g"], "misc3", [], ["bg"])
            self.dma("sp", gbc[:], L["g_pmix"][0:1, :].partition_broadcast(128), "misc4", [], ["gbc"])
            self.wstage = [self.sb(es, "wstgc%d" % i, [128, 2048], F32) for i in range(2)]
            self.load_w(WG, L["w_g"], KD, 2048, gpm, "WG")
            self.load_w(WOG, L["w_og"], KD, 1024, ggl, "WOG")
            self.load_w(WOM, L["w_om"], KD, 1024, None, "WOM")
            join_wout = self.load_w(WOUT, L["w_out"], KD, 1024, None, "WOUT", defer=True)
            xg = [self.sb(es, "xg%d" % i, [128, 1024], F32) for i in range(GT)]
            hTg = self.sb(es, "hTg", [128, KD, 512], BF16)
            ogg = self.sb(es, "ogg", [128, KD, 512], BF16)
            omg = self.sb(es, "omg", [128, KD, 512], BF16)
            ga = [self.sb(es, "ga%d" % i, [128, 512], F32) for i in range(2)]
            gb = [self.sb(es, "gb%d" % i, [128, 512], F32) for i in range(2)]
            m1 = [self.sb(es, "m1%d" % i, [128, 512], F32) for i in range(2)]
            m2 = [self.sb(es, "m2%d" % i, [128, 512], F32) for i in range(2)]
            mixT = self.sb(es, "mixT", [128, KD, 512], BF16)
            self.ptmp = self.sb(es, "ptmp", [128, 1024], F32)
            self.probe('B2')
            for g in range(NG):
                self.dma("sp", ogg[:], L["sc_og"][g], "ogg", ["sc_og"], ["ogg"])
                self.dma("sp", omg[:], L["sc_om"][g], "omg", ["sc_om"], ["omg"])
                items = []
                for i in range(GT):
                    j = g * GT + i
                    self.dma("sp", xg[i][:], L["xown"][j * 128:(j + 1) * 128, :], "xg%d" % i, [], ["xg%d" % i])
                    items.append((xg[i][:], "xg%d" % i, hTg[:, :, i * 128:(i + 1) * 128], "hTg", i))
                self.norm_T_multi(items)
                for fc in range(KD):
                    st = fc % 2
                    bs = (1, 2, 3, 4) if st == 0 else (5, 6, 7, 0)
                    srcs = ((WG, 0, hTg, "hTg", "WG"), (WG, 1024, hTg, "hTg", "WG"),
                            (WOG, 0, ogg, "ogg", "WOG"), (WOM, 0, omg, "omg", "WOM"))
                    for bi, (Wt, off, rhs, rtok, wtok) in enumerate(srcs):
                        for k in range(KD):
                            self.mm(pb[bs[bi]][:, 0:512], Wt[:, k, off + fc * 128:off + (fc + 1) * 128], rhs[:, k, :],
                                    k == 0, k == KD - 1, [rtok, wtok], ["b%d" % bs[bi]])
                    self.act(ga[st][:], pb[bs[0]][:, 0:512], AF.Sigmoid, ["b%d" % bs[0], "bg"], ["ga%d" % st],
                             bias=bg[:, fc:fc + 1])
                    self.act(gb[st][:], pb[bs[1]][:, 0:512], AF.Sigmoid, ["b%d" % bs[1], "bg"], ["gb%d" % st],
                             bias=bg[:, 8 + fc:9 + fc])
                    self.tt("dve", m1[st][:], ga[st][:], pb[bs[2]][:, 0:512], ALU.mult, ["ga%d" % st, "b%d" % bs[2]],
                            ["m1%d" % st])
                    self.tt("dve", m2[st][:], gb[st][:], pb[bs[3]][:, 0:512], ALU.mult, ["gb%d" % st, "b%d" % bs[3]],
                            ["m2%d" % st])
                    self.tt("dve", mixT[:, fc, :], m1[st][:], m2[st][:], ALU.add, ["m1%d" % st, "m2%d" % st], ["mixT"])
                if g == 0:
                    join_wout()
                for i in range(GT):
                    j = g * GT + i
                    bs = (1, 2) if i % 2 == 0 else (3, 4)
                    for hf in range(2):
                        for k in range(KD):
                            self.mm(pb[bs[hf]][:, 0:512], mixT[:, k, i * 128:(i + 1) * 128], WOUT[:, k, hf * 512:(hf + 1) * 512],
                                    k == 0, k == KD - 1, ["mixT", "WOUT"], ["b%d" % bs[hf]])
                    self.post_norm_res([pb[bs[0]], pb[bs[1]]], ["b%d" % bs[0], "b%d" % bs[1]], gbc, xg[i][:], "xg%d" % i,
                                       xg[i][:], "xg%d" % i, 2 + i % 2)
                    self.dma("sp", L["sc_x1"][j * 128:(j + 1) * 128, :], xg[i][:], "x1st%d" % i, ["xg%d" % i], ["sc_x1"])

    def pass_C(self, es_outer, L):
        nc, S, NT, NOWN, NG = self.nc, self.S, self.NT, self.NOWN, self.NG
        pb, pb16 = self.pb, self.pb16
        with ExitStack() as es:
            self.probe('Cstart')
            WFG = self.sb(es, "WFG", [128, KD, DFF], BF16)
            WFU = self.sb(es, "WFU", [128, KD, DFF], BF16)
            WFD = self.sb(es, "WFD", [128, KF, 1024], BF16)
            gpf = self.sb(es, "gpf", [128, 8], F32)
            gbc = self.sb(es, "gbc2", [128, 1024], F32)
            self.dma("sp", gpf[:], L["g_pf"], "misc", [], ["gains"])
            self.dma("sp", gbc[:], L["g_pffn"][0:1, :].partition_broadcast(128), "misc4", [], ["gbc"])
            with ExitStack() as ew:
                self.wstage = [self.sb(ew, "wstgd%d" % i, [128, 2048], F32) for i in range(2)]
                self.load_w(WFG, L["w_fg"], KD, DFF, gpf, "WFG")
                self.load_w(WFU, L["w_fu"], KD, DFF, gpf, "WFU")
                self.load_w(WFD, L["w_fd"], KF, 1024, None, "WFD")
            self.s.barrier()
            self.probe('C0')
            xg = [self.sb(es, "xc%d" % i, [128, 1024], F32) for i in range(GT)]
            hTg = self.sb(es, "hTc", [128, KD, 512], BF16)
            actT = self.sb(es, "actT", [128, KF, 512], BF16)
            sl = [self.sb(es, "sl%d" % i, [128, 512], F32) for i in range(2)]
            self.ptmp = self.sb(es, "ptmp2", [128, 1024], F32)
            self.probe('C')
            for g in range(NG):
                items = []
                for i in range(GT):
                    j = g * GT + i
                    self.dma("sp", xg[i][:], L["sc_x1"][j * 128:(j + 1) * 128, :], "xg%d" % i, ["sc_x1"], ["xg%d" % i])
                    items.append((xg[i][:], "xg%d" % i, hTg[:, :, i * 128:(i + 1) * 128], "hTg", i))
                self.norm_T_multi(items)
                for fc in range(KF):
                    st = fc % 2
                    bg_, bu_ = (1 + 2 * (fc % 3), 2 + 2 * (fc % 3))
                    for (Wt, bank, wtok) in ((WFG, bg_, "WFG"), (WFU, bu_, "WFU")):
                        for k in range(KD):
                            self.mm(pb[bank][:, 0:512], Wt[:, k, fc * 128:(fc + 1) * 128], hTg[:, k, :], k == 0, k == KD - 1,
                                    ["hTg", wtok], ["b%d" % bank])
                    self.act(sl[st][:], pb[bg_][:, 0:512], AF.Silu, ["b%d" % bg_], ["sl%d" % st])
                    self.tt("dve", actT[:, fc, :], sl[st][:], pb[bu_][:, 0:512], ALU.mult, ["sl%d" % st, "b%d" % bu_],
                            ["actT"])
                for i in range(GT):
                    j = g * GT + i
                    bs = (1, 2) if i % 2 == 0 else (3, 4)
                    for hf in range(2):
                        for k in range(KF):
                            self.mm(pb[bs[hf]][:, 0:512], actT[:, k, i * 128:(i + 1) * 128], WFD[:, k, hf * 512:(hf + 1) * 512],
                                    k == 0, k == KF - 1, ["actT", "WFD"], ["b%d" % bs[hf]])
                    self.post_norm_res([pb[bs[0]], pb[bs[1]]], ["b%d" % bs[0], "b%d" % bs[1]], gbc, xg[i][:], "xg%d" % i,
                                       xg[i][:], "xg%d" % i, 2 + i % 2)
                    self.dma("sp", L["out"][j * 128:(j + 1) * 128, :], xg[i][:], "outst%d" % i, ["xg%d" % i], ["out_dram"])


def _consts(parity):
    inv = (1.0 / (10000.0 ** (np.arange(0, 64, 2, dtype=np.float32) / np.float32(64)))).astype(np.float32)
    invf = np.concatenate([inv, inv]).reshape(64, 1).astype(np.float32)
    ident = np.eye(128, dtype=np.float32)
    s = np.arange(128)[:, None]
    t = np.arange(128)[None, :]
    umat = ((s // 64 == t // 64) & (s > t)).astype(np.float32)
    cind = (s // 64 == np.arange(2)[None, :]).astype(np.float32)
    qmask = np.zeros((128, 2), np.float32)
    qmask[:, parity] = 1.0
    diag = ((s // 64) <= (t // 64)).astype(np.float32)
    amask = np.zeros((128, 2, 128), np.float32)
    if parity == 0:
        amask[:, 0, :] = diag
    else:
        amask[:, 0, :] = 1.0
        amask[:, 1, :] = diag
    return dict(invf=invf, ident=ident, umat=umat, cind=cind, qmask=qmask,
                amask=np.ascontiguousarray(amask.reshape(128, 256)))


def _pk(v, n):
    return np.ascontiguousarray(v.reshape(n, 128).T).astype(np.float32)


def _weights(inp):
    w_in = inp["w_in"][0]
    sp = np.cumsum([0, 512, 512, 1024, 1024, 16, 384, 256, 64])
    q, k, v, g, ha, cq, ckv, kpe = [w_in[:, sp[i]:sp[i + 1]] for i in range(8)]
    kpesw = np.concatenate([kpe[:, 32:], kpe[:, :32]], axis=1)
    w_sh = np.ascontiguousarray(np.concatenate([k, v, ckv, kpe, kpesw, ha], axis=1))
    w_ow = np.ascontiguousarray(np.concatenate([q, g, cq], axis=1))
    w_uq = inp["w_uq"][0].reshape(384, 8, 192)
    nope, rope = w_uq[:, :, :128], w_uq[:, :, 128:]
    ropesw = np.concatenate([rope[:, :, 32:], rope[:, :, :32]], axis=2)
    w_uq2 = np.ascontiguousarray(np.concatenate([nope, rope, ropesw], axis=2).reshape(384, 2048))
    w_ukv = inp["w_ukv"][0].reshape(256, 8, 256)
    w_ukT = np.ascontiguousarray(w_ukv[:, :, :128].transpose(2, 1, 0).reshape(128, 2048))
    w_uv = np.ascontiguousarray(w_ukv[:, :, 128:].reshape(256, 1024))
    gla = inp["gla_norm"][0]
    return dict(
        w_sh=w_sh, w_ow=w_ow, g_pm=_pk(inp["pre_mix_norm"][0], 8),
        w_a2=np.ascontiguousarray(inp["w_a2"][0]), b_a2=np.ascontiguousarray(inp["b_a2"][0].reshape(1, 512)),
        g_gla=_pk(np.tile(gla, 4), 8), w_og=np.ascontiguousarray(inp["w_o_gla"][0]),
        g_q=_pk(inp["q_norm"][0], 3), w_uq=w_uq2, g_kv=_pk(inp["kv_norm"][0], 2),
        w_ukT=w_ukT, w_uv=w_uv, w_om=np.ascontiguousarray(inp["w_o_mla"][0]),
        w_g=np.ascontiguousarray(inp["w_gate"][0]), b_g=_pk(inp["b_gate"][0], 16),
        w_out=np.ascontiguousarray(inp["w_out"][0]),
        g_pmix=np.ascontiguousarray(inp["post_mix_norm"][0].reshape(1, D)),
        g_pf=_pk(inp["pre_ffn_norm"][0], 8),
        w_fg=np.ascontiguousarray(inp["w_ffn_gate"][0]), w_fu=np.ascontiguousarray(inp["w_ffn_up"][0]),
        w_fd=np.ascontiguousarray(inp["w_ffn_down"][0]),
        g_pffn=np.ascontiguousarray(inp["post_ffn_norm"][0].reshape(1, D)),
    )


def make_in_maps(inp, S):
    x = np.asarray(inp["x"], dtype=np.float32)
    pos = np.asarray(inp["positions"], dtype=np.int32)
    W = _weights({k: np.asarray(v) for k, v in inp.items()})
    maps = []
    NT = S // 128
    for core in range(8):
        b, par = core // 2, core % 2
        xb = x[b, :S]
        xo = xb.reshape(NT // 2, 2, 128, D)[:, par].reshape(S // 2, D)
        pb = pos[b, :S]
        po = pb.reshape(NT // 2, 2, 128)[:, par].reshape(1, S // 2)
        m = dict(xall=np.ascontiguousarray(xb), xown=np.ascontiguousarray(xo),
                 posall=np.ascontiguousarray(pb.reshape(1, S)), posown=np.ascontiguousarray(po))
        m.update(_consts(par))
        m.update(W)
        maps.append(m)
    return maps


def run(inp, S, upto=99, dbg=False):
    kb = K(S, upto=upto, dbg=dbg)
    nc = kb.build()
    maps = make_in_maps(inp, S)
    maps = [{k: v for k, v in m.items() if k in kb.in_names} for m in maps]
    res = run_bass_kernel_spmd(nc, maps, core_ids=list(range(8)))
    B = 4
    NT = S // 128
    full = np.zeros((B, NT // 2, 2, 128, D), np.float32)
    for core in range(8):
        b, par = core // 2, core % 2
        full[b, :, par] = np.asarray(res.results[core]["out"]).reshape(NT // 2, 128, D)
    return full.reshape(B, S, D), res


def kernel(**inputs):
    o, _ = run(inputs, 8192)
    return o
```

```python
import math
from contextlib import ExitStack

import numpy as np
import concourse.bass as bass
import concourse.mybir as mybir
from concourse.bass_utils import run_bass_kernel_spmd

F32 = mybir.dt.float32
BF16 = mybir.dt.bfloat16
I32 = mybir.dt.int32
AF = mybir.ActivationFunctionType
ALU = mybir.AluOpType

D = 1024
KD = 8
DFF = 2816
KF = 22
EPS = 1e-6
NSH = 1936
NOW = 1920
GT = 4


PSUM_TOKENS = frozenset("b%d" % i for i in range(8))


class Op:
    __slots__ = ("eng", "fn", "deps", "signal", "sem", "count", "key", "idx")

    def __init__(self, eng, fn, key):
        self.eng, self.fn, self.key = eng, fn, key
        self.deps, self.signal, self.sem, self.count = [], False, None, 0


class Sched:
    ENGS = ("pe", "act", "dve", "pool", "sp")

    def __init__(self):
        self.ops = {e: [] for e in self.ENGS}
        self.last_w = {}
        self.readers = {}
        self.nops = 0

    def add(self, eng, fn, r=(), w=(), key=None):
        op = Op(eng, fn, key)
        if key is not None:
            op.signal = True
        op.idx = self.nops
        self.nops += 1
        raw = set()
        deps = set()
        for t in r:
            lw = self.last_w.get(t)
            if lw is not None:
                deps.add(lw)
                raw.add(lw)
            if t in PSUM_TOKENS:
                for rd in self.readers.get(t, {}).values():
                    if rd.eng != eng:
                        deps.add(rd)
        for t in w:
            lw = self.last_w.get(t)
            if lw is not None:
                deps.add(lw)
            for rd in self.readers.get(t, {}).values():
                deps.add(rd)
        for d in deps:
            if d is op:
                continue
            if d.key is None and key is None and d.eng == eng:
                if eng == "pe":
                    continue
            op.deps.append(d)
            d.signal = True
        for t in r:
            rk = eng if key is None else ("dma", op.idx)
            self.readers.setdefault(t, {})[rk] = op
        for t in w:
            self.last_w[t] = op
            self.readers[t] = {}
        self.ops[eng].append(op)
        return op

    def barrier(self):
        lasts = []
        for e in self.ENGS:
            last_eng = None
            last_dma = {}
            for op in self.ops[e]:
                if op.key is None:
                    last_eng = op
                else:
                    last_dma[op.key] = op
            if last_eng is not None:
                lasts.append(last_eng)
            lasts.extend(last_dma.values())
        for e in self.ENGS:
            op = Op(e, lambda eng: eng.nop(), None)
            op.idx = self.nops
            self.nops += 1
            for d in lasts:
                if d.key is None and d.eng == e:
                    continue
                op.deps.append(d)
                d.signal = True
            self.ops[e].append(op)

    def finalize(self, nc, es):
        self.sems = {}
        cnt = {}
        for e in self.ENGS:
            for op in self.ops[e]:
                if not op.signal:
                    continue
                if op.key is not None:
                    k = ("dma", op.key)
                    inc = 16
                else:
                    k = ("eng", e)
                    inc = 1
                if k not in self.sems:
                    self.sems[k] = es.enter_context(nc.semaphore("s_%s_%s" % (k[0], str(k[1]))))
                    cnt[k] = 0
                cnt[k] += inc
                op.sem = k
                op.count = cnt[k]

    def emit(self, eng_name, engine):
        waited = {}
        for op in self.ops[eng_name]:
            need = {}
            for d in op.deps:
                if need.get(d.sem, 0) < d.count:
                    need[d.sem] = d.count
            for k, v in need.items():
                if waited.get(k, 0) < v:
                    engine.wait_ge(self.sems[k], v)
                    waited[k] = v
            ins = op.fn(engine)
            if op.signal:
                ins.then_inc(self.sems[op.sem], 16 if op.key is not None else 1)


class K:
    def __init__(self, S, upto=99, dbg=False):
        self.S = S
        self.NT = S // 128
        self.NOWN = self.NT // 2
        self.NG = self.NOWN // GT
        assert self.NG * GT * 2 * 128 == S
        self.upto = upto
        self.dbg = dbg
        self.nc = bass.Bass("TRN2", target_bir_lowering=False)
        self.s = Sched()
        self.dkeys = {}

    def mm(self, out, lhsT, rhs, start, stop, r, w):
        return self.s.add("pe", lambda e: e.matmul(out, lhsT, rhs, start=start, stop=stop), r, w)

    def tr(self, out, in_, r, w):
        ident = self.identb[:]
        return self.s.add("pe", lambda e: e.transpose(out, in_, ident), r + ["ident"], w)

    def act(self, out, in_, func, r, w, scale=None, bias=None, accum=None):
        kw = {}
        if scale is not None:
            kw["scale"] = scale
        if bias is not None:
            kw["bias"] = bias
        if accum is not None:
            kw["accum_out"] = accum
        return self.s.add("act", lambda e: e.activation(out, in_, func, **kw), r, w)

    def ts(self, eng, out, in0, s1, op0, r, w, s2=None, op1=None):
        if op1 is None:
            return self.s.add(eng, lambda e: e.tensor_scalar(out, in0, s1, None, op0), r, w)
        return self.s.add(eng, lambda e: e.tensor_scalar(out, in0, s1, s2, op0, op1), r, w)

    def tt(self, eng, out, in0, in1, op, r, w):
        return self.s.add(eng, lambda e: e.tensor_tensor(out, in0, in1, op), r, w)

    def stt(self, out, in0, scalar, in1, op0, op1, r, w):
        return self.s.add("dve", lambda e: e.scalar_tensor_tensor(out, in0, scalar, in1, op0, op1), r, w)

    def cp(self, eng, out, in_, r, w):
        if eng == "act":
            return self.s.add("act", lambda e: e.activation(out, in_, AF.Copy), r, w)
        return self.s.add(eng, lambda e: e.tensor_copy(out, in_), r, w)

    def memset(self, eng, ap, val, w):
        return self.s.add(eng, lambda e: e.memset(ap, val), [], w)

    def recip(self, out, in_, r, w):
        return self.s.add("dve", lambda e: e.reciprocal(out, in_), r, w)

    def dma(self, q, out, in_, key, r, w):
        base = key.rstrip("0123456789")
        ch = "chain_" + base
        return self.s.add(q, lambda e: e.dma_start(out, in_), list(r) + [ch], list(w) + [ch], key=base)

    def sb(self, es, name, shape, dt):
        return es.enter_context(self.nc.sbuf_tensor(name, list(shape), dt))

    def din(self, name, shape, dt=F32):
        t = self.nc.dram_tensor(name, list(shape), dt, kind="ExternalInput").ap()
        self.in_names.append(name)
        return t

    def dscr(self, name, shape, dt):
        return self.nc.dram_tensor(name, list(shape), dt, kind="Internal").ap()

    def probe(self, tag):
        import os
        if not os.environ.get("SBPROBE"):
            return
        try:
            with self.nc.sbuf_tensor("probe_" + tag, [128, 100000], F32):
                pass
        except AssertionError as e:
            msg = str(e)
            i = msg.find("(base=")
            print("SBUF", tag, msg[i:i + 40])

    def rstd_from_ss(self, ss, lnv, rstd, n, rt, wt):
        self.act(lnv, ss, AF.Ln, rt, [wt + "_ln"], scale=1.0 / n, bias=self.epsc[:, 0:1])
        self.act(rstd, lnv, AF.Exp, [wt + "_ln"], [wt], scale=-0.5)

    def load_w(self, dst, src, nk, ncol, gain, tok, defer=False):
        engs = ("dve", "act")
        CB = self.wcb
        for k in range(nk):
            for c0 in range(0, ncol, CB):
                c1 = min(ncol, c0 + CB)
                sl = self.wslot % 2
                self.wslot += 1
                stg = self.wstage[sl]
                self.dma("sp", stg[:, 0:c1 - c0], src[k * 128:(k + 1) * 128, c0:c1], "wst" + "AB"[sl],
                         [], ["wst%d" % sl])
                eng = engs[self.wrot % 2]
                self.wrot += 1
                o = dst[:, k, c0:c1]
                i = stg[:, 0:c1 - c0]
                wt = [tok + "#" + eng]
                if gain is not None:
                    g = gain[:, k:k + 1]
                    if eng == "act":
                        self.act(o, i, AF.Copy, ["wst%d" % sl, "gains"], wt, scale=g)
                    else:
                        self.ts(eng, o, i, g, ALU.mult, ["wst%d" % sl, "gains"], wt)
                else:
                    self.cp(eng, o, i, ["wst%d" % sl], wt)

        def join():
            self.s.add("pe", lambda e: e.nop(), [tok + "#dve", tok + "#act"], [tok])
        if defer:
            return join
        join()
        return None

    def norm_T(self, xt, xtok, hT_ap, htok, u):
        self.norm_T_multi([(xt, xtok, hT_ap, htok, u)])

    def norm_T_multi(self, items):
        junk = self.junk
        for (xt, xtok, hT_ap, htok, u) in items:
            self.act(junk[:], xt, AF.Square, [xtok], ["junk", "ss%d" % u], accum=self.st_ss[u][:, 0:1])
        for (xt, xtok, hT_ap, htok, u) in items:
            self.act(self.st_ln[u][:, 0:1], self.st_ss[u][:, 0:1], AF.Ln, ["ss%d" % u], ["rs%d_ln" % u],
                     scale=1.0 / D, bias=self.epsc[:, 0:1])
        for (xt, xtok, hT_ap, htok, u) in items:
            self.act(self.st_rs[u][:, 0:1], self.st_ln[u][:, 0:1], AF.Exp, ["rs%d_ln" % u], ["rs%d" % u], scale=-0.5)
        b0 = self.pb16[0]
        for (xt, xtok, hT_ap, htok, u) in items:
            xu = u % len(self.xs)
            xs = self.xs[xu]
            self.ts("dve", xs[:], xt, self.st_rs[u][:, 0:1], ALU.mult, [xtok, "rs%d" % u], ["xs%d" % xu])
            for k in range(KD):
                self.tr(b0[:, k * 128:(k + 1) * 128], xs[:, k * 128:(k + 1) * 128], ["xs%d" % xu], ["b0"])
            self.cp("act", hT_ap, b0[:, 0:1024].rearrange("p (k n) -> p k n", k=KD), ["b0"], [htok])

    def build(self):
        nc, S, NT, NOWN, NG = self.nc, self.S, self.NT, self.NOWN, self.NG
        self.in_names = []
        SO = S // 2
        xall = self.din("xall", [S, D])
        xown = self.din("xown", [SO, D])
        posall = self.din("posall", [1, S], I32)
        posown = self.din("posown", [1, SO], I32)
        invf = self.din("invf", [64, 1])
        ident = self.din("ident", [128, 128])
        umat = self.din("umat", [128, 128])
        cind = self.din("cind", [128, 2])
        qmask = self.din("qmask", [128, 2])
        amask = self.din("amask", [128, 256])
        w_sh = self.din("w_sh", [D, NSH])
        w_ow = self.din("w_ow", [D, NOW])
        g_pm = self.din("g_pm", [128, 8])
        w_a2 = self.din("w_a2", [16, 512])
        b_a2 = self.din("b_a2", [1, 512])
        g_gla = self.din("g_gla", [128, 8])
        w_og = self.din("w_og", [D, D])
        g_q = self.din("g_q", [128, 3])
        w_uq = self.din("w_uq", [384, 2048])
        g_kv = self.din("g_kv", [128, 2])
        w_ukT = self.din("w_ukT", [128, 2048])
        w_uv = self.din("w_uv", [256, 1024])
        w_om = self.din("w_om", [D, D])
        w_g = self.din("w_g", [D, 2048])
        b_g = self.din("b_g", [128, 16])
        w_out = self.din("w_out", [D, D])
        g_pmix = self.din("g_pmix", [1, D])
        g_pf = self.din("g_pf", [128, 8])
        w_fg = self.din("w_fg", [D, DFF])
        w_fu = self.din("w_fu", [D, DFF])
        w_fd = self.din("w_fd", [DFF, D])
        g_pffn = self.din("g_pffn", [1, D])
        out = nc.dram_tensor("out", [SO, D], F32, kind="ExternalOutput").ap()

        rope_all = self.dscr("rope_all", [2, 64, S], F32)
        rope_own = self.dscr("rope_own", [2, 64, SO], F32)
        sc_og = self.dscr("sc_og", [NG, 128, 8, 512], BF16)
        sc_cq = self.dscr("sc_cq", [NOWN, 128, 3, 128], BF16)
        sc_om = self.dscr("sc_om", [NG, 128, 8, 512], BF16)
        sc_x1 = self.dscr("sc_x1", [SO, D], F32)
        self.dbg_out = None
        if self.dbg:
            self.dbg_out = nc.dram_tensor("dbg", [128, 4096], F32, kind="ExternalOutput").ap()

        with ExitStack() as es:
            self.identf = self.sb(es, "identf", [128, 128], F32)
            self.identb = self.sb(es, "identb", [128, 128], BF16)
            self.epsc = self.sb(es, "epsc", [128, 1], F32)
            self.onec = self.sb(es, "onec", [128, 1], F32)
            self.hpic = self.sb(es, "hpic", [128, 1], F32)
            self.junk = self.sb(es, "junk", [128, 1024], BF16)
            self.st_ss = [self.sb(es, "st_ss%d" % u, [128, 8], F32) for u in range(4)]
            self.st_ln = [self.sb(es, "st_ln%d" % u, [128, 8], F32) for u in range(4)]
            self.st_rs = [self.sb(es, "st_rs%d" % u, [128, 8], F32) for u in range(4)]
            self.xs = [self.sb(es, "xs%d" % u, [128, 1024], BF16) for u in range(2)]
            pbanks = [es.enter_context(nc.psum_tensor("pb%d" % i, [128, 512], F32)) for i in range(8)]
            self.pb = pbanks
            self.pb16 = [b[:].bitcast(BF16) for b in pbanks]
            self.wslot = 0
            self.wcb = 2048
            self.wrot = 0
            pb, pb16 = self.pb, self.pb16

            self.dma("sp", self.identf[:], ident, "misc", [], ["identf"])
            self.cp("dve", self.identb[:], self.identf[:], ["identf"], ["ident"])
            self.memset("dve", self.epsc[:], EPS, ["epsc"])
            self.memset("dve", self.onec[:], 1.0, ["onec"])
            self.memset("dve", self.hpic[:], math.pi / 2, ["hpic"])

            with ExitStack() as er:
                CH = min(2048, SO)
                invf_t = self.sb(er, "invf_t", [64, 1], F32)
                posi = self.sb(er, "posi", [64, CH], I32)
                ang = self.sb(er, "ang", [64, CH], F32)
                tq = self.sb(er, "tq", [64, CH], F32)
                rr = self.sb(er, "rr", [64, CH], F32)
                g1 = self.sb(er, "g1", [64, CH], F32)
                co = self.sb(er, "co", [64, CH], F32)
                si = self.sb(er, "si", [64, CH], F32)
                self.dma("sp", invf_t[:], invf, "misc2", [], ["invf"])
                MAGIC = 12582912.0
                C1 = 6.28125
                C2 = 2.0 * math.pi - 6.28125
                PI = math.pi
                for (pos_d, rope_d, n) in ((posall, rope_all, S), (posown, rope_own, SO)):
                    for c0 in range(0, n, CH):
                        self.dma("sp", posi[:], pos_d[0:1, c0:c0 + CH].partition_broadcast(64), "posi",
                                 [], ["posi"])
                        self.cp("dve", ang[:], posi[:], ["posi"], ["ang"])
                        self.ts("dve", ang[:], ang[:], invf_t[:, 0:1], ALU.mult, ["ang", "invf"], ["ang"])
                        self.ts("dve", tq[:], ang[:], 1.0 / (2 * PI), ALU.mult, ["ang"], ["tq"], s2=MAGIC, op1=ALU.add)
                        self.ts("dve", tq[:], tq[:], -MAGIC, ALU.add, ["tq"], ["tq"])
                        self.stt(rr[:], tq[:], -C1, ang[:], ALU.mult, ALU.add, ["tq", "ang"], ["rr"])
                        self.stt(rr[:], tq[:], -C2, rr[:], ALU.mult, ALU.add, ["tq", "rr"], ["rr"])
                        self.ts("dve", g1[:], rr[:], PI, ALU.is_gt, ["rr"], ["g1"], s2=-2 * PI, op1=ALU.mult)
                        self.tt("dve", rr[:], rr[:], g1[:], ALU.add, ["rr", "g1"], ["rr"])
                        self.ts("dve", g1[:], rr[:], -PI, ALU.is_lt, ["rr"], ["g1"], s2=2 * PI, op1=ALU.mult)
                        self.tt("dve", rr[:], rr[:], g1[:], ALU.add, ["rr", "g1"], ["rr"])
                        self.ts("dve", rr[:], rr[:], -3.1415925, ALU.max, ["rr"], ["rr"], s2=3.1415925, op1=ALU.min)
                        self.act(si[0:32, :], rr[0:32, :], AF.Sin, ["rr"], ["si"], scale=-1.0)
                        self.act(si[32:64, :], rr[32:64, :], AF.Sin, ["rr"], ["si"])
                        self.act(g1[:], rr[:], AF.Abs, ["rr"], ["g1"])
                        self.act(co[:], g1[:], AF.Sin, ["g1"], ["co"], scale=-1.0, bias=self.hpic[0:64, 0:1])
                        self.dma("sp", rope_d[0, :, c0:c0 + CH], co[:], "ropest0", ["co"], ["rope_dram"])
                        self.dma("sp", rope_d[1, :, c0:c0 + CH], si[:], "ropest1", ["si"], ["rope_dram"])

            self.s.barrier()
            if self.upto >= 1:
                self.pass_A(es, locals())
            self.finish(es, out)
        return nc

    def finish(self, es, out):
        nc = self.nc
        self.s.add("sp", lambda e: e.nop(), ["out_dram", "dbg_dram"], [])
        self.s.finalize(nc, es)
        with nc.Block() as block:
            @block.sync
            def _(e):
                self.s.emit("sp", e)

            @block.tensor
            def _(e):
                self.s.emit("pe", e)

            @block.scalar
            def _(e):
                self.s.emit("act", e)

            @block.vector
            def _(e):
                self.s.emit("dve", e)

            @block.gpsimd
            def _(e):
                self.s.emit("pool", e)

    def pass_A(self, es_outer, L):
        nc, S, NT, NOWN, NG = self.nc, self.S, self.NT, self.NOWN, self.NG
        pb, pb16 = self.pb, self.pb16
        xall, xown = L["xall"], L["xown"]
        with ExitStack() as es:
            vaug = self.sb(es, "vaug", [128, NT, 258], BF16)
            kpeT = self.sb(es, "kpeT", [64, S], BF16)
            self.vaug, self.kpeT = vaug, kpeT
            self.memset("pool", vaug[:, :, 256:258], 1.0, ["vaug_ones"])
            with ExitStack() as ea:
                WS = self.sb(ea, "WS", [128, KD, NSH], BF16)
                WO = self.sb(ea, "WO", [128, KD, NOW], BF16)
                gpm = self.sb(ea, "gpm", [128, 8], F32)
                wa2a = self.sb(ea, "wa2a", [32, 512], BF16)
                umb = self.sb(ea, "umb", [128, 128], BF16)
                cif = self.sb(ea, "cif", [128, 2], F32)
                cib = self.sb(ea, "cib", [128, 2], BF16)
                qmf = self.sb(ea, "qmf", [128, 2], F32)
                ew = ExitStack()
                wa2f = self.sb(ew, "wa2f", [32, 512], F32)
                umf = self.sb(ew, "umf", [128, 128], F32)
                self.dma("sp", gpm[:], L["g_pm"], "misc", [], ["gains"])
                self.memset("dve", wa2f[:], 0.0, ["wa2f"])
                self.dma("sp", wa2f[0:16, :], L["w_a2"], "misc2", ["wa2f"], ["wa2f"])
                self.dma("sp", wa2f[16:17, :], L["b_a2"], "misc3", ["wa2f"], ["wa2f"])
                self.cp("dve", wa2a[:], wa2f[:], ["wa2f"], ["wa2a"])
                self.dma("sp", umf[:], L["umat"], "misc4", [], ["umf"])
                self.cp("dve", umb[:], umf[:], ["umf"], ["umb"])
                self.dma("sp", cif[:], L["cind"], "misc5", [], ["cif"])
                self.cp("dve", cib[:], cif[:], ["cif"], ["cib"])
                self.dma("sp", qmf[:], L["qmask"], "misc6", [], ["qmf"])
                self.ts("dve", qmf[:], qmf[:], 128.0 ** -0.5, ALU.mult, ["qmf"], ["qmf"])
                with ew:
                    self.wstage = [self.sb(ew, "wstg%d" % i, [128, 2048], F32) for i in range(2)]
                    self.load_w(WS, L["w_sh"], KD, NSH, gpm, "WS")
                    self.load_w(WO, L["w_ow"], KD, NOW, gpm, "WO")
                self.s.barrier()
                xin1 = [self.sb(ea, "xin_%d" % i, [128, 1024], F32) for i in range(3)]
                xin = [xin1, xin1]
                hTa = self.sb(ea, "hTa", [128, KD, 128], BF16)
                hT = [[hTa] + [self.sb(ea, "hT%d_%d" % (sl, i), [128, KD, 128], BF16) for i in (1, 2)] for sl in range(2)]
                hTn = [["hTa", "hT%d_1" % sl, "hT%d_2" % sl] for sl in range(2)]
                haT = [self.sb(ea, "haT%d" % i, [32, 128], BF16) for i in range(2)]
                ksb = [self.sb(ea, "ksb%d" % i, [128, 512], F32) for i in range(2)]
                e1s = self.sb(ea, "e1s", [128, 512], F32)
                e1 = [e1s, e1s]
                nl = [self.sb(ea, "nl%d" % i, [128, 512], BF16) for i in range(2)]
                dfac = [self.sb(ea, "dfac%d" % i, [128, 512], F32) for i in range(2)]
                dc = [self.sb(ea, "dc%d" % i, [128, 8], F32) for i in range(2)]
                kdec = [self.sb(ea, "kdec%d" % i, [128, 512], BF16) for i in range(2)]
                vb = [self.sb(ea, "vb%d" % i, [128, 1024], BF16) for i in range(2)]
                stc = [[self.sb(ea, "stc%d_%d" % (i, q), [128, 4], F32) for q in range(3)] for i in range(3)]
                Sst = self.sb(ea, "Sst", [128, 4, 256], F32)
                Sbf = self.sb(ea, "Sbf", [128, 4, 4, 256], BF16)
                qpad = self.sb(ea, "qpad", [128, 4, 4, 128], BF16)
                eg = self.sb(ea, "eg", [128, 1024], F32)
                sg = self.sb(ea, "sg", [128, 1024], F32)
                osb = eg
                og = self.sb(ea, "og", [128, 1024], BF16)
                ogT = self.sb(ea, "ogT", [128, KD, 128], BF16)
                cqn = self.sb(ea, "cqn", [128, 384], BF16)
                cqnT = self.sb(ea, "cqnT", [128, 3, 128], BF16)
                rcs1 = self.sb(ea, "rcs", [64, 2, 128], F32)
                t11 = self.sb(ea, "t1", [64, 128], F32)
                t21 = self.sb(ea, "t2", [64, 128], F32)
                rcs, t1, t2 = [rcs1, rcs1], [t11, t11], [t21, t21]
                sso = self.sb(ea, "sso", [128, 4], F32)
                lno = self.sb(ea, "lno", [128, 4], F32)
                rso = self.sb(ea, "rso", [128, 4], F32)
                for i in range(2):
                    self.memset("dve", haT[i][:], 1.0, ["haT%d" % i])
                self.memset("dve", Sst[:], 0.0, ["Sst"])
                self.memset("pool", qpad[:], 0.0, ["qpad"])

                rope_all = L["rope_all"]
                NP = NT // 2

                def S1(j):
                    sl = j % 2
                    items = []
                    for i in range(3):
                        src = xall[(2 * j + i) * 128:(2 * j + i + 1) * 128, :] if i < 2 else xown[j * 128:(j + 1) * 128, :]
                        tk = "xin_%d" % i
                        self.dma("sp", xin[sl][i][:], src, "xin", [], [tk])
                        items.append((xin[sl][i][:], tk, hT[sl][i][:], hTn[sl][i], i))
                    self.norm_T_multi(items)

                def proj(j, ab):
                    sl = j % 2
                    t = 2 * j + ab
                    h, hk = hT[sl][ab], hTn[sl][ab]
                    cb = 5 + ab
                    for k in range(KD):
                        self.mm(pb[cb][0:16, 0:128], WS[:, k, 1920:1936], h[:, k, :], k == 0, k == KD - 1,
                                [hk, "WS"], ["b%d" % cb])
                    self.cp("dve", haT[ab][0:16, :], pb[cb][0:16, 0:128], ["b%d" % cb], ["haT%d" % ab])
                    for (bank, c0, c1) in ((1, 0, 512), (2, 512, 1024), (3, 1024, 1536)):
                        for k in range(KD):
                            self.mm(pb[bank][:, 0:512], h[:, k, :], WS[:, k, c0:c1], k == 0, k == KD - 1,
                                    [hk, "WS"], ["b%d" % bank])
                        if bank == 1:
                            self.cp("act", ksb[ab][:], pb[1][:, 0:512], ["b1"], ["ksb%d" % ab])
                            self.mm(pb[cb][:, 0:512], haT[ab][:, :], wa2a[:, :], True, True, ["haT%d" % ab, "wa2a"],
                                    ["b%d" % cb])
                            self.act(e1[ab][:], pb[cb][:, 0:512], AF.Exp, ["b%d" % cb], ["e1s"], scale=-1.0)
                            self.act(nl[ab][:], e1[ab][:], AF.Ln, ["e1s"], ["nl%d" % ab], bias=self.onec[:, 0:1])
                        elif bank == 2:
                            self.cp("dve", vb[ab][:, 0:512], pb[2][:, 0:512], ["b2"], ["vb%d" % ab])
                        else:
                            self.cp("act", vb[ab][:, 512:1024], pb[3][:, 0:512], ["b3"], ["vb%d" % ab])
                    for k in range(KD):
                        self.mm(pb[4][:, 0:256], h[:, k, :], WS[:, k, 1536:1792], k == 0, k == KD - 1,
                                [hk, "WS"], ["b4"])
                    for (o0, c0) in ((256, 1792), (384, 1856)):
                        for k in range(KD):
                            self.mm(pb[4][0:64, o0:o0 + 128], WS[:, k, c0:c0 + 64], h[:, k, :], k == 0, k == KD - 1,
                                    [hk, "WS"], ["b4"])
                    ssc, lnc, rsc = stc[ab]
                    self.act(self.junk[:, 0:256], pb[4][:, 0:256], AF.Square, ["b4"], ["junk", "ssc%d" % ab],
                             accum=ssc[:, 0:1])
                    self.rstd_from_ss(ssc[:, 0:1], lnc[:, 0:1], rsc[:, 0:1], 256, ["ssc%d" % ab], "rsc%d" % ab)
                    self.ts("dve", vaug[:, t, 0:256], pb[4][:, 0:256], rsc[:, 0:1], ALU.mult, ["b4", "rsc%d" % ab],
                            ["vaug%d" % t])
                    self.dma("sp", rcs[ab][:, :, :], rope_all[:, :, t * 128:(t + 1) * 128].rearrange("a p n -> p a n"),
                             "rcs", ["rope_dram"], ["rcs_"])
                    self.tt("dve", t1[ab][:], pb[4][0:64, 256:384], rcs[ab][:, 0, :], ALU.mult, ["b4", "rcs_"],
                            ["t1_"])
                    self.tt("dve", t2[ab][:], pb[4][0:64, 384:512], rcs[ab][:, 1, :], ALU.mult, ["b4", "rcs_"],
                            ["t2_"])
                    self.tt("dve", kpeT[:, t * 128:(t + 1) * 128], t1[ab][:], t2[ab][:], ALU.add,
                            ["t1_", "t2_"], ["kpeT%d" % t])

                def chain(ab):
                    cb = 5 + ab
                    self.mm(pb[cb][:, 0:512], umb[:, :], nl[ab][:, :], True, True, ["umb", "nl%d" % ab], ["b%d" % cb])
                    self.act(dfac[ab][:], pb[cb][:, 0:512], AF.Exp, ["b%d" % cb], ["dfac%d" % ab], scale=-1.0 / 16.0)
                    for hh in range(4):
                        self.mm(pb[cb][:, 2 * hh:2 * hh + 2], nl[ab][:, hh * 128:(hh + 1) * 128], cib[:, :], True, True,
                                ["nl%d" % ab, "cib"], ["b%d" % cb])
                    self.act(dc[ab][:], pb[cb][:, 0:8], AF.Exp, ["b%d" % cb], ["dc%d" % ab], scale=-1.0 / 16.0)
                    self.tt("dve", kdec[ab][:], ksb[ab][:], dfac[ab][:], ALU.mult, ["ksb%d" % ab, "dfac%d" % ab],
                            ["kdec%d" % ab])

                def state(ab):
                    for c in range(2):
                        pc = 2 * ab + c
                        for half in range(2):
                            bank = 1 + 2 * c + half
                            for q in range(2):
                                hh = 2 * half + q
                                self.mm(pb[bank][:, q * 256:q * 256 + 256],
                                        kdec[ab][c * 64:(c + 1) * 64, hh * 128:(hh + 1) * 128],
                                        vb[ab][c * 64:(c + 1) * 64, hh * 256:(hh + 1) * 256], True, True,
                                        ["kdec%d" % ab, "vb%d" % ab], ["b%d" % bank])
                        for half in range(2):
                            bank = 1 + 2 * c + half
                            for q in range(2):
                                hh = 2 * half + q
                                self.stt(Sst[:, hh, :], Sst[:, hh, :], dc[ab][:, 2 * hh + c:2 * hh + c + 1],
                                         pb[bank][:, q * 256:q * 256 + 256], ALU.mult, ALU.add,
                                         ["Sst", "dc%d" % ab, "b%d" % bank], ["Sst"])
                        self.cp("act", Sbf[:, pc, :, :], Sst[:, :, :], ["Sst"], ["Sbf%d" % pc])

                def own_proj(j):
                    sl = j % 2
                    h, hk = hT[sl][2], hTn[sl][2]
                    for hh in range(4):
                        for k in range(KD):
                            self.mm(pb[1][:, hh * 128:(hh + 1) * 128], WO[:, k, hh * 128:(hh + 1) * 128], h[:, k, :],
                                    k == 0, k == KD - 1, [hk, "WO"], ["b1"])
                    qv = pb[1][:, 0:512].rearrange("p (h n) -> p h n", h=4)
                    for m in range(2):
                        for cc in range(2):
                            c = 2 * m + cc
                            self.ts("dve", qpad[:, c, :, cc * 64:(cc + 1) * 64], qv[:, :, cc * 64:(cc + 1) * 64],
                                    qmf[:, m:m + 1], ALU.mult, ["b1", "qmf"], ["qpad"])
                    for (bank, c0) in ((2, 512), (3, 1024)):
                        for k in range(KD):
                            self.mm(pb[bank][:, 0:512], h[:, k, :], WO[:, k, c0:c0 + 512], k == 0, k == KD - 1,
                                    [hk, "WO"], ["b%d" % bank])
                        o0 = c0 - 512
                        self.act(eg[:, o0:o0 + 512], pb[bank][:, 0:512], AF.Exp, ["b%d" % bank], ["eg%d" % bank], scale=-1.0)
                        self.ts("dve", eg[:, o0:o0 + 512], eg[:, o0:o0 + 512], 1.0, ALU.add, ["eg%d" % bank], ["eg%d" % bank])
                        self.recip(eg[:, o0:o0 + 512], eg[:, o0:o0 + 512], ["eg%d" % bank], ["eg%d" % bank])
                        self.tt("dve", sg[:, o0:o0 + 512], eg[:, o0:o0 + 512], pb[bank][:, 0:512], ALU.mult,
                                ["eg%d" % bank, "b%d" % bank], ["sg%d" % bank])
                    for k in range(KD):
                        self.mm(pb[4][:, 0:384], h[:, k, :], WO[:, k, 1536:1920], k == 0, k == KD - 1,
                                [hk, "WO"], ["b4"])
                    ssq, lnq, rsq = stc[2]
                    self.act(self.junk[:, 0:384], pb[4][:, 0:384], AF.Square, ["b4"], ["junk", "ssq"], accum=ssq[:, 0:1])
                    self.rstd_from_ss(ssq[:, 0:1], lnq[:, 0:1], rsq[:, 0:1], 384, ["ssq"], "rsq")
                    self.ts("dve", cqn[:], pb[4][:, 0:384], rsq[:, 0:1], ALU.mult, ["b4", "rsq"], ["cqn"])

                def own_out(j):
                    for hh in range(4):
                        bank = 5 + hh // 2
                        for c in range(4):
                            self.mm(pb[bank][:, (hh % 2) * 256:(hh % 2) * 256 + 256], qpad[:, c, hh, :],
                                    Sbf[:, c, hh, :], c == 0, c == 3, ["qpad", "Sbf%d" % c], ["b%d" % bank])
                    for hf in range(2):
                        self.cp("act", osb[:, hf * 512:(hf + 1) * 512], pb[5 + hf][:, 0:512], ["b%d" % (5 + hf)],
                                ["eg%d" % (2 + hf)])
                    for hh in range(4):
                        self.act(self.junk[:, 0:256], osb[:, hh * 256:(hh + 1) * 256], AF.Square,
                                 ["eg%d" % (2 + hh // 2)], ["junk", "sso"], accum=sso[:, hh:hh + 1])
                    self.act(lno[:], sso[:], AF.Ln, ["sso"], ["lno"], scale=1.0 / 256, bias=self.epsc[:, 0:1])
                    self.act(rso[:], lno[:], AF.Exp, ["lno"], ["rso"], scale=-0.5)
                    for hh in range(4):
                        self.stt(og[:, hh * 256:(hh + 1) * 256], osb[:, hh * 256:(hh + 1) * 256],
                                 rso[:, hh:hh + 1], sg[:, hh * 256:(hh + 1) * 256], ALU.mult, ALU.mult,
                                 ["eg%d" % (2 + hh // 2), "rso", "sg%d" % (2 + hh // 2)], ["og"])
                    for k in range(KD):
                        self.tr(pb16[0][:, k * 128:(k + 1) * 128], og[:, k * 128:(k + 1) * 128], ["og"], ["b0"])
                    self.cp("act", ogT[:], pb16[0][:, 0:1024].rearrange("p (k n) -> p k n", k=KD), ["b0"], ["ogT"])
                    gi, ii = j // GT, j % GT
                    self.dma("act", L["sc_og"][gi, :, :, ii * 128:(ii + 1) * 128], ogT[:], "ogst", ["ogT"], ["sc_og"])
                    for k in range(3):
                        self.tr(pb16[0][:, k * 128:(k + 1) * 128], cqn[:, k * 128:(k + 1) * 128], ["cqn"], ["b0"])
                    self.cp("dve", cqnT[:], pb16[0][:, 0:384].rearrange("p (k n) -> p k n", k=3), ["b0"], ["cqnT"])
                    self.dma("act", L["sc_cq"][j], cqnT[:], "cqst", ["cqnT"], ["sc_cq"])

                self.probe('A')
                S1(0)
                for j in range(NP):
                    proj(j, 0)
                    if j + 1 < NP:
                        S1(j + 1)
                    proj(j, 1)
                    chain(0)
                    own_proj(j)
                    chain(1)
                    state(0)
                    state(1)
                    own_out(j)
            self.s.barrier()
            if self.upto >= 2:
                self.pass_B1(es, L)
                self.s.barrier()
        if self.upto >= 3:
            self.pass_B2(es_outer, L)
            self.s.barrier()
        if self.upto >= 4:
            self.pass_C(es_outer, L)

    def dbg_dump_A(self, L):
        pass

    def pass_B1(self, es_outer, L):
        nc, S, NT, NOWN, NG = self.nc, self.S, self.NT, self.NOWN, self.NG
        pb, pb16 = self.pb, self.pb16
        vaug, kpeT = self.vaug, self.kpeT
        SCALE = 192.0 ** -0.5
        with ExitStack() as es:
            ckvT = self.sb(es, "ckvT", [128, 2, S], BF16)
            WUQ = self.sb(es, "WUQ", [128, 3, 2048], BF16)
            WUKT = self.sb(es, "WUKT", [128, 1, 2048], BF16)
            WUV = self.sb(es, "WUV", [128, 2, 1024], BF16)
            gq = self.sb(es, "gq", [128, 3], F32)
            gkv = self.sb(es, "gkv", [128, 2], F32)
            amf = self.sb(es, "amf", [128, 256], F32)
            amb = self.sb(es, "amb", [128, 2, 128], BF16)
            self.dma("sp", gq[:], L["g_q"], "misc", [], ["gains"])
            self.dma("sp", gkv[:], L["g_kv"], "misc2", [], ["gains"])
            self.dma("sp", amf[:], L["amask"], "misc3", [], ["amf"])
            self.cp("dve", amb[:].rearrange("p a n -> p (a n)"), amf[:], ["amf"], ["amb"])
            with ExitStack() as ew:
                self.wstage = [self.sb(ew, "wstgb%d" % i, [128, 2048], F32) for i in range(2)]
                self.load_w(WUQ, L["w_uq"], 3, 2048, gq, "WUQ")
                self.load_w(WUKT, L["w_ukT"], 1, 2048, None, "WUKT")
                self.load_w(WUV, L["w_uv"], 2, 1024, gkv, "WUV")
            self.s.barrier()
            for t in range(NT):
                for lc in range(2):
                    self.tr(pb16[0][:, lc * 128:(lc + 1) * 128], vaug[:, t, lc * 128:(lc + 1) * 128],
                            ["vaug%d" % t], ["b0"])
                self.cp("act" if t % 2 else "dve", ckvT[:, :, t * 128:(t + 1) * 128],
                        pb16[0][:, 0:256].rearrange("p (a n) -> p a n", a=2), ["b0"], ["ckvT%d" % t])
            import os
            B1STOP = int(os.environ.get("B1STOP", "9"))
            if B1STOP < 1:
                return
            cqT = [self.sb(es, "cqT%d" % i, [128, 3, 128], BF16) for i in range(2)]
            rco = [self.sb(es, "rco%d" % i, [64, 2, 128], F32) for i in range(2)]
            qn = [self.sb(es, "qn%d" % i, [128, 128], BF16) for i in range(2)]
            qpe = self.sb(es, "qpe", [64, 8, 128], BF16)
            qabs = self.sb(es, "qabs", [128, 2, 8, 128], BF16)
            t1 = self.sb(es, "bt1", [64, 128], F32)
            t2 = self.sb(es, "bt2", [64, 128], F32)
            PT = [self.sb(es, "PT%d" % i, [128, 4, 128], BF16) for i in range(3)]
            rsum = self.sb(es, "rsum", [128, 8], F32)
            olat = self.sb(es, "olat", [128, 8, 256], BF16)
            olatT = self.sb(es, "olatT", [128, 8, 2, 128], BF16)
            omT = self.sb(es, "omT", [128, 8, 128], BF16)
            self.probe('B1')
            def qprep(j):
                    sl = j % 2
                    XV = int(os.environ.get("XV", "3"))
                    if XV & 1:
                        self.dma("sp", cqT[sl][:], L["sc_cq"][j], "cqT%d" % sl, ["sc_cq"], ["cqT%d" % sl])
                    if XV & 2:
                        self.dma("sp", rco[sl][:], L["rope_own"][:, :, j * 128:(j + 1) * 128].rearrange("a p n -> p a n"),
                             "rco%d" % sl, ["rope_dram"], ["rco%d" % sl])
                    cq = cqT[sl]
                    QV = int(os.environ.get("QV", "9"))
                    for h in range(8):
                        if QV < 2:
                            continue
                        qs = h % 2
                        qb = 1 + qs
                        qbt = "b%d" % qb
                        for k in range(3):
                            self.mm(pb[qb][:, 0:128], WUQ[:, k, h * 256:h * 256 + 128], cq[:, k, :], k == 0, k == 2,
                                    ["cqT%d" % sl, "WUQ"], [qbt])
                        for (o0, c0) in ((128, 128), (256, 192)):
                            for k in range(3):
                                self.mm(pb[qb][0:64, o0:o0 + 128], WUQ[:, k, h * 256 + c0:h * 256 + c0 + 64], cq[:, k, :],
                                        k == 0, k == 2, ["cqT%d" % sl, "WUQ"], [qbt])
                        YV = int(os.environ.get("YV", "9"))
                        if YV < 1:
                            continue
                        self.cp("act", qn[qs][:], pb[qb][:, 0:128], [qbt], ["qn%d" % qs])
                        if YV < 2:
                            continue
                        self.tt("dve", t1[:], pb[qb][0:64, 128:256], rco[sl][:, 0, :], ALU.mult, [qbt, "rco%d" % sl], ["bt1"])
                        self.tt("dve", t2[:], pb[qb][0:64, 256:384], rco[sl][:, 1, :], ALU.mult, [qbt, "rco%d" % sl], ["bt2"])
                        self.tt("dve", qpe[:, h, :], t1[:], t2[:], ALU.add, ["bt1", "bt2"], ["qpe"])
                        bank = 3 + qs
                        if QV < 3:
                            continue
                        for lc in range(2):
                            self.mm(pb[bank][:, lc * 128:(lc + 1) * 128], WUKT[:, 0, h * 256 + lc * 128:h * 256 + (lc + 1) * 128],
                                    qn[qs][:], True, True, ["qn%d" % qs, "WUKT"], ["b%d" % bank])
                        for lc in range(2):
                            self.ts("dve", qabs[:, lc, h, :], pb[bank][:, lc * 128:(lc + 1) * 128], gkv[:, lc:lc + 1],
                                    ALU.mult, ["b%d" % bank, "gains"], ["qabs"])

            def attention(j):
                    nkt = 2 * j + 2
                    for gi in range(2):
                        def scores(kt):
                            sbk = 5 + (kt % 3)
                            for lc in range(2):
                                self.mm(pb[sbk][:, 0:512], ckvT[:, lc, kt * 128:(kt + 1) * 128],
                                        qabs[:, lc, 4 * gi:4 * gi + 4, :], lc == 0, False,
                                        ["ckvT%d" % kt, "qabs"], ["b%d" % sbk])
                            self.mm(pb[sbk][:, 0:512], kpeT[:, kt * 128:(kt + 1) * 128], qpe[:, 4 * gi:4 * gi + 4, :],
                                    False, True, ["kpeT%d" % kt, "qpe"], ["b%d" % sbk])

                        scores(0)
                        scores(1)
                        for kt in range(nkt):
                            sbk = 5 + (kt % 3)
                            ps = kt % 3
                            self.act(PT[ps][:], pb[sbk][:, 0:512].rearrange("p (h n) -> p h n", h=4), AF.Exp,
                                     ["b%d" % sbk], ["PT%d" % ps], scale=SCALE)
                            if kt >= nkt - 2:
                                r = kt - (nkt - 2)
                                for hh in range(4):
                                    self.tt("dve", PT[ps][:, hh, :], PT[ps][:, hh, :], amb[:, r, :], ALU.mult,
                                            ["PT%d" % ps, "amb"], ["PT%d" % ps])
                            if kt + 2 < nkt:
                                scores(kt + 2)
                            for hh in range(4):
                                self.mm(pb[1 + hh][:, 0:258], PT[ps][:, hh, :], vaug[:, kt, 0:258], kt == 0, kt == nkt - 1,
                                        ["PT%d" % ps, "vaug%d" % kt, "vaug_ones"], ["b%d" % (1 + hh)])
                        for hh in range(4):
                            h = 4 * gi + hh
                            self.recip(rsum[:, h:h + 1], pb[1 + hh][:, 256:257], ["b%d" % (1 + hh)], ["rsum"])
                            self.ts("dve", olat[:, h, :], pb[1 + hh][:, 0:256], rsum[:, h:h + 1], ALU.mult,
                                    ["b%d" % (1 + hh), "rsum"], ["olat"])

            def unabsorb(j):
                    for half in range(2):
                        for hh in range(4):
                            h = 4 * half + hh
                            for lc in range(2):
                                self.tr(pb16[0][:, (hh * 2 + lc) * 128:(hh * 2 + lc + 1) * 128],
                                        olat[:, h, lc * 128:(lc + 1) * 128], ["olat"], ["b0"])
                        self.cp("act", olatT[:, 4 * half:4 * half + 4, :, :].rearrange("p h a n -> p (h a n)"),
                                pb16[0][:, 0:1024], ["b0"], ["olatT"])
                    for half in range(2):
                        bank = 5 + half
                        for hh in range(4):
                            h = 4 * half + hh
                            for lc in range(2):
                                self.mm(pb[bank][:, hh * 128:(hh + 1) * 128], WUV[:, lc, h * 128:(h + 1) * 128],
                                        olatT[:, h, lc, :], lc == 0, lc == 1, ["olatT", "WUV"], ["b%d" % bank])
                        self.cp("act", omT[:, 4 * half:4 * half + 4, :].rearrange("p h n -> p (h n)"), pb[bank][:, 0:512],
                                ["b%d" % bank], ["omT"])
                    gi_, ii = j // GT, j % GT
                    self.dma("act", L["sc_om"][gi_, :, :, ii * 128:(ii + 1) * 128], omT[:], "omst", ["omT"], ["sc_om"])

            qprep(0)
            for j in range(NOWN):
                attention(j)
                if j + 1 < NOWN:
                    qprep(j + 1)
                unabsorb(j)

    def post_norm_res(self, banks, btoks, gbc, xres, xtok, outt, outtok, u):
        ss, lnv, rs = self.st_ss[u], self.st_ln[u], self.st_rs[u]
        for hf in range(2):
            self.act(self.junk[:, 0:512], banks[hf][:, 0:512], AF.Square, [btoks[hf]], ["junk", "pss%d" % u],
                     accum=ss[:, 2 + hf:3 + hf])
        self.tt("dve", ss[:, 4:5], ss[:, 2:3], ss[:, 3:4], ALU.add, ["pss%d" % u], ["pss2%d" % u])
        self.rstd_from_ss(ss[:, 4:5], lnv[:, 4:5], rs[:, 4:5], D, ["pss2%d" % u], "prs%d" % u)
        for hf in range(2):
            self.stt(self.ptmp[:, hf * 512:(hf + 1) * 512], banks[hf][:, 0:512], rs[:, 4:5],
                     gbc[:, hf * 512:(hf + 1) * 512], ALU.mult, ALU.mult, [btoks[hf], "prs%d" % u, "gbc"], ["ptmp"])
        self.tt("dve", outt, self.ptmp[:], xres, ALU.add, ["ptmp", xtok], [outtok])

    def pass_B2(self, es_outer, L):
        nc, S, NT, NOWN, NG = self.nc, self.S, self.NT, self.NOWN, self.NG
        pb, pb16 = self.pb, self.pb16
        with ExitStack() as es:
            WG = self.sb(es, "WG", [128, KD, 2048], BF16)
            WOG = self.sb(es, "WOG", [128, KD, 1024], BF16)
            WOM = self.sb(es, "WOM", [128, KD, 1024], BF16)
            WOUT = self.sb(es, "WOUT", [128, KD, 1024], BF16)
            gpm = self.sb(es, "gpm2", [128, 8], F32)
            ggl = self.sb(es, "ggl", [128, 8], F32)
            bg = self.sb(es, "bg", [128, 16], F32)
            gbc = self.sb(es, "gbc", [128, 1024], F32)
            self.dma("sp", gpm[:], L["g_pm"], "misc", [], ["gains"])
            self.dma("sp", ggl[:], L["g_gla"], "misc2", [], ["gains"])
            self.dma("sp", bg[:], L["b_g"], "misc3", [], ["bg"])
            self.dma("sp", gbc[:], L["g_pmix"][0:1, :].partition_broadcast(128), "misc4", [], ["gbc"])
            self.wstage = [self.sb(es, "wstgc%d" % i, [128, 2048], F32) for i in range(2)]
            self.load_w(WG, L["w_g"], KD, 2048, gpm, "WG")
            self.load_w(WOG, L["w_og"], KD, 1024, ggl, "WOG")
            self.load_w(WOM, L["w_om"], KD, 1024, None, "WOM")
            join_wout = self.load_w(WOUT, L["w_out"], KD, 1024, None, "WOUT", defer=True)
            xg = [self.sb(es, "xg%d" % i, [128, 1024], F32) for i in range(GT)]
            hTg = self.sb(es, "hTg", [128, KD, 512], BF16)
            ogg = self.sb(es, "ogg", [128, KD, 512], BF16)
            omg = self.sb(es, "omg", [128, KD, 512], BF16)
            ga = [self.sb(es, "ga%d" % i, [128, 512], F32) for i in range(2)]
            gb = [self.sb(es, "gb%d" % i, [128, 512], F32) for i in range(2)]
            m1 = [self.sb(es, "m1%d" % i, [128, 512], F32) for i in range(2)]
            m2 = [self.sb(es, "m2%d" % i, [128, 512], F32) for i in range(2)]
            mixT = self.sb(es, "mixT", [128, KD, 512], BF16)
            self.ptmp = self.sb(es, "ptmp", [128, 1024], F32)
            self.probe('B2')
            for g in range(NG):
                self.dma("sp", ogg[:], L["sc_og"][g], "ogg", ["sc_og"], ["ogg"])
                self.dma("sp", omg[:], L["sc_om"][g], "omg", ["sc_om"], ["omg"])
                items = []
                for i in range(GT):
                    j = g * GT + i
                    self.dma("sp", xg[i][:], L["xown"][j * 128:(j + 1) * 128, :], "xg" + "ABCD"[i], [], ["xg%d" % i])
                    items.append((xg[i][:], "xg%d" % i, hTg[:, :, i * 128:(i + 1) * 128], "hTg", i))
                self.norm_T_multi(items)
                for fc in range(KD):
                    st = fc % 2
                    bs = (1, 2, 3, 4) if st == 0 else (5, 6, 7, 0)
                    srcs = ((WG, 0, hTg, "hTg", "WG"), (WG, 1024, hTg, "hTg", "WG"),
                            (WOG, 0, ogg, "ogg", "WOG"), (WOM, 0, omg, "omg", "WOM"))
                    for bi, (Wt, off, rhs, rtok, wtok) in enumerate(srcs):
                        for k in range(KD):
                            self.mm(pb[bs[bi]][:, 0:512], Wt[:, k, off + fc * 128:off + (fc + 1) * 128], rhs[:, k, :],
                                    k == 0, k == KD - 1, [rtok, wtok], ["b%d" % bs[bi]])
                    self.act(ga[st][:], pb[bs[0]][:, 0:512], AF.Sigmoid, ["b%d" % bs[0], "bg"], ["ga%d" % st],
                             bias=bg[:, fc:fc + 1])
                    self.act(gb[st][:], pb[bs[1]][:, 0:512], AF.Sigmoid, ["b%d" % bs[1], "bg"], ["gb%d" % st],
                             bias=bg[:, 8 + fc:9 + fc])
                    self.tt("dve", m1[st][:], ga[st][:], pb[bs[2]][:, 0:512], ALU.mult, ["ga%d" % st, "b%d" % bs[2]],
                            ["m1%d" % st])
                    self.tt("dve", m2[st][:], gb[st][:], pb[bs[3]][:, 0:512], ALU.mult, ["gb%d" % st, "b%d" % bs[3]],
                            ["m2%d" % st])
                    self.tt("dve", mixT[:, fc, :], m1[st][:], m2[st][:], ALU.add, ["m1%d" % st, "m2%d" % st], ["mixT"])
                if g == 0:
                    join_wout()
                for i in range(GT):
                    j = g * GT + i
                    bs = (1, 2) if i % 2 == 0 else (3, 4)
                    for hf in range(2):
                        for k in range(KD):
                            self.mm(pb[bs[hf]][:, 0:512], mixT[:, k, i * 128:(i + 1) * 128], WOUT[:, k, hf * 512:(hf + 1) * 512],
                                    k == 0, k == KD - 1, ["mixT", "WOUT"], ["b%d" % bs[hf]])
                    self.post_norm_res([pb[bs[0]], pb[bs[1]]], ["b%d" % bs[0], "b%d" % bs[1]], gbc, xg[i][:], "xg%d" % i,
                                       xg[i][:], "xg%d" % i, 2 + i % 2)
                    self.dma("act", L["sc_x1"][j * 128:(j + 1) * 128, :], xg[i][:], "x1st%d" % i, ["xg%d" % i], ["sc_x1"])

    def pass_C(self, es_outer, L):
        nc, S, NT, NOWN, NG = self.nc, self.S, self.NT, self.NOWN, self.NG
        pb, pb16 = self.pb, self.pb16
        with ExitStack() as es:
            self.probe('Cstart')
            WFG = self.sb(es, "WFG", [128, KD, DFF], BF16)
            WFU = self.sb(es, "WFU", [128, KD, DFF], BF16)
            WFD = self.sb(es, "WFD", [128, KF, 1024], BF16)
            gpf = self.sb(es, "gpf", [128, 8], F32)
            gbc = self.sb(es, "gbc2", [128, 1024], F32)
            self.dma("sp", gpf[:], L["g_pf"], "misc", [], ["gains"])
            self.dma("sp", gbc[:], L["g_pffn"][0:1, :].partition_broadcast(128), "misc4", [], ["gbc"])
            with ExitStack() as ew:
                self.wstage = [self.sb(ew, "wstgd%d" % i, [128, 2048], F32) for i in range(2)]
                self.load_w(WFG, L["w_fg"], KD, DFF, gpf, "WFG")
                self.load_w(WFU, L["w_fu"], KD, DFF, gpf, "WFU")
                self.load_w(WFD, L["w_fd"], KF, 1024, None, "WFD")
            self.s.barrier()
            self.probe('C0')
            xg = [self.sb(es, "xc%d" % i, [128, 1024], F32) for i in range(GT)]
            hTg = self.sb(es, "hTc", [128, KD, 512], BF16)
            actT = self.sb(es, "actT", [128, KF, 512], BF16)
            sl = [self.sb(es, "sl%d" % i, [128, 512], F32) for i in range(2)]
            self.ptmp = self.sb(es, "ptmp2", [128, 1024], F32)
            self.probe('C')
            for g in range(NG):
                items = []
                for i in range(GT):
                    j = g * GT + i
                    self.dma("sp", xg[i][:], L["sc_x1"][j * 128:(j + 1) * 128, :], "xg" + "ABCD"[i], ["sc_x1"], ["xg%d" % i])
                    items.append((xg[i][:], "xg%d" % i, hTg[:, :, i * 128:(i + 1) * 128], "hTg", i))
                self.norm_T_multi(items)
                for fc in range(KF):
                    st = fc % 2
                    bg_, bu_ = (1 + 2 * (fc % 3), 2 + 2 * (fc % 3))
                    for (Wt, bank, wtok) in ((WFG, bg_, "WFG"), (WFU, bu_, "WFU")):
                        for k in range(KD):
                            self.mm(pb[bank][:, 0:512], Wt[:, k, fc * 128:(fc + 1) * 128], hTg[:, k, :], k == 0, k == KD - 1,
                                    ["hTg", wtok], ["b%d" % bank])
                    self.act(sl[st][:], pb[bg_][:, 0:512], AF.Silu, ["b%d" % bg_], ["sl%d" % st])
                    self.tt("dve", actT[:, fc, :], sl[st][:], pb[bu_][:, 0:512], ALU.mult, ["sl%d" % st, "b%d" % bu_],
                            ["actT"])
                for i in range(GT):
                    j = g * GT + i
                    bs = (1, 2) if i % 2 == 0 else (3, 4)
                    for hf in range(2):
                        for k in range(KF):
                            self.mm(pb[bs[hf]][:, 0:512], actT[:, k, i * 128:(i + 1) * 128], WFD[:, k, hf * 512:(hf + 1) * 512],
                                    k == 0, k == KF - 1, ["actT", "WFD"], ["b%d" % bs[hf]])
                    self.post_norm_res([pb[bs[0]], pb[bs[1]]], ["b%d" % bs[0], "b%d" % bs[1]], gbc, xg[i][:], "xg%d" % i,
                                       xg[i][:], "xg%d" % i, 2 + i % 2)
                    self.dma("act", L["out"][j * 128:(j + 1) * 128, :], xg[i][:], "outst%d" % i, ["xg%d" % i], ["out_dram"])


def _consts(parity):
    inv = (1.0 / (10000.0 ** (np.arange(0, 64, 2, dtype=np.float32) / np.float32(64)))).astype(np.float32)
    invf = np.concatenate([inv, inv]).reshape(64, 1).astype(np.float32)
    ident = np.eye(128, dtype=np.float32)
    s = np.arange(128)[:, None]
    t = np.arange(128)[None, :]
    umat = ((s // 64 == t // 64) & (s > t)).astype(np.float32)
    cind = (s // 64 == np.arange(2)[None, :]).astype(np.float32)
    qmask = np.zeros((128, 2), np.float32)
    qmask[:, parity] = 1.0
    diag = ((s // 64) <= (t // 64)).astype(np.float32)
    amask = np.zeros((128, 2, 128), np.float32)
    if parity == 0:
        amask[:, 0, :] = diag
    else:
        amask[:, 0, :] = 1.0
        amask[:, 1, :] = diag
    return dict(invf=invf, ident=ident, umat=umat, cind=cind, qmask=qmask,
                amask=np.ascontiguousarray(amask.reshape(128, 256)))


def _pk(v, n):
    return np.ascontiguousarray(v.reshape(n, 128).T).astype(np.float32)


def _weights(inp):
    w_in = inp["w_in"][0]
    sp = np.cumsum([0, 512, 512, 1024, 1024, 16, 384, 256, 64])
    q, k, v, g, ha, cq, ckv, kpe = [w_in[:, sp[i]:sp[i + 1]] for i in range(8)]
    kpesw = np.concatenate([kpe[:, 32:], kpe[:, :32]], axis=1)
    w_sh = np.ascontiguousarray(np.concatenate([k, v, ckv, kpe, kpesw, ha], axis=1))
    w_ow = np.ascontiguousarray(np.concatenate([q, g, cq], axis=1))
    w_uq = inp["w_uq"][0].reshape(384, 8, 192)
    nope, rope = w_uq[:, :, :128], w_uq[:, :, 128:]
    ropesw = np.concatenate([rope[:, :, 32:], rope[:, :, :32]], axis=2)
    w_uq2 = np.ascontiguousarray(np.concatenate([nope, rope, ropesw], axis=2).reshape(384, 2048))
    w_ukv = inp["w_ukv"][0].reshape(256, 8, 256)
    w_ukT = np.ascontiguousarray(w_ukv[:, :, :128].transpose(2, 1, 0).reshape(128, 2048))
    w_uv = np.ascontiguousarray(w_ukv[:, :, 128:].reshape(256, 1024))
    gla = inp["gla_norm"][0]
    return dict(
        w_sh=w_sh, w_ow=w_ow, g_pm=_pk(inp["pre_mix_norm"][0], 8),
        w_a2=np.ascontiguousarray(inp["w_a2"][0]), b_a2=np.ascontiguousarray(inp["b_a2"][0].reshape(1, 512)),
        g_gla=_pk(np.tile(gla, 4), 8), w_og=np.ascontiguousarray(inp["w_o_gla"][0]),
        g_q=_pk(inp["q_norm"][0], 3), w_uq=w_uq2, g_kv=_pk(inp["kv_norm"][0], 2),
        w_ukT=w_ukT, w_uv=w_uv, w_om=np.ascontiguousarray(inp["w_o_mla"][0]),
        w_g=np.ascontiguousarray(inp["w_gate"][0]), b_g=_pk(inp["b_gate"][0], 16),
        w_out=np.ascontiguousarray(inp["w_out"][0]),
        g_pmix=np.ascontiguousarray(inp["post_mix_norm"][0].reshape(1, D)),
        g_pf=_pk(inp["pre_ffn_norm"][0], 8),
        w_fg=np.ascontiguousarray(inp["w_ffn_gate"][0]), w_fu=np.ascontiguousarray(inp["w_ffn_up"][0]),
        w_fd=np.ascontiguousarray(inp["w_ffn_down"][0]),
        g_pffn=np.ascontiguousarray(inp["post_ffn_norm"][0].reshape(1, D)),
    )


def make_in_maps(inp, S):
    x = np.asarray(inp["x"], dtype=np.float32)
    pos = np.asarray(inp["positions"], dtype=np.int32)
    W = _weights({k: np.asarray(v) for k, v in inp.items()})
    maps = []
    NT = S // 128
    for core in range(8):
        b, par = core // 2, core % 2
        xb = x[b, :S]
        xo = xb.reshape(NT // 2, 2, 128, D)[:, par].reshape(S // 2, D)
        pb = pos[b, :S]
        po = pb.reshape(NT // 2, 2, 128)[:, par].reshape(1, S // 2)
        m = dict(xall=np.ascontiguousarray(xb), xown=np.ascontiguousarray(xo),
                 posall=np.ascontiguousarray(pb.reshape(1, S)), posown=np.ascontiguousarray(po))
        m.update(_consts(par))
        m.update(W)
        maps.append(m)
    return maps


def run(inp, S, upto=99, dbg=False):
    kb = K(S, upto=upto, dbg=dbg)
    nc = kb.build()
    maps = make_in_maps(inp, S)
    maps = [{k: v for k, v in m.items() if k in kb.in_names} for m in maps]
    res = run_bass_kernel_spmd(nc, maps, core_ids=list(range(8)))
    B = 4
    NT = S // 128
    full = np.zeros((B, NT // 2, 2, 128, D), np.float32)
    for core in range(8):
        b, par = core // 2, core % 2
        full[b, :, par] = np.asarray(res.results[core]["out"]).reshape(NT // 2, 128, D)
    return full.reshape(B, S, D), res


def kernel(**inputs):
    o, _ = run(inputs, 8192)
    return o
```
